# Optimizing a Trainium2 kernel written in Bass

```python
import math
import jax
import jax.numpy as jnp
from jax import lax
import numpy as np

D_MODEL = 2048
BATCH = 2
SEQ = 4096
DEPTH = 4
DEC_BATCH = 8
DEC_SEQ = 64
PAST_LEN = 1024

CHUNK = 64
Q_BLOCK = 128
N_MIXERS = 4
MIX_WIDTH = 1536
MEM_TOKENS = 256
MEM_HEADS = 4
MEM_HEAD_DIM = 128
MEM_WIDTH = MEM_HEADS * MEM_HEAD_DIM
OUT_WIDTH = MIX_WIDTH + MEM_WIDTH
ROPE_THETA = 500000.0

A_HEADS = 6
A_QK_DIM = 128
A_V_DIM = 2 * A_QK_DIM
A_ROT_DIM = A_QK_DIM // 4
A_QK_WIDTH = A_HEADS * 2 * A_QK_DIM
A_V_WIDTH = A_HEADS * A_V_DIM

B_HEADS = 12
B_HEAD_DIM = 128
B_WIDTH = B_HEADS * B_HEAD_DIM
B_LEFT_CHUNKS = 8
B_WINDOW = B_LEFT_CHUNKS * CHUNK
B_REL_CLIP = 128

C_HEADS = 12
C_Q_RANK = 768
C_KV_RANK = 512
C_NOPE_DIM = 128
C_ROPE_DIM = 64
C_V_DIM = 128
C_ROPE_THETA = 10000.0

D_HEADS = 12
D_HEAD_DIM = 128
D_WIDTH = D_HEADS * D_HEAD_DIM

D_FF = 5632
CONV_WIDTH = 3

IN_WIDTH_A = 2 * A_QK_WIDTH + A_V_WIDTH + MEM_WIDTH
IN_WIDTH_B = 3 * B_WIDTH + MEM_WIDTH
IN_WIDTH_C = C_Q_RANK + C_KV_RANK + C_ROPE_DIM + MEM_WIDTH
IN_WIDTH_D = 3 * D_WIDTH + MEM_WIDTH

N_LAYERS_A = (DEPTH + N_MIXERS - 1) // N_MIXERS
N_LAYERS_B = (DEPTH + N_MIXERS - 2) // N_MIXERS
N_LAYERS_C = (DEPTH + N_MIXERS - 3) // N_MIXERS
N_LAYERS_D = (DEPTH + N_MIXERS - 4) // N_MIXERS

DEEPNORM_ALPHA = (2 * DEPTH) ** 0.25
DEEPNORM_BETA = (8 * DEPTH) ** -0.25
NORM_EPS = 1e-5
NEG_INF = -1e30

kernel_name = "hybrid_streaming_encoder_step"


def layer_norm(x, g, b):
    xf = x.astype(jnp.float32)
    mu = jnp.mean(xf, axis=-1, keepdims=True)
    var = jnp.mean(jnp.square(xf - mu), axis=-1, keepdims=True)
    y = (xf - mu) * lax.rsqrt(var + NORM_EPS) * g.astype(jnp.float32) + b.astype(jnp.float32)
    return y.astype(x.dtype)


def rms_norm(x, g):
    xf = x.astype(jnp.float32)
    y = xf * lax.rsqrt(jnp.mean(jnp.square(xf), axis=-1, keepdims=True) + NORM_EPS) * g.astype(jnp.float32)
    return y.astype(x.dtype)


def rope(x, pos, rot_dim, theta):
    half = rot_dim // 2
    inv_freq = theta ** (-jnp.arange(half, dtype=jnp.float32) / half)
    ang = pos.astype(jnp.float32)[:, None] * inv_freq[None, :]
    cos = jnp.cos(ang)[:, None, :].astype(x.dtype)
    sin = jnp.sin(ang)[:, None, :].astype(x.dtype)
    x1, x2, rest = x[..., :half], x[..., half:rot_dim], x[..., rot_dim:]
    return jnp.concatenate([x1 * cos - x2 * sin, x2 * cos + x1 * sin, rest], axis=-1)


def chunk_mask(q_pos, k_pos):
    return (k_pos[None, :] // CHUNK) <= (q_pos[:, None] // CHUNK)


def sweep_queries(block_fn, q, q_pos):
    bsz, t = q.shape[0], q.shape[1]
    if t <= Q_BLOCK or t % Q_BLOCK:
        return block_fn(q, q_pos)
    nb = t // Q_BLOCK
    qb = jnp.moveaxis(q.reshape(bsz, nb, Q_BLOCK, *q.shape[2:]), 1, 0)
    out = lax.map(lambda a: block_fn(a[0], a[1]), (qb, q_pos.reshape(nb, Q_BLOCK)))
    return jnp.moveaxis(out, 0, 1).reshape(bsz, t, *out.shape[3:])


def diff_lambda_init(layer_idx):
    return 0.8 - 0.6 * math.exp(-0.3 * layer_idx)


def mixer_diff(z, pos, past, lam_q1, lam_k1, lam_q2, lam_k2, head_norm_g, lam_init):
    bsz, t, _ = z.shape
    q, k, v = jnp.split(z, [A_QK_WIDTH, 2 * A_QK_WIDTH], axis=-1)
    q = rope(q.reshape(bsz, t, 2 * A_HEADS, A_QK_DIM), pos, A_ROT_DIM, ROPE_THETA)
    k = rope(k.reshape(bsz, t, 2 * A_HEADS, A_QK_DIM), pos, A_ROT_DIM, ROPE_THETA)
    q = q.reshape(bsz, t, A_HEADS, 2, A_QK_DIM)
    k_rows = k.reshape(bsz, t, A_HEADS, 2 * A_QK_DIM)
    v_rows = v.reshape(bsz, t, A_HEADS, A_V_DIM)
    if past is None:
        k_all, v_all = k_rows, v_rows
    else:
        k_all = jnp.concatenate([past[0], k_rows], axis=1)
        v_all = jnp.concatenate([past[1], v_rows], axis=1)
    k_pos = jnp.arange(k_all.shape[1])
    k_all = k_all.reshape(bsz, -1, A_HEADS, 2, A_QK_DIM)
    f32 = jnp.float32
    lam = (jnp.exp(jnp.sum(lam_q1.astype(f32) * lam_k1.astype(f32)))
           - jnp.exp(jnp.sum(lam_q2.astype(f32) * lam_k2.astype(f32))) + lam_init)
    scale = A_QK_DIM ** -0.5

    def block(qb, pb):
        s = jnp.einsum('bqhmd,bkhmd->bhmqk', qb, k_all).astype(f32) * scale
        p = jax.nn.softmax(jnp.where(chunk_mask(pb, k_pos)[None, None, None], s, NEG_INF), axis=-1)
        w = p[:, :, 0] - lam * p[:, :, 1]
        return jnp.einsum('bhqk,bkhd->bqhd', w.astype(v_all.dtype), v_all)

    o = sweep_queries(block, q, pos)
    o = rms_norm(o, head_norm_g) * (1.0 - lam_init)
    return o.reshape(bsz, t, A_V_WIDTH), (k_rows, v_rows)


def rel_bias_lookup(rel_bias, rel):
    return rel_bias[:, jnp.clip(rel, -B_REL_CLIP, B_REL_CLIP) + B_REL_CLIP]


def band_core(q, k, v, bias, mask):
    s = jnp.einsum('bnqhd,bnkhd->bnhqk', q, k).astype(jnp.float32) * (B_HEAD_DIM ** -0.5)
    s = s + bias[None, None].astype(jnp.float32)
    p = jax.nn.softmax(jnp.where(mask[None, :, None], s, NEG_INF), axis=-1)
    return jnp.einsum('bnhqk,bnkhd->bnqhd', p.astype(v.dtype), v)


def mixer_band(z, pos, past, rel_bias):
    bsz, t, _ = z.shape
    q, k, v = [a.reshape(bsz, t, B_HEADS, B_HEAD_DIM) for a in jnp.split(z, [B_WIDTH, 2 * B_WIDTH], axis=-1)]
    if past is None:
        nch = t // CHUNK
        band = (B_LEFT_CHUNKS + 1) * CHUNK
        padw = ((0, 0), (B_WINDOW, 0), (0, 0), (0, 0))
        kp = jnp.pad(k, padw).reshape(bsz, nch + B_LEFT_CHUNKS, CHUNK, B_HEADS, B_HEAD_DIM)
        vp = jnp.pad(v, padw).reshape(bsz, nch + B_LEFT_CHUNKS, CHUNK, B_HEADS, B_HEAD_DIM)
        idx = jnp.arange(nch)[:, None] + jnp.arange(B_LEFT_CHUNKS + 1)[None, :]
        kb = kp[:, idx].reshape(bsz, nch, band, B_HEADS, B_HEAD_DIM)
        vb = vp[:, idx].reshape(bsz, nch, band, B_HEADS, B_HEAD_DIM)
        q_rel = jnp.arange(CHUNK)
        k_rel = jnp.arange(band) - B_WINDOW
        bias = rel_bias_lookup(rel_bias, q_rel[:, None] - k_rel[None, :])
        key_chunk = jnp.arange(nch)[:, None] + (k_rel // CHUNK)[None, :]
        mask = (key_chunk >= 0)[:, None, :]
        o = band_core(q.reshape(bsz, nch, CHUNK, B_HEADS, B_HEAD_DIM), kb, vb, bias, mask)
        keep = min(B_WINDOW, t)
        new_state = (k[:, t - keep:], v[:, t - keep:])
    else:
        buf_len = past[0].shape[1]
        k_all = jnp.concatenate([past[0], k], axis=1)
        v_all = jnp.concatenate([past[1], v], axis=1)
        k_pos = pos[0] - buf_len + jnp.arange(buf_len + t)
        bias = rel_bias_lookup(rel_bias, pos[:, None] - k_pos[None, :])
        qc, kc = pos[:, None] // CHUNK, k_pos[None, :] // CHUNK
        mask = ((kc <= qc) & (kc >= qc - B_LEFT_CHUNKS))[None]
        o = band_core(q[:, None], k_all[:, None], v_all[:, None], bias, mask)
        new_state = (k_all[:, t:], v_all[:, t:])
    return o.reshape(bsz, t, B_WIDTH), new_state


def mixer_mla(z, pos, past, q_norm_g, kv_norm_g, w_uq, w_ukv):
    bsz, t, _ = z.shape
    c_q, c_kv, k_rope = jnp.split(z, [C_Q_RANK, C_Q_RANK + C_KV_RANK], axis=-1)
    q = (rms_norm(c_q, q_norm_g) @ w_uq).reshape(bsz, t, C_HEADS, C_NOPE_DIM + C_ROPE_DIM)
    q = jnp.concatenate([q[..., :C_NOPE_DIM], rope(q[..., C_NOPE_DIM:], pos, C_ROPE_DIM, C_ROPE_THETA)], axis=-1)
    latent = rms_norm(c_kv, kv_norm_g)
    kr = rope(k_rope[:, :, None, :], pos, C_ROPE_DIM, C_ROPE_THETA)[:, :, 0, :]
    if past is None:
        lat_all, kr_all = latent, kr
    else:
        lat_all = jnp.concatenate([past[0], latent], axis=1)
        kr_all = jnp.concatenate([past[1], kr], axis=1)
    s_len = lat_all.shape[1]
    kv = (lat_all @ w_ukv).reshape(bsz, s_len, C_HEADS, C_NOPE_DIM + C_V_DIM)
    k_nope, v = kv[..., :C_NOPE_DIM], kv[..., C_NOPE_DIM:]
    k_pos = jnp.arange(s_len)
    scale = (C_NOPE_DIM + C_ROPE_DIM) ** -0.5

    def block(qb, pb):
        s = (jnp.einsum('bqhd,bkhd->bhqk', qb[..., :C_NOPE_DIM], k_nope)
             + jnp.einsum('bqhd,bkd->bhqk', qb[..., C_NOPE_DIM:], kr_all)).astype(jnp.float32) * scale
        p = jax.nn.softmax(jnp.where(chunk_mask(pb, k_pos)[None, None], s, NEG_INF), axis=-1)
        return jnp.einsum('bhqk,bkhd->bqhd', p.astype(v.dtype), v)

    o = sweep_queries(block, q, pos)
    return o.reshape(bsz, t, C_HEADS * C_V_DIM), (latent, kr)


def mixer_stick(z, pos, past):
    bsz, t, _ = z.shape
    q, k, v = [a.reshape(bsz, t, D_HEADS, D_HEAD_DIM) for a in jnp.split(z, [D_WIDTH, 2 * D_WIDTH], axis=-1)]
    if past is None:
        k_all, v_all = k, v
    else:
        k_all = jnp.concatenate([past[0], k], axis=1)
        v_all = jnp.concatenate([past[1], v], axis=1)
    k_pos = jnp.arange(k_all.shape[1])
    scale = D_HEAD_DIM ** -0.5

    def block(qb, pb):
        logits = jnp.einsum('bqhd,bkhd->bhqk', qb, k_all).astype(jnp.float32) * scale
        allowed = (k_pos[None, :] < pb[:, None])[None, None]
        log_1m_beta = jnp.where(allowed, jax.nn.log_sigmoid(-logits), 0.0)
        tail = lax.cumsum(log_1m_beta, axis=3, reverse=True) - log_1m_beta
        a = jnp.where(allowed, jnp.exp(jax.nn.log_sigmoid(logits) + tail), 0.0)
        return jnp.einsum('bhqk,bkhd->bqhd', a.astype(v_all.dtype), v_all)

    o = sweep_queries(block, q, pos)
    return o.reshape(bsz, t, D_WIDTH), (k, v)


def project_memory(mem, w_mem_kv):
    bsz, n, _ = mem.shape
    kv = mem @ w_mem_kv
    return (kv[..., :MEM_WIDTH].reshape(bsz, n, MEM_HEADS, MEM_HEAD_DIM),
            kv[..., MEM_WIDTH:].reshape(bsz, n, MEM_HEADS, MEM_HEAD_DIM))


def memory_attention(q_mem, mem_k, mem_v):
    bsz, t, _ = q_mem.shape
    q = q_mem.reshape(bsz, t, MEM_HEADS, MEM_HEAD_DIM)
    s = jnp.einsum('bqhd,bkhd->bhqk', q, mem_k).astype(jnp.float32) * (MEM_HEAD_DIM ** -0.5)
    p = jax.nn.softmax(s, axis=-1)
    return jnp.einsum('bhqk,bkhd->bqhd', p.astype(mem_v.dtype), mem_v).reshape(bsz, t, MEM_WIDTH)


def conv_ffn(x, conv_state, w_up, conv_w, conv_b, w_down):
    t = x.shape[1]
    u = x @ w_up
    u_ext = jnp.concatenate([conv_state.astype(u.dtype), u], axis=1)
    h = conv_b
    for j in range(CONV_WIDTH):
        h = h + conv_w[j] * u_ext[:, j:j + t]
    gate, val = jnp.split(h, 2, axis=-1)
    return (jax.nn.silu(gate) * val) @ w_down, u_ext[:, t:]


def trunk_layer(i, x, pos, past_mix, mem_k, mem_v, conv_state, mix_params,
                w_in, w_o, ln1_g, ln1_b, w_up, conv_w, conv_b, w_down, ln2_g, ln2_b):
    m = i % N_MIXERS
    z = x @ w_in
    z_mix, q_mem = z[..., :-MEM_WIDTH], z[..., -MEM_WIDTH:]
    if m == 0:
        o_mix, mix_state = mixer_diff(z_mix, pos, past_mix, *mix_params, lam_init=diff_lambda_init(i))
    elif m == 1:
        o_mix, mix_state = mixer_band(z_mix, pos, past_mix, *mix_params)
    elif m == 2:
        o_mix, mix_state = mixer_mla(z_mix, pos, past_mix, *mix_params)
    else:
        o_mix, mix_state = mixer_stick(z_mix, pos, past_mix)
    o = jnp.concatenate([o_mix, memory_attention(q_mem, mem_k, mem_v)], axis=-1) @ w_o
    x = layer_norm(DEEPNORM_ALPHA * x + o, ln1_g, ln1_b)
    f, conv_new = conv_ffn(x, conv_state, w_up, conv_w, conv_b, w_down)
    x = layer_norm(DEEPNORM_ALPHA * x + f, ln2_g, ln2_b)
    return x, mix_state, conv_new


def setup_inputs(seed: int = 0) -> dict:
    keys = iter(jax.random.split(jax.random.key(seed), 48))

    def normal(shape, scale=1.0):
        return jax.random.normal(next(keys), shape, jnp.float32) * scale

    def gain(shape):
        return 1.0 + normal(shape, 0.02)

    b_rows = min(B_WINDOW, PAST_LEN)
    return {
        "x_prompt": normal((BATCH, SEQ, D_MODEL)),
        "x_sample": normal((DEC_BATCH, DEC_SEQ, D_MODEL)),
        "mem_prompt": normal((BATCH, MEM_TOKENS, D_MODEL)),
        "cache_a_k": normal((N_LAYERS_A, DEC_BATCH, PAST_LEN, A_HEADS, 2 * A_QK_DIM)),
        "cache_a_v": normal((N_LAYERS_A, DEC_BATCH, PAST_LEN, A_HEADS, A_V_DIM)),
        "cache_b_k": normal((N_LAYERS_B, DEC_BATCH, b_rows, B_HEADS, B_HEAD_DIM)),
        "cache_b_v": normal((N_LAYERS_B, DEC_BATCH, b_rows, B_HEADS, B_HEAD_DIM)),
        "cache_c_latent": normal((N_LAYERS_C, DEC_BATCH, PAST_LEN, C_KV_RANK)),
        "cache_c_krope": normal((N_LAYERS_C, DEC_BATCH, PAST_LEN, C_ROPE_DIM)),
        "cache_d_k": normal((N_LAYERS_D, DEC_BATCH, PAST_LEN, D_HEADS, D_HEAD_DIM)),
        "cache_d_v": normal((N_LAYERS_D, DEC_BATCH, PAST_LEN, D_HEADS, D_HEAD_DIM)),
        "cache_mem_k": normal((DEPTH, DEC_BATCH, MEM_TOKENS, MEM_HEADS, MEM_HEAD_DIM)),
        "cache_mem_v": normal((DEPTH, DEC_BATCH, MEM_TOKENS, MEM_HEADS, MEM_HEAD_DIM)),
        "state_ffn_conv": normal((DEPTH, DEC_BATCH, CONV_WIDTH - 1, 2 * D_FF)),
        "w_in_a": normal((N_LAYERS_A, D_MODEL, IN_WIDTH_A), D_MODEL ** -0.5),
        "w_in_b": normal((N_LAYERS_B, D_MODEL, IN_WIDTH_B), D_MODEL ** -0.5),
        "w_in_c": normal((N_LAYERS_C, D_MODEL, IN_WIDTH_C), D_MODEL ** -0.5),
        "w_in_d": normal((N_LAYERS_D, D_MODEL, IN_WIDTH_D), D_MODEL ** -0.5),
        "diff_lambda_q1": normal((N_LAYERS_A, A_QK_DIM), 0.1),
        "diff_lambda_k1": normal((N_LAYERS_A, A_QK_DIM), 0.1),
        "diff_lambda_q2": normal((N_LAYERS_A, A_QK_DIM), 0.1),
        "diff_lambda_k2": normal((N_LAYERS_A, A_QK_DIM), 0.1),
        "diff_norm_g": gain((N_LAYERS_A, A_V_DIM)),
        "band_rel_bias": normal((N_LAYERS_B, B_HEADS, 2 * B_REL_CLIP + 1), 0.1),
        "mla_q_norm_g": gain((N_LAYERS_C, C_Q_RANK)),
        "mla_kv_norm_g": gain((N_LAYERS_C, C_KV_RANK)),
        "mla_w_uq": normal((N_LAYERS_C, C_Q_RANK, C_HEADS * (C_NOPE_DIM + C_ROPE_DIM)), C_Q_RANK ** -0.5),
        "mla_w_ukv": normal((N_LAYERS_C, C_KV_RANK, C_HEADS * (C_NOPE_DIM + C_V_DIM)), C_KV_RANK ** -0.5),
        "w_mem_kv": normal((DEPTH, D_MODEL, 2 * MEM_WIDTH), D_MODEL ** -0.5),
        "w_o": normal((DEPTH, OUT_WIDTH, D_MODEL), DEEPNORM_BETA * OUT_WIDTH ** -0.5),
        "ln1_g": gain((DEPTH, D_MODEL)),
        "ln1_b": normal((DEPTH, D_MODEL), 0.02),
        "w_up": normal((DEPTH, D_MODEL, 2 * D_FF), D_MODEL ** -0.5),
        "conv_ffn_w": normal((DEPTH, CONV_WIDTH, 2 * D_FF), CONV_WIDTH ** -0.5),
        "conv_ffn_b": normal((DEPTH, 2 * D_FF), 0.02),
        "w_down": normal((DEPTH, D_FF, D_MODEL), DEEPNORM_BETA * D_FF ** -0.5),
        "ln2_g": gain((DEPTH, D_MODEL)),
        "ln2_b": normal((DEPTH, D_MODEL), 0.02),
    }


def reference(x_prompt, x_sample, mem_prompt,
              cache_a_k, cache_a_v, cache_b_k, cache_b_v, cache_c_latent, cache_c_krope,
              cache_d_k, cache_d_v, cache_mem_k, cache_mem_v, state_ffn_conv,
              w_in_a, w_in_b, w_in_c, w_in_d,
              diff_lambda_q1, diff_lambda_k1, diff_lambda_q2, diff_lambda_k2, diff_norm_g,
              band_rel_bias, mla_q_norm_g, mla_kv_norm_g, mla_w_uq, mla_w_ukv,
              w_mem_kv, w_o, ln1_g, ln1_b, w_up, conv_ffn_w, conv_ffn_b, w_down, ln2_g, ln2_b):
    past_len = cache_d_k.shape[2]
    pos_p = jnp.arange(x_prompt.shape[1])
    pos_s = past_len + jnp.arange(x_sample.shape[1])
    w_in_by_type = (w_in_a, w_in_b, w_in_c, w_in_d)
    caches_by_type = ((cache_a_k, cache_a_v), (cache_b_k, cache_b_v),
                      (cache_c_latent, cache_c_krope), (cache_d_k, cache_d_v))
    params_by_type = ((diff_lambda_q1, diff_lambda_k1, diff_lambda_q2, diff_lambda_k2, diff_norm_g),
                      (band_rel_bias,),
                      (mla_q_norm_g, mla_kv_norm_g, mla_w_uq, mla_w_ukv),
                      ())
    states_p = [([], []) for _ in range(N_MIXERS)]
    states_s = [([], []) for _ in range(N_MIXERS)]
    mem_k_p, mem_v_p, conv_p, conv_s = [], [], [], []
    x_p, x_s = x_prompt, x_sample
    for i in range(DEPTH):
        m, j = i % N_MIXERS, i // N_MIXERS
        mix_params = tuple(p[j] for p in params_by_type[m])
        shared = (w_in_by_type[m][j], w_o[i], ln1_g[i], ln1_b[i], w_up[i], conv_ffn_w[i],
                  conv_ffn_b[i], w_down[i], ln2_g[i], ln2_b[i])
        mk, mv = project_memory(mem_prompt, w_mem_kv[i])
        conv0 = jnp.zeros((x_p.shape[0], CONV_WIDTH - 1, 2 * D_FF), x_p.dtype)
        x_p, st_p, cv_p = trunk_layer(i, x_p, pos_p, None, mk, mv, conv0, mix_params, *shared)
        past = (caches_by_type[m][0][j], caches_by_type[m][1][j])
        x_s, st_s, cv_s = trunk_layer(i, x_s, pos_s, past, cache_mem_k[i], cache_mem_v[i],
                                      state_ffn_conv[i], mix_params, *shared)
        for a in range(2):
            states_p[m][a].append(st_p[a])
            states_s[m][a].append(st_s[a])
        mem_k_p.append(mk)
        mem_v_p.append(mv)
        conv_p.append(cv_p)
        conv_s.append(cv_s)
    new_a_k_prompt, new_a_v_prompt = jnp.stack(states_p[0][0]), jnp.stack(states_p[0][1])
    new_b_k_prompt, new_b_v_prompt = jnp.stack(states_p[1][0]), jnp.stack(states_p[1][1])
    new_c_latent_prompt, new_c_krope_prompt = jnp.stack(states_p[2][0]), jnp.stack(states_p[2][1])
    new_d_k_prompt, new_d_v_prompt = jnp.stack(states_p[3][0]), jnp.stack(states_p[3][1])
    new_mem_k_prompt, new_mem_v_prompt = jnp.stack(mem_k_p), jnp.stack(mem_v_p)
    new_ffn_conv_prompt = jnp.stack(conv_p)
    new_a_k_sample, new_a_v_sample = jnp.stack(states_s[0][0]), jnp.stack(states_s[0][1])
    new_b_k_sample, new_b_v_sample = jnp.stack(states_s[1][0]), jnp.stack(states_s[1][1])
    new_c_latent_sample, new_c_krope_sample = jnp.stack(states_s[2][0]), jnp.stack(states_s[2][1])
    new_d_k_sample, new_d_v_sample = jnp.stack(states_s[3][0]), jnp.stack(states_s[3][1])
    new_ffn_conv_sample = jnp.stack(conv_s)
    return (x_p, x_s,
            new_a_k_prompt, new_a_v_prompt, new_b_k_prompt, new_b_v_prompt,
            new_c_latent_prompt, new_c_krope_prompt, new_d_k_prompt, new_d_v_prompt,
            new_mem_k_prompt, new_mem_v_prompt, new_ffn_conv_prompt,
            new_a_k_sample, new_a_v_sample, new_b_k_sample, new_b_v_sample,
            new_c_latent_sample, new_c_krope_sample, new_d_k_sample, new_d_v_sample,
            new_ffn_conv_sample)
```

```python
import contextlib
import math
import numpy as np
import concourse.bass as bass
import concourse.mybir as mybir
from concourse.bass_utils import run_bass_kernel_spmd

F32 = mybir.dt.float32
BF16 = mybir.dt.bfloat16
AF = mybir.ActivationFunctionType
ALU = mybir.AluOpType

D = 2048
NT = 512
NPT = 8
TP = 4096
NS = 64
TOT = TP + NS
PAST = 1024
DFF = 5632
ALPHA = 8 ** 0.25
EPS = 1e-5
WEL = 8192
SAME_ENG_SYNC = True
NDS = 40
MASKV = -1.0e4

V_LN = 0
V_CW = V_LN + 4 * 4 * 16
V_CB = V_CW + 4 * 3 * 88
V_DG = V_CB + 4 * 88
V_QG = V_DG + 2
V_KG = V_QG + 6
V_LAM = V_KG + 4
NV = V_LAM + 4
C_ONES = 0
C_PERMA = 128
C_PERMC = 160
C_U = 288
C_MCC = 416
C_MSC = 544
C_DUP = 672
NC = 800


def lam_init(i):
    return 0.8 - 0.6 * math.exp(-0.3 * i)


def _colblock(w, cols):
    kc = w.shape[0] // 128
    t = np.zeros((128, 16, 512), np.float32)
    t[:, :kc, :len(cols)] = w[:, cols].reshape(kc, 128, len(cols)).transpose(1, 0, 2)
    return t.reshape(128, WEL), kc * 512


def _downblock(w, oc):
    t = np.zeros((128, WEL), np.float32)
    t[:, :44 * 128] = w[:, oc * 128:(oc + 1) * 128].reshape(44, 128, 128).transpose(1, 0, 2).reshape(128, 44 * 128)
    return t, 44 * 128


def build_weight_tiles(inp):
    tiles, nels = [], []
    layer_seq = [[] for _ in range(4)]
    pro_seq = []

    def add(t, lst):
        lst.append(len(tiles))
        tiles.append(t[0])
        nels.append(t[1])

    for l in range(4):
        wm = inp["w_mem_kv"][l]
        add(_colblock(wm, np.arange(0, 512)), pro_seq)
        add(_colblock(wm, np.arange(512, 1024)), pro_seq)
    w_in = [inp["w_in_a"][0], inp["w_in_b"][0], inp["w_in_c"][0], inp["w_in_d"][0]]
    for l in range(4):
        seq = layer_seq[l]
        w = w_in[l]
        if l != 2:
            for j in range(10):
                add(_colblock(w, np.arange(j * 512, (j + 1) * 512)), seq)
        else:
            add(_colblock(w, np.arange(0, 512)), seq)
            add(_colblock(w, np.concatenate([np.arange(512, 768), np.arange(1280, 1344)])), seq)
            add(_colblock(w, np.arange(768, 1280)), seq)
            add(_colblock(w, np.arange(1344, 1856)), seq)
            wq = inp["mla_w_uq"][0]
            nope = np.concatenate([np.arange(h * 192, h * 192 + 128) for h in range(12)])
            rope = np.concatenate([np.arange(h * 192 + 128, h * 192 + 192) for h in range(12)])
            qcols = np.concatenate([nope, rope])
            for j in range(5):
                add(_colblock(wq, qcols[j * 512:(j + 1) * 512]), seq)
            wkv = inp["mla_w_ukv"][0]
            kn = np.concatenate([np.arange(h * 256, h * 256 + 128) for h in range(12)])
            vv = np.concatenate([np.arange(h * 256 + 128, h * 256 + 256) for h in range(12)])
            kvcols = np.concatenate([kn, vv])
            for j in range(6):
                add(_colblock(wkv, kvcols[j * 512:(j + 1) * 512]), seq)
        wo = inp["w_o"][l]
        for j in range(4):
            add(_colblock(wo, np.arange(j * 512, (j + 1) * 512)), seq)
        wu = inp["w_up"][l]
        for t in range(22):
            cols = np.concatenate([np.arange(2 * t * 128, (2 * t + 2) * 128), DFF + np.arange(2 * t * 128, (2 * t + 2) * 128)])
            add(_colblock(wu, cols), seq)
        wd = inp["w_down"][l]
        for oc in range(16):
            add(_downblock(wd, oc), seq)
    return np.stack(tiles), nels, pro_seq, layer_seq


def weight_tile_meta():
    nels, layer_seq, pro_seq = [], [[] for _ in range(4)], []

    def add(nel, lst):
        lst.append(len(nels))
        nels.append(nel)

    for l in range(4):
        add(WEL, pro_seq)
        add(WEL, pro_seq)
    for l in range(4):
        seq = layer_seq[l]
        if l != 2:
            for j in range(10):
                add(WEL, seq)
        else:
            for j in range(4):
                add(WEL, seq)
            for j in range(5):
                add(6 * 512, seq)
            for j in range(6):
                add(4 * 512, seq)
        for j in range(4):
            add(WEL, seq)
        for t in range(22):
            add(WEL, seq)
        for oc in range(16):
            add(44 * 128, seq)
    return nels, pro_seq, layer_seq


def fm(v):
    return np.ascontiguousarray(v.reshape(-1, 128).T)


def build_shared(inp):
    vecs = np.zeros((128, NV), np.float32)
    for k, name in enumerate(["ln1_g", "ln1_b", "ln2_g", "ln2_b"]):
        for l in range(4):
            o = V_LN + (k * 4 + l) * 16
            vecs[:, o:o + 16] = fm(inp[name][l])
    for l in range(4):
        for j in range(3):
            o = V_CW + (l * 3 + j) * 88
            vecs[:, o:o + 88] = fm(inp["conv_ffn_w"][l, j])
        o = V_CB + l * 88
        vecs[:, o:o + 88] = fm(inp["conv_ffn_b"][l])
    vecs[:, V_DG:V_DG + 2] = fm(inp["diff_norm_g"][0])
    vecs[:, V_QG:V_QG + 6] = fm(inp["mla_q_norm_g"][0])
    vecs[:, V_KG:V_KG + 4] = fm(inp["mla_kv_norm_g"][0])
    for k, name in enumerate(["diff_lambda_q1", "diff_lambda_k1", "diff_lambda_q2", "diff_lambda_k2"]):
        vecs[:, V_LAM + k] = inp[name][0]
    c = np.zeros((128, NC), np.float32)
    c[:, C_ONES:C_ONES + 128] = 1.0
    for p in range(32):
        c[p, C_PERMA + (p + 16) % 32] = 1.0
    for p in range(128):
        g = (p // 64) * 64
        c[p, C_PERMC + g + ((p - g) + 32) % 64] = 1.0
    j = np.arange(128)[:, None]
    k = np.arange(128)[None, :]
    c[:, C_U:C_U + 128] = (j > k)
    c[:, C_MCC:C_MCC + 128] = (j // 64 <= k // 64)
    c[:, C_MSC:C_MSC + 128] = (j < k)
    for p in range(64):
        c[p, C_DUP + p] = 1.0
        c[p, C_DUP + 64 + p] = 1.0
    pos = np.concatenate([np.arange(TP), PAST + np.arange(NS)]).astype(np.float32)
    tabA = np.zeros((32, 2, TOT), np.float32)
    invA = (500000.0 ** (-np.arange(16, dtype=np.float32) / 16)).astype(np.float32)
    angA = pos[None, :] * invA[:, None]
    tabA[:16, 0] = np.cos(angA); tabA[16:, 0] = np.cos(angA)
    tabA[:16, 1] = -np.sin(angA); tabA[16:, 1] = np.sin(angA)
    tabC = np.zeros((128, 2, TOT), np.float32)
    invC = (10000.0 ** (-np.arange(32, dtype=np.float32) / 32)).astype(np.float32)
    angC = pos[None, :] * invC[:, None]
    for g in range(2):
        tabC[g * 64:g * 64 + 32, 0] = np.cos(angC); tabC[g * 64 + 32:g * 64 + 64, 0] = np.cos(angC)
        tabC[g * 64:g * 64 + 32, 1] = -np.sin(angC); tabC[g * 64 + 32:g * 64 + 64, 1] = np.sin(angC)
    rb = inp["band_rel_bias"][0]
    ki = np.arange(128)[:, None]
    qi = np.arange(128)[None, :]
    bias = np.zeros((128, 12, 5, 128), np.float32)
    for r in range(5):
        rel = 128 * (4 - r) + qi - ki
        idx = np.clip(rel, -128, 128) + 128
        dch = (8 - 2 * r) + qi // 64 - ki // 64
        ok = (dch >= 0) & (dch <= 8)
        for h in range(12):
            b = rb[h][idx]
            bias[:, h, r, :] = np.where(ok, b, np.float32(MASKV))
    return dict(vecs=vecs, consts=c, tabA=tabA, tabC=tabC, bbias=bias)


def build_core_inputs(inp, c):
    b, s = c % 2, c
    d = {}
    xT = np.empty((D, TOT), np.float32)
    xT[:, :TP] = inp["x_prompt"][b].T
    xT[:, TP:] = inp["x_sample"][s].T
    d["xT"] = xT
    d["memT"] = np.ascontiguousarray(inp["mem_prompt"][b].T)
    d["skt_a"] = np.ascontiguousarray(inp["cache_a_k"][0, s].reshape(PAST, 12, 128).transpose(1, 2, 0))
    d["sv_a"] = np.ascontiguousarray(inp["cache_a_v"][0, s].reshape(PAST, 1536))
    d["skt_b"] = np.ascontiguousarray(inp["cache_b_k"][0, s].transpose(1, 2, 0))
    d["sv_b"] = np.ascontiguousarray(inp["cache_b_v"][0, s].reshape(512, 1536))
    d["slat"] = np.ascontiguousarray(inp["cache_c_latent"][0, s].T.reshape(4, 128, PAST))
    kr = inp["cache_c_krope"][0, s].T
    d["skr"] = np.ascontiguousarray(np.concatenate([kr, kr], 0))
    d["skt_d"] = np.ascontiguousarray(inp["cache_d_k"][0, s].transpose(1, 2, 0))
    d["sv_d"] = np.ascontiguousarray(inp["cache_d_v"][0, s].reshape(PAST, 1536))
    d["smk"] = np.ascontiguousarray(inp["cache_mem_k"][:, s].transpose(0, 3, 2, 1))
    d["smv"] = np.ascontiguousarray(inp["cache_mem_v"][:, s].reshape(4, 2, 128, 512).transpose(0, 2, 1, 3))
    st = inp["state_ffn_conv"][:, s]
    d["cst_s"] = np.ascontiguousarray(st.reshape(4, 2, 88, 128).transpose(3, 0, 2, 1))
    return d


class Prog:
    def __init__(self, nc, es):
        self.nc, self.es = nc, es
        self.eng = {"pe": nc.tensor, "act": nc.scalar, "dve": nc.vector, "pool": nc.gpsimd, "sp": nc.sync}
        self.csem = {e: es.enter_context(nc.semaphore("c_" + e)) for e in ["pe", "act", "dve", "pool"]}
        self.ccnt = {e: 0 for e in self.csem}
        self.dsem = [es.enter_context(nc.semaphore("d%d" % i)) for i in range(NDS)]
        self.dcnt = [0] * NDS
        self.dnext = 0
        self.seen = {e: {} for e in self.eng}
        self.lw, self.rd = {}, {}
        self.ninst = 0

    def _sem(self, k):
        return self.csem[k[1]] if k[0] == "c" else self.dsem[k[1]]

    def _wait(self, e, deps):
        need = {}
        for k, v in deps:
            if v > need.get(k, 0):
                need[k] = v
        for k, v in need.items():
            if k == ("c", e) and (e == "pe" or not SAME_ENG_SYNC):
                continue
            if self.seen[e].get(k, 0) >= v:
                continue
            self.eng[e].wait_ge(self._sem(k), v)
            self.seen[e][k] = v
            self.ninst += 1

    def _deps(self, reads, writes):
        d = []
        for r in reads:
            if r in self.lw:
                d.append(self.lw[r])
        for w in writes:
            if w in self.lw:
                d.append(self.lw[w])
            d.extend(self.rd.get(w, {}).items())
        return d

    def _record(self, iid, reads, writes):
        for r in reads:
            rr = self.rd.setdefault(r, {})
            if iid[1] > rr.get(iid[0], 0):
                rr[iid[0]] = iid[1]
        for w in writes:
            self.lw[w] = iid
            self.rd[w] = {}

    def op(self, e, fn, reads=(), writes=()):
        self._wait(e, self._deps(reads, writes))
        self.ccnt[e] += 1
        iid = (("c", e), self.ccnt[e])
        fn(self.eng[e]).then_inc(self.csem[e], 1)
        self.ninst += 1
        self._record(iid, reads, writes)

    def dma(self, q, out, in_, reads=(), writes=()):
        k = self.dnext
        self.dnext = (k + 1) % NDS
        deps = self._deps(reads, writes)
        if self.dcnt[k] > 0:
            deps.append((("d", k), self.dcnt[k]))
        self._wait(q, deps)
        self.dcnt[k] += 16
        iid = (("d", k), self.dcnt[k])
        self.eng[q].dma_start(out=out, in_=in_).then_inc(self.dsem[k], 16)
        self.ninst += 1
        self._record(iid, reads, writes)

    def barrier(self):
        deps = [(("c", e), self.ccnt[e]) for e in self.ccnt if self.ccnt[e] > 0]
        deps += [(("d", k), self.dcnt[k]) for k in range(NDS) if self.dcnt[k] > 0]
        for e in ["pe", "act", "dve", "pool"]:
            self._wait(e, [d for d in deps if d[0] != ("c", e)])

    def finish(self):
        deps = [(("c", e), self.ccnt[e]) for e in self.ccnt if self.ccnt[e] > 0]
        deps += [(("d", k), self.dcnt[k]) for k in range(NDS) if self.dcnt[k] > 0]
        self._wait("sp", deps)

    def mm(self, out, lhsT, rhs, start, stop, reads, writes):
        self.op("pe", lambda e: e.matmul(out, lhsT, rhs, start=start, stop=stop), reads, writes)

    def act(self, out, in_, func, reads, writes, bias=None, scale=None):
        kw = {}
        if bias is not None:
            kw["bias"] = bias
        if scale is not None:
            kw["scale"] = scale
        self.op("act", lambda e: e.activation(out, in_, func, **kw), reads, writes)

    def tt(self, out, in0, in1, op, reads, writes, e="dve"):
        self.op(e, lambda g: g.tensor_tensor(out, in0, in1, op), reads, writes)

    def ts(self, out, in0, s1, s2, op0, op1, reads, writes, e="dve"):
        if op1 is None:
            self.op(e, lambda g: g.tensor_scalar(out, in0, s1, None, op0), reads, writes)
        else:
            self.op(e, lambda g: g.tensor_scalar(out, in0, s1, s2, op0, op1), reads, writes)

    def stt(self, out, in0, sc, in1, op0, op1, reads, writes):
        self.op("dve", lambda g: g.scalar_tensor_tensor(out, in0, sc, in1, op0, op1), reads, writes)

    def copy(self, out, in_, reads, writes, e="dve"):
        if e == "act":
            self.act(out, in_, AF.Copy, reads, writes)
        else:
            self.op(e, lambda g: g.tensor_copy(out, in_), reads, writes)


def R(name, idx):
    return [(name, i) for i in idx]


class Kern:
    def __init__(self, cfg):
        self.cfg = cfg
        self.NL = cfg.get("NL", 4)
        self.tiles = cfg.get("tiles", list(range(9)))

    def build(self):
        nc = bass.Bass("TRN2", target_bir_lowering=False)
        self.nc = nc
        nels, pro_seq, layer_seq = weight_tile_meta()
        self.nels, self.pro_seq, self.layer_seq = nels, pro_seq, layer_seq
        NWT = len(nels)
        dt = nc.dram_tensor

        def inp(name, shape):
            return dt(name, list(shape), F32, kind="ExternalInput")

        def outp(name, shape):
            return dt(name, list(shape), F32, kind="ExternalOutput")

        I = {}
        self.wgroups = [pro_seq] + layer_seq
        self.wloc = {}
        for gi, grp in enumerate(self.wgroups):
            for li, ti in enumerate(grp):
                self.wloc[ti] = (gi, li)
        for gi, grp in enumerate(self.wgroups):
            ng = len(grp) if gi <= self.NL else 1
            I["wt%d" % gi] = inp("wt%d" % gi, [ng, 128, WEL])
        I["vecs"] = inp("vecs", [128, NV])
        I["consts"] = inp("consts", [128, NC])
        I["tabA"] = inp("tabA", [32, 2, TOT])
        I["tabC"] = inp("tabC", [128, 2, TOT])
        I["bbias"] = inp("bbias", [128, 12, 5, 128])
        I["xT"] = inp("xT", [D, TOT])
        I["memT"] = inp("memT", [D, 256])
        I["skt_a"] = inp("skt_a", [12, 128, PAST]); I["sv_a"] = inp("sv_a", [PAST, 1536])
        I["skt_b"] = inp("skt_b", [12, 128, 512]); I["sv_b"] = inp("sv_b", [512, 1536])
        I["slat"] = inp("slat", [4, 128, PAST]); I["skr"] = inp("skr", [128, PAST])
        I["skt_d"] = inp("skt_d", [12, 128, PAST]); I["sv_d"] = inp("sv_d", [PAST, 1536])
        I["smk"] = inp("smk", [4, 128, 4, 256]); I["smv"] = inp("smv", [4, 128, 2, 512])
        I["cst_s"] = inp("cst_s", [128, 4, 88, 2])
        self.I = I
        O = {}
        O["yT"] = outp("yT", [D, TOT])
        for l in (0, 1, 3):
            O["kT%d" % l] = outp("kT%d" % l, [1536, TOT])
            O["v%d" % l] = outp("v%d" % l, [TOT, 1536])
        O["latT"] = outp("latT", [512, TOT])
        O["krT"] = outp("krT", [64, TOT])
        O["bkT_s"] = outp("bkT_s", [1536, 512])
        O["bv_s"] = outp("bv_s", [512, 1536])
        O["memkT"] = outp("memkT", [4, 512, 256])
        O["memv"] = outp("memv", [4, 256, 512])
        O["conv_p"] = outp("conv_p", [128, 4, 88, 2])
        O["conv_s"] = outp("conv_s", [128, 4, 88, 2])
        self.O = O
        S = {}
        for gi, grp in enumerate(self.wgroups):
            ng = len(grp) if gi <= self.NL else 1
            S["wb%d" % gi] = dt("wb%d" % gi, [ng, 128, WEL], BF16)
        for l in range(4):
            S["KT%d" % l] = dt("KTs%d" % l, [12, 128, TP], BF16)
            S["V%d" % l] = dt("Vs%d" % l, [TP, 1536], BF16)
        S["KR"] = dt("KRs", [128, TP], BF16)
        S["sKT0"] = dt("sKT0", [12, 128, PAST], BF16); S["sV0"] = dt("sV0", [PAST, 1536], BF16)
        S["sKT1"] = dt("sKT1", [12, 128, 512], BF16); S["sV1"] = dt("sV1", [512, 1536], BF16)
        S["sKT2"] = dt("sKT2", [12, 128, PAST], BF16); S["sV2"] = dt("sV2", [PAST, 1536], BF16)
        S["sKT3"] = dt("sKT3", [12, 128, PAST], BF16); S["sV3"] = dt("sV3", [PAST, 1536], BF16)
        S["sLat"] = dt("sLat", [4, 128, PAST], BF16); S["sKR"] = dt("sKR", [128, PAST], BF16)
        S["mK"] = dt("mK", [2, 4, 128, 4, 256], BF16); S["mV"] = dt("mV", [2, 4, 128, 2, 512], BF16)
        self.S = S

        with contextlib.ExitStack() as es:
            P = Prog(nc, es)
            self.P = P
            sb = lambda name, shape, dty: es.enter_context(nc.sbuf_tensor("s_" + name, list(shape), dty))
            self.xf = sb("xf", [128, 16, NT], F32)
            self.xb = sb("xb", [128, 16, NT], BF16)
            self.wsl = [sb("w%d" % i, [128, WEL], BF16) for i in range(2)]
            self.reg = sb("reg", [128, 44 * NT], BF16)
            self.oT = sb("oT", [128, 16, NT], BF16)
            self.ks = [sb("ks%d" % i, [128, 2, NT], BF16) for i in range(2)]
            self.vs = [sb("vs%d" % i, [128, 4, 256], BF16) for i in range(2)]
            self.pt = [sb("pt%d" % i, [128, NT], BF16) for i in range(3)]
            self.st = [sb("st%d" % i, [128, NT], F32) for i in range(3)]
            self.ue = [sb("ue%d" % i, [128, NT + 4], F32) for i in range(2)]
            self.hb = [sb("hb%d" % i, [128, NT], F32) for i in range(2)]
            self.tm = [sb("tm%d" % i, [128, NT], F32) for i in range(4)]
            self.dtm = sb("dtm", [128, 4 * NT], F32)
            self.dtmv = lambda c: self.dtm[:, c * NT:(c + 1) * NT]
            self.hcv = [self.dtm[:, c * NT:(c + 1) * NT] for c in range(4)]
            dtb = self.dtm[:, :].bitcast(BF16)
            self.qrb = lambda jj: dtb[:, jj * NT:(jj + 1) * NT]
            self.vecs = sb("vecs", [128, NV], F32)
            self.cf = sb("cf", [128, NC], F32)
            self.cb = sb("cbf", [128, NC], BF16)
            self.cst = sb("cst", [128, 4, 88, 2], F32)
            self.mk = sb("mk", [128, 4, 256], BF16)
            self.mv = sb("mv", [128, 2, 512], BF16)
            self.tab = sb("tab", [128, 2, NT], F32)
            self.ebf = sb("ebf", [128, 5, 128], F32)
            self.sm = sb("sm", [128, 16], F32)
            self.spacc = self.ebf[:, :, :].rearrange("p a b -> p (a b)")[:, 0:NT]
            self.ps = es.enter_context(nc.psum_tensor("ps", [128, 8, NT], F32))
            self.bank_rr = 0
            self.rr = {}
            reg = self.reg
            self.qT = lambda j: reg[:, j * NT:(j + 1) * NT]
            self.kT = lambda j: reg[:, (12 + j) * NT:(13 + j) * NT]
            self.vv = lambda tb: reg[:, 24 * NT + tb * 1536: 24 * NT + (tb + 1) * 1536]
            self.qm = lambda j: reg[:, 36 * NT + j * NT: 36 * NT + (j + 1) * NT]
            self.hT = lambda j: reg[:, j * NT:(j + 1) * NT]

            self.prologue()
            for i in self.tiles:
                self.do_tile(i)
            if NPT in self.tiles:
                P.dma("pool", self.O["conv_s"][:, :, :, :], self.cst[:], reads=[("cst",)], writes=[("o_convs",)])
            else:
                P.dma("pool", self.O["conv_p"][:, :, :, :], self.cst[:], reads=[("cst",)], writes=[("o_convp",)])
            P.finish()
        self.ninst = P.ninst
        return nc

    def rot(self, name, n):
        k = self.rr.get(name, 0)
        self.rr[name] = (k + 1) % n
        return k

    def bank(self, lo=0, hi=8):
        k = lo + self.rot(("bank", lo, hi), hi - lo)
        return self.ps[:, k, :], ("ps", k)

    def wnext(self):
        ws = self.wstate
        seq = ws["seq"]
        while ws["issued"] < min(len(seq), ws["pos"] + 2):
            k = ws["issued"]
            self.cast_upto(k + 6)
            ti = seq[k]
            nel = self.nels[ti]
            gi, li = self.wloc[ti]
            self.P.dma("sp", self.wsl[k % 2][:, :nel], self.S["wb%d" % gi][li][:, :nel], reads=[("wb", ti)], writes=[("w", k % 2)])
            ws["issued"] += 1
        k = ws["pos"]
        ws["pos"] += 1
        return self.wsl[k % 2], ("w", k % 2)

    def cast_upto(self, k):
        ws = self.wstate
        seq = ws["seq"]
        while ws["cpos"] < min(len(seq), k + 1):
            ti = seq[ws["cpos"]]
            ws["cpos"] += 1
            if ti in ws["casted"]:
                continue
            ws["casted"].add(ti)
            nel = self.nels[ti]
            gi, li = self.wloc[ti]
            self.P.dma("pool", self.S["wb%d" % gi][li][:, :nel], self.I["wt%d" % gi][li][:, :nel], writes=[("wb", ti)])

    def prologue(self):
        P, I, S = self.P, self.I, self.S
        seq = list(self.pro_seq)
        for i in self.tiles:
            for l in range(self.NL):
                seq += self.layer_seq[l]
        self.wstate = dict(seq=seq, pos=0, issued=0, cpos=0, casted=set())
        used = sorted(set(seq))
        P.dma("sp", self.vecs[:], I["vecs"][:, :], writes=[("vecs",)])
        P.dma("sp", self.cf[:], I["consts"][:, :], writes=[("cf",)])
        for a, b_, key in [("skt_a", "sKT0", "sKT0"), ("sv_a", "sV0", "sV0"), ("skt_b", "sKT1", "sKT1"), ("sv_b", "sV1", "sV1"),
                           ("slat", "sLat", "sLat"), ("skr", "sKR", "sKR"), ("skt_d", "sKT3", "sKT3"), ("sv_d", "sV3", "sV3")]:
            src, dst = I[a].ap(), S[b_].ap()
            P.dma("pool", dst, src, writes=[(key,)])
        for l in range(4):
            P.dma("pool", S["mK"][1, l], I["smk"][l], writes=[("mK", 1, l)])
            P.dma("pool", S["mV"][1, l], I["smv"][l], writes=[("mV", 1, l)])
        P.copy(self.cb[:], self.cf[:], reads=[("cf",)], writes=[("cb",)])
        P.op("dve", lambda g: g.memset(self.cst[:], 0.0), writes=[("cst",)])
        P.op("dve", lambda g: g.memset(self.sm[:], 0.0), writes=[("sm",)])
        P.op("dve", lambda g: g.memset(self.sm[:, 0:1], EPS), reads=[], writes=[("sm",)])
        li = lam_init(0)
        v = self.vecs
        P.tt(self.sm[:, 4:5], v[:, V_LAM:V_LAM + 1], v[:, V_LAM + 1:V_LAM + 2], ALU.mult, [("vecs",), ("sm",)], [("sm",)])
        P.tt(self.sm[:, 5:6], v[:, V_LAM + 2:V_LAM + 3], v[:, V_LAM + 3:V_LAM + 4], ALU.mult, [("vecs",), ("sm",)], [("sm",)])
        pb, pk = self.bank()
        P.mm(pb[:, 0:2], self.cf[:, C_ONES:C_ONES + 128], self.sm[:, 4:6], True, True, [("cf",), ("sm",)], [pk])
        P.act(self.sm[:, 6:8], pb[:, 0:2], AF.Exp, [pk, ("sm",)], [("sm",)])
        P.tt(self.sm[:, 8:9], self.sm[:, 7:8], self.sm[:, 6:7], ALU.subtract, [("sm",)], [("sm",)])
        P.ts(self.sm[:, 1:2], self.sm[:, 8:9], -li, None, ALU.add, None, [("sm",)], [("sm",)])
        P.ts(self.sm[:, 2:4], v[:, V_DG:V_DG + 2], 1.0 - li, None, ALU.mult, None, [("vecs",), ("sm",)], [("sm",)])
        P.dma("sp", self.xf[:, :, 0:256], I["memT"].ap().rearrange("(kc p) t -> p kc t", p=128), writes=R("xf", range(16)))
        P.copy(self.xb[:, :, 0:256], self.xf[:, :, 0:256], R("xf", range(16)), R("xb", range(16)))
        for l in range(4):
            W, wk = self.wnext()
            W3 = W[:, :].rearrange("p (k c) -> p k c", c=512)
            for j in range(4):
                pb, pk = self.bank()
                for kc in range(16):
                    P.mm(pb[:, 0:256], W3[:, kc, j * 128:(j + 1) * 128], self.xb[:, kc, 0:256], kc == 0, kc == 15, [wk] + R("xb", [kc]), [pk])
                s = self.rot("st", 3)
                P.copy(self.st[s][:, 0:256], pb[:, 0:256], [pk], [("st", s)], e="act")
                P.dma("pool", self.O["memkT"][l, j * 128:(j + 1) * 128, :], self.st[s][:, 0:256], reads=[("st", s)], writes=[("o_mk", l, j)])
                P.copy(self.mk[:, j, :], self.st[s][:, 0:256], [("st", s)], [("mk",)])
            P.dma("pool", S["mK"][0, l], self.mk[:], reads=[("mk",)], writes=[("mK", 0, l)])
            W, wk = self.wnext()
            W3 = W[:, :].rearrange("p (k c) -> p k c", c=512)
            for tb in range(2):
                pb, pk = self.bank()
                for kc in range(16):
                    P.mm(pb[:, :], self.xb[:, kc, tb * 128:(tb + 1) * 128], W3[:, kc, :], kc == 0, kc == 15, [wk] + R("xb", [kc]), [pk])
                s = self.rot("st", 3)
                P.copy(self.st[s][:, :], pb[:, :], [pk], [("st", s)], e="act")
                P.dma("pool", self.O["memv"][l, tb * 128:(tb + 1) * 128, :], self.st[s][:, :], reads=[("st", s)], writes=[("o_mv", l, tb)])
                P.copy(self.mv[:, tb, :], self.st[s][:, :], [("st", s)], [("mv",)])
            P.dma("pool", S["mV"][0, l], self.mv[:], reads=[("mv",)], writes=[("mV", 0, l)])

    def do_tile(self, i):
        P, I = self.P, self.I
        self.i = i
        self.samp = (i == NPT)
        self.n = NS if self.samp else NT
        self.tok0 = i * NT
        n, t0 = self.n, self.tok0
        P.barrier()
        if self.samp:
            P.dma("pool", self.O["conv_p"][:, :, :, :], self.cst[:], reads=[("cst",)], writes=[("o_convp",)])
            P.dma("sp", self.cst[:], I["cst_s"][:, :, :, :], reads=[], writes=[("cst",)])
        P.dma("sp", self.xf[:, :, :n], I["xT"].ap().rearrange("(kc p) t -> p kc t", p=128)[:, :, t0:t0 + n], writes=R("xf", range(16)))
        P.copy(self.xb[:, :, :n], self.xf[:, :, :n], R("xf", range(16)), R("xb", range(16)))
        for l in range(self.NL):
            self.layer(l)
        P.dma("pool", self.O["yT"].ap().rearrange("(kc p) t -> p kc t", p=128)[:, :, t0:t0 + n], self.xf[:, :, :n], reads=R("xf", range(16)), writes=[("o_y", i)])

    def proj_fm(self, W3, wk, cj, nk, src, srckey, n):
        P = self.P
        pb, pk = self.bank(0, 4)
        for kc in range(nk):
            P.mm(pb[:, :n], W3[:, kc, cj * 128:(cj + 1) * 128], src(kc), kc == 0, kc == nk - 1, [wk, (srckey, kc)], [pk])
        return pb, pk

    def rope(self, s, npart, perm_off, tabkey):
        P, n = self.P, self.n
        stt_ = self.st[s]
        pb, pk = self.bank(4, 6)
        P.mm(pb[0:npart, :n], self.cf[0:npart, perm_off:perm_off + npart], stt_[0:npart, :n], True, True, [("cf",), ("st", s)], [pk])
        t = self.rot("tm", 4)
        P.tt(self.tm[t][0:npart, :n], pb[0:npart, :n], self.tab[0:npart, 1, :n], ALU.mult, [pk, tabkey], [("tm", t)])
        P.tt(stt_[0:npart, :n], stt_[0:npart, :n], self.tab[0:npart, 0, :n], ALU.mult, [("st", s), tabkey], [("st", s)])
        P.tt(stt_[0:npart, :n], stt_[0:npart, :n], self.tm[t][0:npart, :n], ALU.add, [("st", s), ("tm", t)], [("st", s)])

    def kv_out_names(self, l):
        return self.O["kT%d" % l], self.O["v%d" % l]

    def layer(self, l):
        P, n, t0, i = self.P, self.n, self.tok0, self.i
        m = l % 4
        P.barrier()
        g = 1 if self.samp else 0
        P.dma("sp", self.mk[:], self.S["mK"][g, l], reads=[("mK", g, l)], writes=[("mk",)])
        P.dma("sp", self.mv[:], self.S["mV"][g, l], reads=[("mV", g, l)], writes=[("mv",)])
        if m == 2:
            self.mla_proj(l)
        else:
            self.qkv_proj(l)
        if m == 0:
            self.attn_diff(l)
        elif m == 1:
            self.attn_band(l)
        elif m == 2:
            self.attn_mla(l)
        else:
            self.attn_stick(l)
        self.attn_mem(l)
        self.out_proj_ffn(l)

    def qkv_proj(self, l):
        P, n, t0, i = self.P, self.n, self.tok0, self.i
        xsrc = lambda kc: self.xb[:, kc, :n]
        OK, OV = self.kv_out_names(l)
        if l == 0:
            P.dma("sp", self.tab[0:32, :, :n], self.I["tabA"][:, :, t0:t0 + n], writes=[("tab",)])
        for part in range(2):
            for jt in range(3):
                W, wk = self.wnext()
                W3 = W[:, :].rearrange("p (k c) -> p k c", c=512)
                for cj in range(4):
                    j = jt * 4 + cj
                    pb, pk = self.proj_fm(W3, wk, cj, 16, xsrc, "xb", n)
                    dst = self.qT(j) if part == 0 else self.kT(j)
                    dkey = ("qT", j) if part == 0 else ("kT", j)
                    if l == 0 or part == 1:
                        s = self.rot("st", 3)
                        P.copy(self.st[s][:, :n], pb[:, :n], [pk], [("st", s)], e="act")
                        if l == 0:
                            self.rope(s, 32, C_PERMA, ("tab",))
                        if part == 1:
                            P.dma("pool", OK[j * 128:(j + 1) * 128, t0:t0 + n], self.st[s][:, :n], reads=[("st", s)], writes=[("o_k", l, j, i)])
                            if l == 1 and self.samp:
                                P.dma("pool", self.O["bkT_s"][j * 128:(j + 1) * 128, 448:512], self.st[s][:, :n], reads=[("st", s)], writes=[("o_bks", j)])
                        P.copy(dst[:, :n], self.st[s][:, :n], [("st", s)], [dkey], e="pool")
                    else:
                        P.copy(dst[:, :n], pb[:, :n], [pk], [dkey], e="act")
        if not self.samp:
            P.dma("pool", self.S["KT%d" % l].ap().rearrange("j p t -> p j t")[:, :, t0:t0 + n],
                  self.reg[:, 12 * NT:24 * NT].rearrange("p (j t) -> p j t", t=NT), reads=R("kT", range(12)), writes=[("KTs", l, i)])
        ntb = (n + 127) // 128
        for jt in range(3):
            W, wk = self.wnext()
            W3 = W[:, :].rearrange("p (k c) -> p k c", c=512)
            for tb in range(ntb):
                np_ = min(128, n - tb * 128)
                pb, pk = self.bank(0, 4)
                for kc in range(16):
                    P.mm(pb[:np_, :], self.xb[:, kc, tb * 128:tb * 128 + np_], W3[:, kc, :], kc == 0, kc == 15, [wk, ("xb", kc)], [pk])
                s = self.rot("st", 3)
                P.copy(self.st[s][:np_, :], pb[:np_, :], [pk], [("st", s)], e="act")
                r0 = t0 + tb * 128
                P.dma("pool", OV[r0:r0 + np_, jt * 512:(jt + 1) * 512], self.st[s][:np_, :], reads=[("st", s)], writes=[("o_v", l, jt, tb, i)])
                if l == 1 and self.samp:
                    P.dma("pool", self.O["bv_s"][448:512, jt * 512:(jt + 1) * 512], self.st[s][:np_, :], reads=[("st", s)], writes=[("o_bvs", jt)])
                P.copy(self.vv(tb)[:np_, jt * 512:(jt + 1) * 512], self.st[s][:np_, :], [("st", s)], [("v", tb)], e="pool")
        if not self.samp:
            P.dma("pool", self.S["V%d" % l].ap()[t0:t0 + n, :].rearrange("(tb p) c -> p tb c", p=128),
                  self.reg[:, 24 * NT:24 * NT + 4 * 1536].rearrange("p (tb c) -> p tb c", c=1536), reads=R("v", range(4)), writes=[("Vs", l, i)])
        if l == 1 and self.samp:
            P.dma("pool", self.O["bkT_s"][:, 0:448], self.I["skt_b"].ap().rearrange("j p t -> (j p) t")[:, 64:512], writes=[("o_bks2",)])
            P.dma("pool", self.O["bv_s"][0:448, :], self.I["sv_b"][64:512, :], writes=[("o_bvs2",)])
        W, wk = self.wnext()
        W3 = W[:, :].rearrange("p (k c) -> p k c", c=512)
        for cj in range(4):
            pb, pk = self.proj_fm(W3, wk, cj, 16, xsrc, "xb", n)
            P.copy(self.qm(cj)[:, :n], pb[:, :n], [pk], [("qm", cj)], e="act")

    def kv_groups(self, l):
        S = self.S
        if self.samp:
            ng = 1 if l == 1 else 2
            return [(S["sKT%d" % l], S["sV%d" % l], S["sKR"], g, [("sKT%d" % l,), ("sV%d" % l,), ("sKR",)]) for g in range(ng)]
        return [(S["KT%d" % l], S["V%d" % l], S["KR"], g, [("KTs", l, g), ("Vs", l, g), ("KRs", g)]) for g in range(self.i)]

    def load_group(self, grp, j, vc0, dv, with_kr=False):
        P = self.P
        KT, V, KR, g, keys = grp
        s = self.rot("kvs", 2)
        P.dma("sp", self.ks[s][:, 0, :], KT[j, :, g * 512:(g + 1) * 512], reads=[keys[0]], writes=[("ks", s)])
        if with_kr:
            P.dma("sp", self.ks[s][:, 1, :], KR[:, g * 512:(g + 1) * 512], reads=[keys[2]], writes=[("ks", s)])
        P.dma("sp", self.vs[s][:, :, 0:dv], V.ap()[g * 512:(g + 1) * 512, vc0:vc0 + dv].rearrange("(kb p) c -> p kb c", p=128),
              reads=[keys[1]], writes=[("vs", s)])
        return s

    def softmax_heads(self, l, heads, scale, masked):
        P, n, i = self.P, self.n, self.i
        groups = self.kv_groups(l)
        items = [(hd, grp) for hd in heads for grp in groups]
        slots = {}

        def issue(k):
            if k < len(items) and k not in slots:
                hd, grp = items[k]
                slots[k] = self.load_group(grp, hd["kj"], hd["vc0"], hd["dv"], with_kr=hd.get("qr") is not None)

        k = 0
        ones_b = self.cb[:, C_ONES:C_ONES + 128]
        for hd in heads:
            q, qk = hd["q"]
            ndv = hd["dv"] // 128
            oacc = [(self.ps[:, 3 + c, :], ("ps", 3 + c)) for c in range(ndv)]
            sacc = (self.ps[:, 5, :], ("ps", 5))
            first = True
            ntb = (n + 127) // 128
            nblocks = 4 * len(groups) + ntb
            bi = 0

            def block(klhs, kkeys, krlhs, vfn, vkey, nk, q0, diag):
                nonlocal first, bi
                sb_, sk = self.bank(0, 3)
                if krlhs is None:
                    P.mm(sb_[:nk, q0:n], klhs, q[:, q0:n], True, True, kkeys + [qk], [sk])
                else:
                    qr, qrk, hp = hd["qr"]
                    P.mm(sb_[:nk, q0:n], klhs, q[:, q0:n], True, False, kkeys + [qk], [sk])
                    P.mm(sb_[:nk, q0:n], krlhs, qr[hp:hp + 64, q0:n], False, True, kkeys + [qrk], [sk])
                p = self.rot("pt", 3)
                pt = self.pt[p]
                P.act(pt[:nk, q0:n], sb_[:nk, q0:n], AF.Exp, [sk], [("pt", p)], scale=scale)
                if diag and masked:
                    w = min(128, n - q0)
                    P.tt(pt[:nk, q0:q0 + w], pt[:nk, q0:q0 + w], self.cb[:nk, C_MCC:C_MCC + w], ALU.mult, [("pt", p), ("cb",)], [("pt", p)])
                last = (bi == nblocks - 1)
                for c in range(ndv):
                    P.mm(oacc[c][0][:, q0:n], vfn(c), pt[:nk, q0:n], first, last, [vkey, ("pt", p)], [oacc[c][1]])
                P.mm(sacc[0][:, q0:n], ones_b[:nk, :], pt[:nk, q0:n], first, last, [("cb",), ("pt", p)], [sacc[1]])
                first = False
                bi += 1

            for grp in groups:
                issue(k)
                issue(k + 1)
                s = slots[k]
                k += 1
                for kb in range(4):
                    krl = None
                    if hd.get("qr") is not None:
                        hp = hd["qr"][2]
                        krl = self.ks[s][hp:hp + 64, 1, kb * 128:(kb + 1) * 128]
                    block(self.ks[s][:, 0, kb * 128:(kb + 1) * 128], [("ks", s)], krl,
                          lambda c, s=s, kb=kb: self.vs[s][:, kb, c * 128:(c + 1) * 128], ("vs", s), 128, 0, False)
            kown, kownkey = hd["kown"]
            for kb in range(ntb):
                nk = min(128, n - kb * 128)
                krl = None
                if hd.get("qr") is not None:
                    hp = hd["qr"][2]
                    krl = self.krT2[hp:hp + 64, kb * 128:kb * 128 + nk]
                block(kown[:, kb * 128:kb * 128 + nk], [kownkey] + ([("krT2",)] if krl is not None else []), krl,
                      lambda c, kb=kb, nk=nk: self.vv(kb)[:nk, hd["vc0"] + c * 128: hd["vc0"] + (c + 1) * 128], ("v", kb), nk, kb * 128, True)
            t = self.rot("tm", 4)
            P.op("dve", lambda g_: g_.reciprocal(self.tm[t][:, :n], sacc[0][:, :n]), [sacc[1]], [("tm", t)])
            hd["fin"](oacc, (self.tm[t], ("tm", t)))

    def attn_diff(self, l):
        P, n = self.P, self.n
        heads = []
        for h in range(6):
            for m in range(2):
                j = 2 * h + m

                def fin(oacc, rec, h=h, m=m):
                    for c in range(2):
                        P.tt(self.dtmv(2 * m + c)[:, :n], oacc[c][0][:, :n], rec[0][:, :n], ALU.mult, [oacc[c][1], rec[1]], [("dtm", 2 * m + c)])
                    if m == 1:
                        for c in range(2):
                            P.stt(self.dtmv(c)[:, :n], self.dtmv(2 + c)[:, :n], self.sm[:, 1:2], self.dtmv(c)[:, :n], ALU.mult, ALU.add,
                                  [("dtm", 2 + c), ("dtm", c), ("sm",)], [("dtm", c)])
                        pb, pk = self.bank(6, 8)
                        for c in range(2):
                            t = self.rot("tm", 4)
                            P.act(self.tm[t][:, :n], self.dtmv(c)[:, :n], AF.Square, [("dtm", c)], [("tm", t)])
                            P.mm(pb[:, :n], self.cf[:, C_ONES:C_ONES + 128], self.tm[t][:, :n], c == 0, c == 1, [("cf",), ("tm", t)], [pk])
                        t = self.rot("tm", 4)
                        P.act(self.tm[t][:, :n], pb[:, :n], AF.Sqrt, [pk, ("sm",)], [("tm", t)], bias=self.sm[:, 0:1], scale=1.0 / 256)
                        P.op("dve", lambda g_: g_.reciprocal(self.tm[t][:, :n], self.tm[t][:, :n]), [("tm", t)], [("tm", t)])
                        for c in range(2):
                            P.tt(self.dtmv(c)[:, :n], self.dtmv(c)[:, :n], self.tm[t][:, :n], ALU.mult, [("dtm", c), ("tm", t)], [("dtm", c)])
                            P.ts(self.oT[:, 2 * h + c, :n], self.dtmv(c)[:, :n], self.sm[:, 2 + c:3 + c], None, ALU.mult, None,
                                 [("dtm", c), ("sm",)], [("oT", 2 * h + c)], e="pool")

                heads.append(dict(q=(self.qT(j), ("qT", j)), kj=j, kown=(self.kT(j), ("kT", j)), vc0=256 * h, dv=256, fin=fin))
        self.softmax_heads(l, heads, 128 ** -0.5, True)

    def attn_mem(self, l):
        P, n = self.P, self.n
        ones_b = self.cb[:, C_ONES:C_ONES + 128]
        for h in range(4):
            oacc = (self.ps[:, 3, :], ("ps", 3))
            sacc = (self.ps[:, 5, :], ("ps", 5))
            for kb in range(2):
                sb_, sk = self.bank(0, 3)
                P.mm(sb_[:, :n], self.mk[:, h, kb * 128:(kb + 1) * 128], self.qm(h)[:, :n], True, True, [("mk",), ("qm", h)], [sk])
                p = self.rot("pt", 3)
                P.act(self.pt[p][:, :n], sb_[:, :n], AF.Exp, [sk], [("pt", p)], scale=128 ** -0.5)
                P.mm(oacc[0][:, :n], self.mv[:, kb, h * 128:(h + 1) * 128], self.pt[p][:, :n], kb == 0, kb == 1, [("mv",), ("pt", p)], [oacc[1]])
                P.mm(sacc[0][:, :n], ones_b, self.pt[p][:, :n], kb == 0, kb == 1, [("cb",), ("pt", p)], [sacc[1]])
            t = self.rot("tm", 4)
            P.op("dve", lambda g_: g_.reciprocal(self.tm[t][:, :n], sacc[0][:, :n]), [sacc[1]], [("tm", t)])
            P.tt(self.oT[:, 12 + h, :n], oacc[0][:, :n], self.tm[t][:, :n], ALU.mult, [oacc[1], ("tm", t)], [("oT", 12 + h)])

    def attn_band(self, l):
        P, n, i = self.P, self.n, self.i
        ones_b = self.cb[:, C_ONES:C_ONES + 128]
        has_prev = self.samp or i > 0
        npb = (n + 127) // 128
        for h in range(12):
            P.dma("sp", self.ebf[:], self.I["bbias"][:, h], writes=[("ebf",)])
            P.act(self.ebf[:], self.ebf[:], AF.Exp, [("ebf",)], [("ebf",)])
            s = None
            if has_prev:
                s = self.rot("kvs", 2)
                if self.samp:
                    KT, V, g0, kk = self.S["sKT1"], self.S["sV1"], 0, [("sKT1",), ("sV1",)]
                else:
                    KT, V, g0, kk = self.S["KT1"], self.S["V1"], (i - 1) * 512, [("KTs", 1, i - 1), ("Vs", 1, i - 1)]
                P.dma("sp", self.ks[s][:, 0, :], KT[h, :, g0:g0 + 512], reads=[kk[0]], writes=[("ks", s)])
                P.dma("sp", self.vs[s][:, :, 0:128], V.ap()[g0:g0 + 512, h * 128:(h + 1) * 128].rearrange("(kb p) c -> p kb c", p=128),
                      reads=[kk[1]], writes=[("vs", s)])
            oacc = (self.ps[:, 3, :], ("ps", 3))
            sacc = (self.ps[:, 5, :], ("ps", 5))
            first = True
            rs = [4, 3, 2, 1, 0]
            valid_r = [r for r in rs if has_prev or r == 4 or (npb - 1) - 4 + r >= 0]
            for ri, r in enumerate(valid_r):
                pbs = [pb_ for pb_ in range(npb) if (pb_ - 4 + r >= 0) or has_prev]
                pb0 = pbs[0]
                sb_, sk = self.bank(0, 3)
                for pb_ in pbs:
                    lb = pb_ - 4 + r
                    nq = min(128, n - pb_ * 128)
                    if lb < 0:
                        klhs, kkey, nk = self.ks[s][:, 0, (4 + lb) * 128:(5 + lb) * 128], ("ks", s), 128
                    else:
                        nk = min(128, n - lb * 128)
                        klhs, kkey = self.kT(h)[:, lb * 128:lb * 128 + nk], ("kT", h)
                    P.mm(sb_[:nk, pb_ * 128:pb_ * 128 + nq], klhs, self.qT(h)[:, pb_ * 128:pb_ * 128 + nq], True, True, [kkey, ("qT", h)], [sk])
                nk_all = nk
                c0, c1 = pb0 * 128, n
                p = self.rot("pt", 3)
                pt = self.pt[p]
                P.act(pt[:nk_all, c0:c1], sb_[:nk_all, c0:c1], AF.Exp, [sk], [("pt", p)], scale=128 ** -0.5)
                nqq = min(128, n)
                npbs = len(pbs)
                ptv = pt[:nk_all, c0:c1].rearrange("p (a b) -> p a b", b=nqq)
                ebv = self.ebf[:nk_all, r:r + 1, 0:nqq].broadcast_to([nk_all, npbs, nqq])
                P.tt(ptv, ptv, ebv, ALU.mult, [("pt", p), ("ebf",)], [("pt", p)])
                last = (ri == len(valid_r) - 1)
                for pb_ in pbs:
                    lb = pb_ - 4 + r
                    nq = min(128, n - pb_ * 128)
                    if lb < 0:
                        vl, vkey, nk = self.vs[s][:, 4 + lb, 0:128], ("vs", s), 128
                    else:
                        nk = min(128, n - lb * 128)
                        vl, vkey = self.vv(lb)[:nk, h * 128:(h + 1) * 128], ("v", lb)
                    P.mm(oacc[0][:, pb_ * 128:pb_ * 128 + nq], vl, pt[:nk, pb_ * 128:pb_ * 128 + nq], first, last and pb_ == pbs[-1], [vkey, ("pt", p)], [oacc[1]])
                    first = False
                P.mm(sacc[0][:, c0:c1], ones_b[:nk_all, :], pt[:nk_all, c0:c1], ri == 0, last, [("cb",), ("pt", p)], [sacc[1]])
            t = self.rot("tm", 4)
            P.op("dve", lambda g_: g_.reciprocal(self.tm[t][:, :n], sacc[0][:, :n]), [sacc[1]], [("tm", t)])
            P.tt(self.oT[:, h, :n], oacc[0][:, :n], self.tm[t][:, :n], ALU.mult, [oacc[1], ("tm", t)], [("oT", h)])

    def attn_stick(self, l):
        P, n, i = self.P, self.n, self.i
        groups = self.kv_groups(l)
        rgroups = list(reversed(groups))
        items = [(h, grp) for h in range(12) for grp in rgroups]
        slots = {}

        def issue(k):
            if k < len(items) and k not in slots:
                h, grp = items[k]
                slots[k] = self.load_group(grp, h, h * 128, 128)

        k = 0
        scale = 128 ** -0.5
        onesf = self.cf[:, C_ONES:C_ONES + 128]
        ntb = (n + 127) // 128
        for h in range(12):
            oacc = (self.ps[:, 3, :], ("ps", 3))
            P.op("pool", lambda g_: g_.memset(self.spacc[:, :n], 0.0), [], [("ebf",)])
            state = dict(first=True, bi=0, has_acc=False)
            nblocks = 4 * len(groups) + ntb

            def block(klhs, kkey, vl, vkey, nk, q0, diag):
                sb_, sk = self.bank(0, 3)
                P.mm(sb_[:nk, q0:n], klhs, self.qT(h)[:, q0:n], True, True, [kkey, ("qT", h)], [sk])
                te = self.rot("tm", 4)
                P.act(self.tm[te][:nk, q0:n], sb_[:nk, q0:n], AF.Exp, [sk], [("tm", te)], scale=scale)
                tsp = self.rot("st", 3)
                sp = self.st[tsp]
                P.act(sp[:nk, q0:n], self.tm[te][:nk, q0:n], AF.Ln, [("tm", te)], [("st", tsp)], bias=1.0)
                if diag:
                    w = min(128, n - q0)
                    P.tt(sp[:nk, q0:q0 + w], sp[:nk, q0:q0 + w], self.cf[:nk, C_MSC:C_MSC + w], ALU.mult, [("st", tsp), ("cf",)], [("st", tsp)])
                tb_, tk = self.bank(6, 8)
                P.mm(tb_[:nk, q0:n], self.cf[:nk, C_U:C_U + nk], sp[:nk, q0:n], True, not state["has_acc"], [("cf",), ("st", tsp)], [tk])
                if state["has_acc"]:
                    P.mm(tb_[:nk, q0:n], onesf[:, :nk], self.spacc[:, q0:n], False, True, [("cf",), ("ebf",)], [tk])
                ta = self.rot("tm", 4)
                arg = self.tm[ta]
                P.stt(arg[:nk, q0:n], sb_[:nk, q0:n], scale, sp[:nk, q0:n], ALU.mult, ALU.subtract, [sk, ("st", tsp)], [("tm", ta)])
                P.tt(arg[:nk, q0:n], arg[:nk, q0:n], tb_[:nk, q0:n], ALU.subtract, [("tm", ta), tk], [("tm", ta)])
                P.tt(self.spacc[:nk, q0:n], self.spacc[:nk, q0:n], sp[:nk, q0:n], ALU.add, [("ebf",), ("st", tsp)], [("ebf",)], e="pool")
                state["has_acc"] = True
                p = self.rot("pt", 3)
                pt = self.pt[p]
                P.act(pt[:nk, q0:n], arg[:nk, q0:n], AF.Exp, [("tm", ta)], [("pt", p)])
                if diag:
                    w = min(128, n - q0)
                    P.tt(pt[:nk, q0:q0 + w], pt[:nk, q0:q0 + w], self.cb[:nk, C_MSC:C_MSC + w], ALU.mult, [("pt", p), ("cb",)], [("pt", p)])
                last = state["bi"] == nblocks - 1
                P.mm(oacc[0][:, q0:n], vl, pt[:nk, q0:n], state["first"], last, [vkey, ("pt", p)], [oacc[1]])
                state["first"] = False
                state["bi"] += 1

            for kb in reversed(range(ntb)):
                nk = min(128, n - kb * 128)
                block(self.kT(h)[:, kb * 128:kb * 128 + nk], ("kT", h), self.vv(kb)[:nk, h * 128:(h + 1) * 128], ("v", kb), nk, kb * 128, True)
            for grp in rgroups:
                issue(k)
                issue(k + 1)
                s = slots[k]
                k += 1
                for kb in reversed(range(4)):
                    block(self.ks[s][:, 0, kb * 128:(kb + 1) * 128], ("ks", s), self.vs[s][:, kb, 0:128], ("vs", s), 128, 0, False)
            P.copy(self.oT[:, h, :n], oacc[0][:, :n], [oacc[1]], [("oT", h)], e="act")

    def rms_fm(self, src, srckey, nch, dim, gofs, outs):
        P, n = self.P, self.n
        pb, pk = self.bank(6, 8)
        for c in range(nch):
            t = self.rot("tm", 4)
            P.act(self.tm[t][:, :n], src(c)[:, :n], AF.Square, [(srckey, c)], [("tm", t)])
            P.mm(pb[:, :n], self.cf[:, C_ONES:C_ONES + 128], self.tm[t][:, :n], c == 0, c == nch - 1, [("cf",), ("tm", t)], [pk])
        tr = self.rot("hb", 2)
        rs = self.hb[tr]
        P.act(rs[:, :n], pb[:, :n], AF.Sqrt, [pk, ("sm",)], [("hb", tr)], bias=self.sm[:, 0:1], scale=1.0 / dim)
        P.op("dve", lambda g_: g_.reciprocal(rs[:, :n], rs[:, :n]), [("hb", tr)], [("hb", tr)])
        for c in range(nch):
            t = self.rot("tm", 4)
            P.tt(self.tm[t][:, :n], src(c)[:, :n], rs[:, :n], ALU.mult, [(srckey, c), ("hb", tr)], [("tm", t)])
            for (fn, key) in outs:
                P.ts(fn(c)[:, :n], self.tm[t][:, :n], self.vecs[:, gofs + c:gofs + c + 1], None, ALU.mult, None, [("tm", t), ("vecs",)], [(key, c)], e="pool")

    def mla_proj(self, l):
        P, n, t0, i = self.P, self.n, self.tok0, self.i
        xsrc = lambda kc: self.xb[:, kc, :n]
        regf = self.reg[:, 0:20 * NT].bitcast(F32)
        cqf = lambda c: regf[:, c * NT:(c + 1) * NT]
        lf = lambda c: regf[:, (6 + c) * NT:(7 + c) * NT]
        cqn = lambda c: self.oT[:, c, :]
        latT = lambda c: self.oT[:, 6 + c, :]
        self.krT2 = self.reg[:, 43 * NT:44 * NT]
        P.dma("sp", self.tab[:, :, :n], self.I["tabC"][:, :, t0:t0 + n], writes=[("tab",)])
        W, wk = self.wnext(); W3 = W[:, :].rearrange("p (k c) -> p k c", c=512)
        for cj in range(4):
            pb, pk = self.proj_fm(W3, wk, cj, 16, xsrc, "xb", n)
            P.copy(cqf(cj)[:, :n], pb[:, :n], [pk], [("cqf", cj)], e="act")
        W, wk = self.wnext(); W3 = W[:, :].rearrange("p (k c) -> p k c", c=512)
        for cj in range(2):
            pb, pk = self.proj_fm(W3, wk, cj, 16, xsrc, "xb", n)
            P.copy(cqf(4 + cj)[:, :n], pb[:, :n], [pk], [("cqf", 4 + cj)], e="act")
        pb, pk = self.bank(0, 4)
        for kc in range(16):
            P.mm(pb[0:64, :n], W3[:, kc, 256:320], xsrc(kc), kc == 0, kc == 15, [wk, ("xb", kc)], [pk])
        s = self.rot("st", 3)
        P.copy(self.st[s][0:64, :n], pb[0:64, :n], [pk], [("st", s)], e="act")
        pb2, pk2 = self.bank(4, 6)
        P.mm(pb2[:, :n], self.cf[0:64, C_DUP:C_DUP + 128], self.st[s][0:64, :n], True, True, [("cf",), ("st", s)], [pk2])
        s2 = self.rot("st", 3)
        P.copy(self.st[s2][:, :n], pb2[:, :n], [pk2], [("st", s2)], e="act")
        self.rope(s2, 128, C_PERMC, ("tab",))
        P.dma("pool", self.O["krT"][:, t0:t0 + n], self.st[s2][0:64, :n], reads=[("st", s2)], writes=[("o_kr", i)])
        P.copy(self.krT2[:, :n], self.st[s2][:, :n], [("st", s2)], [("krT2",)], e="pool")
        if not self.samp:
            P.dma("pool", self.S["KR"][:, t0:t0 + n], self.krT2[:, :n], reads=[("krT2",)], writes=[("KRs", i)])
        W, wk = self.wnext(); W3 = W[:, :].rearrange("p (k c) -> p k c", c=512)
        for cj in range(4):
            pb, pk = self.proj_fm(W3, wk, cj, 16, xsrc, "xb", n)
            P.copy(lf(cj)[:, :n], pb[:, :n], [pk], [("lf", cj)], e="act")
        W, wk = self.wnext(); W3 = W[:, :].rearrange("p (k c) -> p k c", c=512)
        for cj in range(4):
            pb, pk = self.proj_fm(W3, wk, cj, 16, xsrc, "xb", n)
            P.copy(self.qm(cj)[:, :n], pb[:, :n], [pk], [("qm", cj)], e="act")
        self.rms_fm(cqf, "cqf", 6, 768, V_QG, [(cqn, "cqn")])
        self.rms_fm(lf, "lf", 4, 512, V_KG, [(lf, "lf2"), (latT, "latT")])
        for c in range(4):
            P.dma("pool", self.O["latT"][c * 128:(c + 1) * 128, t0:t0 + n], lf(c)[:, :n], reads=[("lf2", c)], writes=[("o_lat", c, i)])
        P.barrier()
        csrc = lambda kc: cqn(kc)[:, :n]
        for jt in range(5):
            W, wk = self.wnext(); W3 = W[:, :].rearrange("p (k c) -> p k c", c=512)
            for cj in range(4):
                jglob = jt * 4 + cj
                if jglob >= 18:
                    break
                pb, pk = self.proj_fm(W3, wk, cj, 6, csrc, "cqn", n)
                if jglob < 12:
                    P.copy(self.qT(jglob)[:, :n], pb[:, :n], [pk], [("qT", jglob)], e="act")
                else:
                    jj = jglob - 12
                    s = self.rot("st", 3)
                    P.copy(self.st[s][:, :n], pb[:, :n], [pk], [("st", s)], e="act")
                    self.rope(s, 128, C_PERMC, ("tab",))
                    P.copy(self.qrb(jj)[:, :n], self.st[s][:, :n], [("st", s)], [("qrb", jj)], e="pool")
        sets = [("past", 0), ("past", 1)] if self.samp else []
        sets.append(("own", None))
        vsf = [self.vs[k_][:, :, :].rearrange("p a b -> p (a b)") for k_ in range(2)]
        plat = lambda kc: vsf[kc // 2][:, (kc % 2) * 512:(kc % 2 + 1) * 512]
        for jt in range(6):
            W, wk = self.wnext(); W3 = W[:, :].rearrange("p (k c) -> p k c", c=512)
            for kind, g in sets:
                if kind == "past":
                    for kc in range(4):
                        P.dma("sp", plat(kc), self.S["sLat"][kc, :, g * 512:(g + 1) * 512], reads=[("sLat",)], writes=[("vs", kc // 2)])
                    src = lambda kc: plat(kc)
                    rkey = lambda kc: ("vs", kc // 2)
                    nn = 512
                else:
                    src = lambda kc: latT(kc)[:, :n]
                    rkey = lambda kc: ("latT", kc)
                    nn = n
                if jt < 3:
                    for cj in range(4):
                        j = jt * 4 + cj
                        pb, pk = self.bank(0, 4)
                        for kc in range(4):
                            P.mm(pb[:, :nn], W3[:, kc, cj * 128:(cj + 1) * 128], src(kc), kc == 0, kc == 3, [wk, rkey(kc)], [pk])
                        if kind == "own":
                            P.copy(self.kT(j)[:, :n], pb[:, :n], [pk], [("kT", j)], e="act")
                        else:
                            p = self.rot("pt", 3)
                            P.copy(self.pt[p][:, :], pb[:, :], [pk], [("pt", p)], e="act")
                            P.dma("pool", self.S["sKT2"][j, :, g * 512:(g + 1) * 512], self.pt[p][:, :], reads=[("pt", p)], writes=[("sKT2",)])
                else:
                    c0 = (jt - 3) * 512
                    for tb in range((nn + 127) // 128):
                        np_ = min(128, nn - tb * 128)
                        pb, pk = self.bank(0, 4)
                        for kc in range(4):
                            P.mm(pb[:np_, :], src(kc)[:, tb * 128:tb * 128 + np_], W3[:, kc, :], kc == 0, kc == 3, [wk, rkey(kc)], [pk])
                        if kind == "own":
                            P.copy(self.vv(tb)[:np_, c0:c0 + 512], pb[:np_, :], [pk], [("v", tb)], e="act")
                        else:
                            p = self.rot("pt", 3)
                            P.copy(self.pt[p][:, :], pb[:, :], [pk], [("pt", p)], e="act")
                            r0 = g * 512 + tb * 128
                            P.dma("pool", self.S["sV2"][r0:r0 + 128, c0:c0 + 512], self.pt[p][:, :], reads=[("pt", p)], writes=[("sV2",)])
        if not self.samp:
            P.dma("pool", self.S["KT2"].ap().rearrange("j p t -> p j t")[:, :, t0:t0 + n],
                  self.reg[:, 12 * NT:24 * NT].rearrange("p (j t) -> p j t", t=NT), reads=R("kT", range(12)), writes=[("KTs", 2, i)])
            P.dma("pool", self.S["V2"].ap()[t0:t0 + n, :].rearrange("(tb p) c -> p tb c", p=128),
                  self.reg[:, 24 * NT:24 * NT + 4 * 1536].rearrange("p (tb c) -> p tb c", c=1536), reads=R("v", range(4)), writes=[("Vs", 2, i)])

    def attn_mla(self, l):
        P, n = self.P, self.n
        P.barrier()
        heads = []
        for h in range(12):
            def fin(oacc, rec, h=h):
                P.tt(self.oT[:, h, :n], oacc[0][0][:, :n], rec[0][:, :n], ALU.mult, [oacc[0][1], rec[1]], [("oT", h)])
            heads.append(dict(q=(self.qT(h), ("qT", h)), qr=(self.qrb(h // 2), ("qrb", h // 2), 64 * (h % 2)), kj=h,
                              kown=(self.kT(h), ("kT", h)), vc0=128 * h, dv=128, fin=fin))
        self.softmax_heads(l, heads, 192 ** -0.5, True)

    def layer_norm(self, l, which):
        P, n = self.P, self.n
        go = V_LN + ((2 * which) * 4 + l) * 16
        bo = V_LN + ((2 * which + 1) * 4 + l) * 16
        onesf = self.cf[:, C_ONES:C_ONES + 128]
        pa, pak = self.ps[:, 6, :], ("ps", 6)
        pq, pqk = self.ps[:, 7, :], ("ps", 7)
        for c in range(16):
            P.mm(pa[:, :n], onesf, self.xf[:, c, :n], c == 0, c == 15, [("cf",), ("xf", c)], [pak])
        for c in range(16):
            t = self.rot("tm", 4)
            P.act(self.tm[t][:, :n], self.xf[:, c, :n], AF.Square, [("xf", c)], [("tm", t)])
            P.mm(pq[:, :n], onesf, self.tm[t][:, :n], c == 0, c == 15, [("cf",), ("tm", t)], [pqk])
        mu, rstd = self.hb[0], self.hb[1]
        P.act(mu[:, :n], pa[:, :n], AF.Identity, [pak], [("hb", 0)], scale=1.0 / D)
        t = self.rot("tm", 4)
        P.tt(self.tm[t][:, :n], mu[:, :n], mu[:, :n], ALU.mult, [("hb", 0)], [("tm", t)])
        P.stt(rstd[:, :n], pq[:, :n], 1.0 / D, self.tm[t][:, :n], ALU.mult, ALU.subtract, [pqk, ("tm", t)], [("hb", 1)])
        P.act(rstd[:, :n], rstd[:, :n], AF.Sqrt, [("hb", 1), ("sm",)], [("hb", 1)], bias=self.sm[:, 0:1])
        P.op("dve", lambda g_: g_.reciprocal(rstd[:, :n], rstd[:, :n]), [("hb", 1)], [("hb", 1)])
        P.stt(mu[:, :n], mu[:, :n], -1.0, rstd[:, :n], ALU.mult, ALU.mult, [("hb", 0), ("hb", 1)], [("hb", 0)])
        for c in range(16):
            t = self.rot("tm", 4)
            P.tt(self.tm[t][:, :n], self.xf[:, c, :n], rstd[:, :n], ALU.mult, [("xf", c), ("hb", 1)], [("tm", t)])
            P.tt(self.tm[t][:, :n], self.tm[t][:, :n], mu[:, :n], ALU.add, [("tm", t), ("hb", 0)], [("tm", t)], e="pool")
            P.act(self.xf[:, c, :n], self.tm[t][:, :n], AF.Identity, [("tm", t), ("vecs",)], [("xf", c)],
                  bias=self.vecs[:, bo + c:bo + c + 1], scale=self.vecs[:, go + c:go + c + 1])
            P.copy(self.xb[:, c, :n], self.xf[:, c, :n], [("xf", c)], [("xb", c)], e="pool")

    def out_proj_ffn(self, l):
        P, n, i = self.P, self.n, self.i
        for jt in range(4):
            W, wk = self.wnext(); W3 = W[:, :].rearrange("p (k c) -> p k c", c=512)
            for cj in range(4):
                oc = jt * 4 + cj
                pb, pk = self.bank(0, 4)
                for kc in range(16):
                    P.mm(pb[:, :n], W3[:, kc, cj * 128:(cj + 1) * 128], self.oT[:, kc, :n], kc == 0, kc == 15, [wk, ("oT", kc)], [pk])
                P.stt(self.xf[:, oc, :n], self.xf[:, oc, :n], ALPHA, pb[:, :n], ALU.mult, ALU.add, [("xf", oc), pk], [("xf", oc)])
        self.layer_norm(l, 0)
        P.barrier()
        cst = self.cst
        ckey = ("cst",)
        v = self.vecs
        for t in range(22):
            W, wk = self.wnext(); W3 = W[:, :].rearrange("p (k c) -> p k c", c=512)
            hbufs = {}
            for cj in range(4):
                g = 2 * t + (cj % 2)
                ch = g if cj < 2 else 44 + g
                pb, pk = self.bank(0, 6)
                for kc in range(16):
                    P.mm(pb[:, :n], W3[:, kc, cj * 128:(cj + 1) * 128], self.xb[:, kc, :n], kc == 0, kc == 15, [wk, ("xb", kc)], [pk])
                u = self.rot("ue", 2)
                ue = self.ue[u]
                P.copy(ue[:, 0:2], cst[:, l, ch, :], [ckey], [("ue", u)], e="pool")
                P.copy(ue[:, 2:2 + n], pb[:, :n], [pk, ("ue", u)], [("ue", u)], e="act")
                P.copy(cst[:, l, ch, :], ue[:, n:n + 2], [("ue", u)], [ckey], e="pool")
                hbk = self.rot("hbc", 4)
                hb = self.hcv[hbk]
                cw = lambda j_: v[:, V_CW + (l * 3 + j_) * 88 + ch: V_CW + (l * 3 + j_) * 88 + ch + 1]
                cbias = v[:, V_CB + l * 88 + ch: V_CB + l * 88 + ch + 1]
                P.ts(hb[:, :n], ue[:, 2:2 + n], cw(2), cbias, ALU.mult, ALU.add, [("ue", u), ("vecs",)], [("hcv", hbk)])
                P.stt(hb[:, :n], ue[:, 1:1 + n], cw(1), hb[:, :n], ALU.mult, ALU.add, [("ue", u), ("vecs",), ("hcv", hbk)], [("hcv", hbk)])
                P.stt(hb[:, :n], ue[:, 0:n], cw(0), hb[:, :n], ALU.mult, ALU.add, [("ue", u), ("vecs",), ("hcv", hbk)], [("hcv", hbk)])
                hbufs[cj] = hbk
            for gg in range(2):
                g = 2 * t + gg
                a, b_ = hbufs[gg], hbufs[2 + gg]
                P.act(self.hcv[a][:, :n], self.hcv[a][:, :n], AF.Silu, [("hcv", a)], [("hcv", a)])
                P.tt(self.hT(g)[:, :n], self.hcv[a][:, :n], self.hcv[b_][:, :n], ALU.mult, [("hcv", a), ("hcv", b_)], [("hT", g)], e="pool")
        for oc in range(16):
            W, wk = self.wnext(); Wd = W[:, 0:44 * 128].rearrange("p (k c) -> p k c", c=128)
            pb, pk = self.bank(0, 6)
            for fc in range(44):
                P.mm(pb[:, :n], Wd[:, fc, :], self.hT(fc)[:, :n], fc == 0, fc == 43, [wk, ("hT", fc)], [pk])
            P.stt(self.xf[:, oc, :n], self.xf[:, oc, :n], ALPHA, pb[:, :n], ALU.mult, ALU.add, [("xf", oc), pk], [("xf", oc)])
        self.layer_norm(l, 1)


def run(inputs, cfg):
    k = Kern(cfg)
    nc = k.build()
    wt, nels, pro_seq, layer_seq = build_weight_tiles(inputs)
    shared = build_shared(inputs)
    in_maps = []
    for c in range(8):
        d = dict(shared)
        for gi, grp in enumerate([pro_seq] + layer_seq):
            d["wt%d" % gi] = wt[grp[0]:grp[-1] + 1] if gi <= k.NL else wt[grp[0]:grp[0] + 1]
        d.update(build_core_inputs(inputs, c))
        in_maps.append(d)
    import time as _t
    t0_ = _t.time()
    res = run_bass_kernel_spmd(nc, in_maps, core_ids=list(range(8)))
    print("spmd run seconds", _t.time() - t0_, flush=True)
    return res.results


def assemble(R_):
    f = np.float32
    def P2(fn):
        return np.stack([fn(R_[b]) for b in range(2)])
    def S8(fn):
        return np.stack([fn(R_[c]) for c in range(8)])
    y_p = P2(lambda r: r["yT"][:, :TP].T)
    y_s = S8(lambda r: r["yT"][:, TP:].T)
    def kfm(r, name, sl, H, dd):
        a = r[name][:, sl].T
        return a.reshape(a.shape[0], H, dd)
    pa, sa = slice(0, TP), slice(TP, TOT)
    outs = [y_p, y_s]
    outs.append(P2(lambda r: kfm(r, "kT0", pa, 6, 256))[None])
    outs.append(P2(lambda r: r["v0"][pa].reshape(TP, 6, 256))[None])
    outs.append(P2(lambda r: kfm(r, "kT1", slice(TP - 512, TP), 12, 128))[None])
    outs.append(P2(lambda r: r["v1"][TP - 512:TP].reshape(512, 12, 128))[None])
    outs.append(P2(lambda r: r["latT"][:, pa].T)[None])
    outs.append(P2(lambda r: r["krT"][:, pa].T)[None])
    outs.append(P2(lambda r: kfm(r, "kT3", pa, 12, 128))[None])
    outs.append(P2(lambda r: r["v3"][pa].reshape(TP, 12, 128))[None])
    outs.append(np.stack([np.stack([R_[b]["memkT"][l].T.reshape(256, 4, 128) for b in range(2)]) for l in range(4)]))
    outs.append(np.stack([np.stack([R_[b]["memv"][l].reshape(256, 4, 128) for b in range(2)]) for l in range(4)]))
    def conv(a):
        return a.transpose(1, 3, 2, 0).reshape(4, 2, 88 * 128)
    outs.append(np.stack([conv(R_[b]["conv_p"]) for b in range(2)], axis=1))
    outs.append(S8(lambda r: kfm(r, "kT0", sa, 6, 256))[None])
    outs.append(S8(lambda r: r["v0"][sa].reshape(NS, 6, 256))[None])
    outs.append(S8(lambda r: r["bkT_s"].T.reshape(512, 12, 128))[None])
    outs.append(S8(lambda r: r["bv_s"].reshape(512, 12, 128))[None])
    outs.append(S8(lambda r: r["latT"][:, sa].T)[None])
    outs.append(S8(lambda r: r["krT"][:, sa].T)[None])
    outs.append(S8(lambda r: kfm(r, "kT3", sa, 12, 128))[None])
    outs.append(S8(lambda r: r["v3"][sa].reshape(NS, 12, 128))[None])
    outs.append(np.stack([conv(R_[c]["conv_s"]) for c in range(8)], axis=1))
    return tuple(np.ascontiguousarray(o.astype(f)) for o in outs)


def kernel(**inputs):
    inputs = {k: np.asarray(v) for k, v in inputs.items()}
    R_ = run(inputs, {})
    return assemble(R_)
```

```python
import contextlib
import math
import numpy as np
import concourse.bass as bass
import concourse.mybir as mybir
from concourse.bass_utils import run_bass_kernel_spmd

F32 = mybir.dt.float32
BF16 = mybir.dt.bfloat16
AF = mybir.ActivationFunctionType
ALU = mybir.AluOpType

D = 2048
NT = 512
NPT = 8
TP = 4096
NS = 64
TOT = TP + NS
PAST = 1024
DFF = 5632
ALPHA = 8 ** 0.25
EPS = 1e-5
WEL = 8192
SAME_ENG_SYNC = True
NDS = 40
MASKV = -1.0e4

V_LN = 0
V_CW = V_LN + 4 * 4 * 16
V_CB = V_CW + 4 * 3 * 88
V_DG = V_CB + 4 * 88
V_QG = V_DG + 2
V_KG = V_QG + 6
V_LAM = V_KG + 4
NV = V_LAM + 4
C_ONES = 0
C_PERMA = 128
C_PERMC = 160
C_U = 288
C_MCC = 416
C_MSC = 544
C_DUP = 672
NC = 800


def lam_init(i):
    return 0.8 - 0.6 * math.exp(-0.3 * i)


def _colblock(w, cols):
    kc = w.shape[0] // 128
    t = np.zeros((128, 16, 512), np.float32)
    t[:, :kc, :len(cols)] = w[:, cols].reshape(kc, 128, len(cols)).transpose(1, 0, 2)
    return t.reshape(128, WEL), kc * 512


def _downblock(w, oc):
    t = np.zeros((128, WEL), np.float32)
    t[:, :44 * 128] = w[:, oc * 128:(oc + 1) * 128].reshape(44, 128, 128).transpose(1, 0, 2).reshape(128, 44 * 128)
    return t, 44 * 128


def build_weight_tiles(inp):
    tiles, nels = [], []
    layer_seq = [[] for _ in range(4)]
    pro_seq = []

    def add(t, lst):
        lst.append(len(tiles))
        tiles.append(t[0])
        nels.append(t[1])

    for l in range(4):
        wm = inp["w_mem_kv"][l]
        add(_colblock(wm, np.arange(0, 512)), pro_seq)
        add(_colblock(wm, np.arange(512, 1024)), pro_seq)
    w_in = [inp["w_in_a"][0], inp["w_in_b"][0], inp["w_in_c"][0], inp["w_in_d"][0]]
    for l in range(4):
        seq = layer_seq[l]
        w = w_in[l]
        if l != 2:
            for j in range(10):
                add(_colblock(w, np.arange(j * 512, (j + 1) * 512)), seq)
        else:
            add(_colblock(w, np.arange(0, 512)), seq)
            add(_colblock(w, np.concatenate([np.arange(512, 768), np.arange(1280, 1344)])), seq)
            add(_colblock(w, np.arange(768, 1280)), seq)
            add(_colblock(w, np.arange(1344, 1856)), seq)
            wq = inp["mla_w_uq"][0]
            nope = np.concatenate([np.arange(h * 192, h * 192 + 128) for h in range(12)])
            rope = np.concatenate([np.arange(h * 192 + 128, h * 192 + 192) for h in range(12)])
            qcols = np.concatenate([nope, rope])
            for j in range(5):
                add(_colblock(wq, qcols[j * 512:(j + 1) * 512]), seq)
            wkv = inp["mla_w_ukv"][0]
            kn = np.concatenate([np.arange(h * 256, h * 256 + 128) for h in range(12)])
            vv = np.concatenate([np.arange(h * 256 + 128, h * 256 + 256) for h in range(12)])
            kvcols = np.concatenate([kn, vv])
            for j in range(6):
                add(_colblock(wkv, kvcols[j * 512:(j + 1) * 512]), seq)
        wo = inp["w_o"][l]
        for j in range(4):
            add(_colblock(wo, np.arange(j * 512, (j + 1) * 512)), seq)
        wu = inp["w_up"][l]
        for t in range(22):
            cols = np.concatenate([np.arange(2 * t * 128, (2 * t + 2) * 128), DFF + np.arange(2 * t * 128, (2 * t + 2) * 128)])
            add(_colblock(wu, cols), seq)
        wd = inp["w_down"][l]
        for oc in range(16):
            add(_downblock(wd, oc), seq)
    return np.stack(tiles), nels, pro_seq, layer_seq


def weight_tile_meta():
    nels, layer_seq, pro_seq = [], [[] for _ in range(4)], []

    def add(nel, lst):
        lst.append(len(nels))
        nels.append(nel)

    for l in range(4):
        add(WEL, pro_seq)
        add(WEL, pro_seq)
    for l in range(4):
        seq = layer_seq[l]
        if l != 2:
            for j in range(10):
                add(WEL, seq)
        else:
            for j in range(4):
                add(WEL, seq)
            for j in range(5):
                add(6 * 512, seq)
            for j in range(6):
                add(4 * 512, seq)
        for j in range(4):
            add(WEL, seq)
        for t in range(22):
            add(WEL, seq)
        for oc in range(16):
            add(44 * 128, seq)
    return nels, pro_seq, layer_seq


def fm(v):
    return np.ascontiguousarray(v.reshape(-1, 128).T)


def build_shared(inp):
    vecs = np.zeros((128, NV), np.float32)
    for k, name in enumerate(["ln1_g", "ln1_b", "ln2_g", "ln2_b"]):
        for l in range(4):
            o = V_LN + (k * 4 + l) * 16
            vecs[:, o:o + 16] = fm(inp[name][l])
    for l in range(4):
        for j in range(3):
            o = V_CW + (l * 3 + j) * 88
            vecs[:, o:o + 88] = fm(inp["conv_ffn_w"][l, j])
        o = V_CB + l * 88
        vecs[:, o:o + 88] = fm(inp["conv_ffn_b"][l])
    vecs[:, V_DG:V_DG + 2] = fm(inp["diff_norm_g"][0])
    vecs[:, V_QG:V_QG + 6] = fm(inp["mla_q_norm_g"][0])
    vecs[:, V_KG:V_KG + 4] = fm(inp["mla_kv_norm_g"][0])
    for k, name in enumerate(["diff_lambda_q1", "diff_lambda_k1", "diff_lambda_q2", "diff_lambda_k2"]):
        vecs[:, V_LAM + k] = inp[name][0]
    c = np.zeros((128, NC), np.float32)
    c[:, C_ONES:C_ONES + 128] = 1.0
    for p in range(32):
        c[p, C_PERMA + (p + 16) % 32] = 1.0
    for p in range(128):
        g = (p // 64) * 64
        c[p, C_PERMC + g + ((p - g) + 32) % 64] = 1.0
    j = np.arange(128)[:, None]
    k = np.arange(128)[None, :]
    c[:, C_U:C_U + 128] = (j > k)
    c[:, C_MCC:C_MCC + 128] = (j // 64 <= k // 64)
    c[:, C_MSC:C_MSC + 128] = (j < k)
    for p in range(64):
        c[p, C_DUP + p] = 1.0
        c[p, C_DUP + 64 + p] = 1.0
    pos = np.concatenate([np.arange(TP), PAST + np.arange(NS)]).astype(np.float32)
    tabA = np.zeros((32, 2, TOT), np.float32)
    invA = (500000.0 ** (-np.arange(16, dtype=np.float32) / 16)).astype(np.float32)
    angA = pos[None, :] * invA[:, None]
    tabA[:16, 0] = np.cos(angA); tabA[16:, 0] = np.cos(angA)
    tabA[:16, 1] = -np.sin(angA); tabA[16:, 1] = np.sin(angA)
    tabC = np.zeros((128, 2, TOT), np.float32)
    invC = (10000.0 ** (-np.arange(32, dtype=np.float32) / 32)).astype(np.float32)
    angC = pos[None, :] * invC[:, None]
    for g in range(2):
        tabC[g * 64:g * 64 + 32, 0] = np.cos(angC); tabC[g * 64 + 32:g * 64 + 64, 0] = np.cos(angC)
        tabC[g * 64:g * 64 + 32, 1] = -np.sin(angC); tabC[g * 64 + 32:g * 64 + 64, 1] = np.sin(angC)
    rb = inp["band_rel_bias"][0]
    ki = np.arange(128)[:, None]
    qi = np.arange(128)[None, :]
    bias = np.zeros((128, 12, 5, 128), np.float32)
    for r in range(5):
        rel = 128 * (4 - r) + qi - ki
        idx = np.clip(rel, -128, 128) + 128
        dch = (8 - 2 * r) + qi // 64 - ki // 64
        ok = (dch >= 0) & (dch <= 8)
        for h in range(12):
            b = rb[h][idx]
            bias[:, h, r, :] = np.where(ok, b, np.float32(MASKV))
    return dict(vecs=vecs, consts=c, tabA=tabA, tabC=tabC, bbias=bias)


def build_core_inputs(inp, c):
    b, s = c % 2, c
    d = {}
    xT = np.empty((D, TOT), np.float32)
    xT[:, :TP] = inp["x_prompt"][b].T
    xT[:, TP:] = inp["x_sample"][s].T
    d["xT"] = xT
    d["memT"] = np.ascontiguousarray(inp["mem_prompt"][b].T)
    d["skt_a"] = np.ascontiguousarray(inp["cache_a_k"][0, s].reshape(PAST, 12, 128).transpose(1, 2, 0))
    d["sv_a"] = np.ascontiguousarray(inp["cache_a_v"][0, s].reshape(PAST, 1536))
    d["skt_b"] = np.ascontiguousarray(inp["cache_b_k"][0, s].transpose(1, 2, 0))
    d["sv_b"] = np.ascontiguousarray(inp["cache_b_v"][0, s].reshape(512, 1536))
    d["slat"] = np.ascontiguousarray(inp["cache_c_latent"][0, s].T.reshape(4, 128, PAST))
    kr = inp["cache_c_krope"][0, s].T
    d["skr"] = np.ascontiguousarray(np.concatenate([kr, kr], 0))
    d["skt_d"] = np.ascontiguousarray(inp["cache_d_k"][0, s].transpose(1, 2, 0))
    d["sv_d"] = np.ascontiguousarray(inp["cache_d_v"][0, s].reshape(PAST, 1536))
    d["smk"] = np.ascontiguousarray(inp["cache_mem_k"][:, s].transpose(0, 3, 2, 1))
    d["smv"] = np.ascontiguousarray(inp["cache_mem_v"][:, s].reshape(4, 2, 128, 512).transpose(0, 2, 1, 3))
    st = inp["state_ffn_conv"][:, s]
    d["cst_s"] = np.ascontiguousarray(st.reshape(4, 2, 88, 128).transpose(3, 0, 2, 1))
    return d


class Prog:
    def __init__(self, nc, es):
        self.nc, self.es = nc, es
        self.eng = {"pe": nc.tensor, "act": nc.scalar, "dve": nc.vector, "pool": nc.gpsimd, "sp": nc.sync}
        self.csem = {e: es.enter_context(nc.semaphore("c_" + e)) for e in ["pe", "act", "dve", "pool"]}
        self.ccnt = {e: 0 for e in self.csem}
        self.dsem = [es.enter_context(nc.semaphore("d%d" % i)) for i in range(NDS)]
        self.dcnt = [0] * NDS
        self.dnext = 0
        self.seen = {e: {} for e in self.eng}
        self.lw, self.rd = {}, {}
        self.ninst = 0

    def _sem(self, k):
        return self.csem[k[1]] if k[0] == "c" else self.dsem[k[1]]

    def _wait(self, e, deps):
        need = {}
        for k, v in deps:
            if v > need.get(k, 0):
                need[k] = v
        for k, v in need.items():
            if k == ("c", e) and (e == "pe" or not SAME_ENG_SYNC):
                continue
            if self.seen[e].get(k, 0) >= v:
                continue
            self.eng[e].wait_ge(self._sem(k), v)
            self.seen[e][k] = v
            self.ninst += 1

    def _deps(self, reads, writes):
        d = []
        for r in reads:
            if r in self.lw:
                d.append(self.lw[r])
        for w in writes:
            if w in self.lw:
                d.append(self.lw[w])
            d.extend(self.rd.get(w, {}).items())
        return d

    def _record(self, iid, reads, writes):
        for r in reads:
            rr = self.rd.setdefault(r, {})
            if iid[1] > rr.get(iid[0], 0):
                rr[iid[0]] = iid[1]
        for w in writes:
            self.lw[w] = iid
            self.rd[w] = {}

    def op(self, e, fn, reads=(), writes=()):
        self._wait(e, self._deps(reads, writes))
        self.ccnt[e] += 1
        iid = (("c", e), self.ccnt[e])
        fn(self.eng[e]).then_inc(self.csem[e], 1)
        self.ninst += 1
        self._record(iid, reads, writes)

    def dma(self, q, out, in_, reads=(), writes=()):
        k = self.dnext
        self.dnext = (k + 1) % NDS
        deps = self._deps(reads, writes)
        if self.dcnt[k] > 0:
            deps.append((("d", k), self.dcnt[k]))
        self._wait(q, deps)
        self.dcnt[k] += 16
        iid = (("d", k), self.dcnt[k])
        self.eng[q].dma_start(out=out, in_=in_).then_inc(self.dsem[k], 16)
        self.ninst += 1
        self._record(iid, reads, writes)

    def barrier(self):
        deps = [(("c", e), self.ccnt[e]) for e in self.ccnt if self.ccnt[e] > 0]
        skip = set()
        for key in (("w", 0), ("w", 1)):
            if key in self.lw and self.lw[key][0][0] == "d":
                skip.add(self.lw[key])
        for k in range(NDS):
            if self.dcnt[k] > 0:
                iid = (("d", k), self.dcnt[k])
                if iid in skip:
                    if self.dcnt[k] > 16:
                        deps.append((("d", k), self.dcnt[k] - 16))
                else:
                    deps.append(iid)
        for e in ["pe", "act", "dve", "pool"]:
            self._wait(e, [d for d in deps if d[0] != ("c", e)])

    def finish(self):
        deps = [(("c", e), self.ccnt[e]) for e in self.ccnt if self.ccnt[e] > 0]
        deps += [(("d", k), self.dcnt[k]) for k in range(NDS) if self.dcnt[k] > 0]
        self._wait("sp", deps)

    def mm(self, out, lhsT, rhs, start, stop, reads, writes):
        self.op("pe", lambda e: e.matmul(out, lhsT, rhs, start=start, stop=stop), reads, writes)

    def act(self, out, in_, func, reads, writes, bias=None, scale=None):
        kw = {}
        if bias is not None:
            kw["bias"] = bias
        if scale is not None:
            kw["scale"] = scale
        self.op("act", lambda e: e.activation(out, in_, func, **kw), reads, writes)

    def tt(self, out, in0, in1, op, reads, writes, e="dve"):
        self.op(e, lambda g: g.tensor_tensor(out, in0, in1, op), reads, writes)

    def ts(self, out, in0, s1, s2, op0, op1, reads, writes, e="dve"):
        if op1 is None:
            self.op(e, lambda g: g.tensor_scalar(out, in0, s1, None, op0), reads, writes)
        else:
            self.op(e, lambda g: g.tensor_scalar(out, in0, s1, s2, op0, op1), reads, writes)

    def stt(self, out, in0, sc, in1, op0, op1, reads, writes):
        self.op("dve", lambda g: g.scalar_tensor_tensor(out, in0, sc, in1, op0, op1), reads, writes)

    def copy(self, out, in_, reads, writes, e="dve"):
        if e == "act":
            self.act(out, in_, AF.Copy, reads, writes)
        else:
            self.op(e, lambda g: g.tensor_copy(out, in_), reads, writes)


def R(name, idx):
    return [(name, i) for i in idx]


class Kern:
    def __init__(self, cfg):
        self.cfg = cfg
        self.NL = cfg.get("NL", 4)
        self.tiles = cfg.get("tiles", list(range(9)))
        self.tile_order = [i for i in self.tiles if i == NPT] + [i for i in self.tiles if i != NPT]

    def build(self):
        nc = bass.Bass("TRN2", target_bir_lowering=False)
        self.nc = nc
        nels, pro_seq, layer_seq = weight_tile_meta()
        self.nels, self.pro_seq, self.layer_seq = nels, pro_seq, layer_seq
        NWT = len(nels)
        dt = nc.dram_tensor

        def inp(name, shape):
            return dt(name, list(shape), F32, kind="ExternalInput")

        def outp(name, shape):
            return dt(name, list(shape), F32, kind="ExternalOutput")

        I = {}
        self.wgroups = [pro_seq] + layer_seq
        self.wloc = {}
        for gi, grp in enumerate(self.wgroups):
            for li, ti in enumerate(grp):
                self.wloc[ti] = (gi, li)
        for gi, grp in enumerate(self.wgroups):
            ng = len(grp) if gi <= self.NL else 1
            I["wt%d" % gi] = inp("wt%d" % gi, [ng, 128, WEL])
        I["vecs"] = inp("vecs", [128, NV])
        I["consts"] = inp("consts", [128, NC])
        I["tabA"] = inp("tabA", [32, 2, TOT])
        I["tabC"] = inp("tabC", [128, 2, TOT])
        I["bbias"] = inp("bbias", [128, 12, 5, 128])
        I["xT"] = inp("xT", [D, TOT])
        I["memT"] = inp("memT", [D, 256])
        I["skt_a"] = inp("skt_a", [12, 128, PAST]); I["sv_a"] = inp("sv_a", [PAST, 1536])
        I["skt_b"] = inp("skt_b", [12, 128, 512]); I["sv_b"] = inp("sv_b", [512, 1536])
        I["slat"] = inp("slat", [4, 128, PAST]); I["skr"] = inp("skr", [128, PAST])
        I["skt_d"] = inp("skt_d", [12, 128, PAST]); I["sv_d"] = inp("sv_d", [PAST, 1536])
        I["smk"] = inp("smk", [4, 128, 4, 256]); I["smv"] = inp("smv", [4, 128, 2, 512])
        I["cst_s"] = inp("cst_s", [128, 4, 88, 2])
        self.I = I
        O = {}
        O["yT"] = outp("yT", [D, TOT])
        for l in (0, 1, 3):
            O["kT%d" % l] = outp("kT%d" % l, [1536, TOT])
            O["v%d" % l] = outp("v%d" % l, [TOT, 1536])
        O["latT"] = outp("latT", [512, TOT])
        O["krT"] = outp("krT", [64, TOT])
        O["bkT_s"] = outp("bkT_s", [1536, 512])
        O["bv_s"] = outp("bv_s", [512, 1536])
        O["memkT"] = outp("memkT", [4, 512, 256])
        O["memv"] = outp("memv", [4, 256, 512])
        O["conv_p"] = outp("conv_p", [128, 4, 88, 2])
        O["conv_s"] = outp("conv_s", [128, 4, 88, 2])
        self.O = O
        S = {}
        for gi, grp in enumerate(self.wgroups):
            ng = len(grp) if gi <= self.NL else 1
            S["wb%d" % gi] = dt("wb%d" % gi, [ng, 128, WEL], BF16)
        for l in range(4):
            S["KT%d" % l] = dt("KTs%d" % l, [12, 128, TP], BF16)
            S["V%d" % l] = dt("Vs%d" % l, [TP, 1536], BF16)
        S["KR"] = dt("KRs", [128, TP], BF16)
        S["sKT0"] = dt("sKT0", [12, 128, PAST], BF16); S["sV0"] = dt("sV0", [PAST, 1536], BF16)
        S["sKT1"] = dt("sKT1", [12, 128, 512], BF16); S["sV1"] = dt("sV1", [512, 1536], BF16)
        S["sKT2"] = dt("sKT2", [12, 128, PAST], BF16); S["sV2"] = dt("sV2", [PAST, 1536], BF16)
        S["sKT3"] = dt("sKT3", [12, 128, PAST], BF16); S["sV3"] = dt("sV3", [PAST, 1536], BF16)
        S["sLat"] = dt("sLat", [4, 128, PAST], BF16); S["sKR"] = dt("sKR", [128, PAST], BF16)
        S["mK"] = dt("mK", [2, 4, 128, 4, 256], BF16); S["mV"] = dt("mV", [2, 4, 128, 2, 512], BF16)
        self.S = S

        with contextlib.ExitStack() as es:
            P = Prog(nc, es)
            self.P = P
            sb = lambda name, shape, dty: es.enter_context(nc.sbuf_tensor("s_" + name, list(shape), dty))
            self.xf = sb("xf", [128, 16, NT], F32)
            self.xb = sb("xb", [128, 16, NT], BF16)
            self.wsl = [sb("w%d" % i, [128, WEL], BF16) for i in range(2)]
            self.reg = sb("reg", [128, 44 * NT], BF16)
            self.oT = sb("oT", [128, 16, NT], BF16)
            self.ks = [sb("ks%d" % i, [128, 2, NT], BF16) for i in range(2)]
            self.vs = [sb("vs%d" % i, [128, 4, 256], BF16) for i in range(2)]
            self.pt = [sb("pt%d" % i, [128, NT], BF16) for i in range(3)]
            self.st = [sb("st%d" % i, [128, NT], F32) for i in range(3)]
            self.ue = [sb("ue%d" % i, [128, NT + 4], F32) for i in range(2)]
            self.hb = [sb("hb%d" % i, [128, NT], F32) for i in range(2)]
            self.tm = [sb("tm%d" % i, [128, NT], F32) for i in range(4)]
            self.dtm = sb("dtm", [128, 4 * NT], F32)
            self.dtmv = lambda c: self.dtm[:, c * NT:(c + 1) * NT]
            self.hcv = [self.dtm[:, c * NT:(c + 1) * NT] for c in range(4)]
            dtb = self.dtm[:, :].bitcast(BF16)
            self.qrb = lambda jj: dtb[:, jj * NT:(jj + 1) * NT]
            self.vecs = sb("vecs", [128, NV], F32)
            self.cf = sb("cf", [128, NC], F32)
            self.cb = sb("cbf", [128, NC], BF16)
            self.cst = sb("cst", [128, 4, 88, 2], F32)
            self.mk = sb("mk", [128, 4, 256], BF16)
            self.mv = sb("mv", [128, 2, 512], BF16)
            self.tab = sb("tab", [128, 2, NT], F32)
            self.ebf = sb("ebf", [128, 5, 128], F32)
            self.sm = sb("sm", [128, 16], F32)
            self.ps = es.enter_context(nc.psum_tensor("ps", [128, 8, NT], F32))
            self.bank_rr = 0
            self.rr = {}
            reg = self.reg
            self.qT = lambda j: reg[:, j * NT:(j + 1) * NT]
            self.kT = lambda j: reg[:, (12 + j) * NT:(13 + j) * NT]
            self.vv = lambda tb: reg[:, 24 * NT + tb * 1536: 24 * NT + (tb + 1) * 1536]
            self.qm = lambda j: reg[:, 36 * NT + j * NT: 36 * NT + (j + 1) * NT]
            self.hT = lambda j: reg[:, j * NT:(j + 1) * NT]

            self.prologue()
            for i in self.tile_order:
                self.do_tile(i)
            P.dma("pool", self.O["conv_p"][:, :, :, :], self.cst[:], reads=[("cst",)], writes=[("o_convp",)])
            P.finish()
        self.ninst = P.ninst
        return nc

    def rot(self, name, n):
        k = self.rr.get(name, 0)
        self.rr[name] = (k + 1) % n
        return k

    def bank(self, lo=0, hi=8):
        k = lo + self.rot(("bank", lo, hi), hi - lo)
        return self.ps[:, k, :], ("ps", k)

    def wnext(self):
        ws = self.wstate
        seq = ws["seq"]
        while ws["issued"] < min(len(seq), ws["pos"] + 2):
            k = ws["issued"]
            self.cast_upto(k + 6)
            ti = seq[k]
            nel = self.nels[ti]
            gi, li = self.wloc[ti]
            self.P.dma("sp", self.wsl[k % 2][:, :nel], self.S["wb%d" % gi][li][:, :nel], reads=[("wb", ti)], writes=[("w", k % 2)])
            ws["issued"] += 1
        k = ws["pos"]
        ws["pos"] += 1
        return self.wsl[k % 2], ("w", k % 2)

    def cast_upto(self, k):
        ws = self.wstate
        seq = ws["seq"]
        while ws["cpos"] < min(len(seq), k + 1):
            ti = seq[ws["cpos"]]
            ws["cpos"] += 1
            if ti in ws["casted"]:
                continue
            ws["casted"].add(ti)
            nel = self.nels[ti]
            gi, li = self.wloc[ti]
            self.P.dma("pool", self.S["wb%d" % gi][li][:, :nel], self.I["wt%d" % gi][li][:, :nel], writes=[("wb", ti)])

    def prologue(self):
        P, I, S = self.P, self.I, self.S
        seq = list(self.pro_seq)
        for i in self.tile_order:
            for l in range(self.NL):
                seq += self.layer_seq[l]
        self.wstate = dict(seq=seq, pos=0, issued=0, cpos=0, casted=set())
        used = sorted(set(seq))
        P.dma("sp", self.vecs[:], I["vecs"][:, :], writes=[("vecs",)])
        P.dma("sp", self.cf[:], I["consts"][:, :], writes=[("cf",)])
        for a, b_, key in [("skt_a", "sKT0", "sKT0"), ("sv_a", "sV0", "sV0"), ("skt_b", "sKT1", "sKT1"), ("sv_b", "sV1", "sV1"),
                           ("slat", "sLat", "sLat"), ("skr", "sKR", "sKR"), ("skt_d", "sKT3", "sKT3"), ("sv_d", "sV3", "sV3")]:
            src, dst = I[a].ap(), S[b_].ap()
            P.dma("pool", dst, src, writes=[(key,)])
        for l in range(4):
            P.dma("pool", S["mK"][1, l], I["smk"][l], writes=[("mK", 1, l)])
            P.dma("pool", S["mV"][1, l], I["smv"][l], writes=[("mV", 1, l)])
        P.copy(self.cb[:], self.cf[:], reads=[("cf",)], writes=[("cb",)])
        P.op("dve", lambda g: g.memset(self.cst[:], 0.0), writes=[("cst",)])
        P.op("dve", lambda g: g.memset(self.sm[:], 0.0), writes=[("sm",)])
        P.op("dve", lambda g: g.memset(self.sm[:, 0:1], EPS), reads=[], writes=[("sm",)])
        li = lam_init(0)
        v = self.vecs
        P.tt(self.sm[:, 4:5], v[:, V_LAM:V_LAM + 1], v[:, V_LAM + 1:V_LAM + 2], ALU.mult, [("vecs",), ("sm",)], [("sm",)])
        P.tt(self.sm[:, 5:6], v[:, V_LAM + 2:V_LAM + 3], v[:, V_LAM + 3:V_LAM + 4], ALU.mult, [("vecs",), ("sm",)], [("sm",)])
        pb, pk = self.bank()
        P.mm(pb[:, 0:2], self.cf[:, C_ONES:C_ONES + 128], self.sm[:, 4:6], True, True, [("cf",), ("sm",)], [pk])
        P.act(self.sm[:, 6:8], pb[:, 0:2], AF.Exp, [pk, ("sm",)], [("sm",)])
        P.tt(self.sm[:, 8:9], self.sm[:, 7:8], self.sm[:, 6:7], ALU.subtract, [("sm",)], [("sm",)])
        P.ts(self.sm[:, 1:2], self.sm[:, 8:9], -li, None, ALU.add, None, [("sm",)], [("sm",)])
        P.ts(self.sm[:, 2:4], v[:, V_DG:V_DG + 2], 1.0 - li, None, ALU.mult, None, [("vecs",), ("sm",)], [("sm",)])
        P.dma("sp", self.xf[:, :, 0:256], I["memT"].ap().rearrange("(kc p) t -> p kc t", p=128), writes=R("xf", range(16)))
        P.copy(self.xb[:, :, 0:256], self.xf[:, :, 0:256], R("xf", range(16)), R("xb", range(16)))
        for l in range(4):
            W, wk = self.wnext()
            W3 = W[:, :].rearrange("p (k c) -> p k c", c=512)
            for j in range(4):
                pb, pk = self.bank()
                for kc in range(16):
                    P.mm(pb[:, 0:256], W3[:, kc, j * 128:(j + 1) * 128], self.xb[:, kc, 0:256], kc == 0, kc == 15, [wk] + R("xb", [kc]), [pk])
                s = self.rot("st", 3)
                P.copy(self.st[s][:, 0:256], pb[:, 0:256], [pk], [("st", s)], e="act")
                P.dma("pool", self.O["memkT"][l, j * 128:(j + 1) * 128, :], self.st[s][:, 0:256], reads=[("st", s)], writes=[("o_mk", l, j)])
                P.copy(self.mk[:, j, :], self.st[s][:, 0:256], [("st", s)], [("mk",)])
            P.dma("pool", S["mK"][0, l], self.mk[:], reads=[("mk",)], writes=[("mK", 0, l)])
            W, wk = self.wnext()
            W3 = W[:, :].rearrange("p (k c) -> p k c", c=512)
            for tb in range(2):
                pb, pk = self.bank()
                for kc in range(16):
                    P.mm(pb[:, :], self.xb[:, kc, tb * 128:(tb + 1) * 128], W3[:, kc, :], kc == 0, kc == 15, [wk] + R("xb", [kc]), [pk])
                s = self.rot("st", 3)
                P.copy(self.st[s][:, :], pb[:, :], [pk], [("st", s)], e="act")
                P.dma("pool", self.O["memv"][l, tb * 128:(tb + 1) * 128, :], self.st[s][:, :], reads=[("st", s)], writes=[("o_mv", l, tb)])
                P.copy(self.mv[:, tb, :], self.st[s][:, :], [("st", s)], [("mv",)])
            P.dma("pool", S["mV"][0, l], self.mv[:], reads=[("mv",)], writes=[("mV", 0, l)])

    def do_tile(self, i):
        P, I = self.P, self.I
        self.i = i
        self.samp = (i == NPT)
        self.n = NS if self.samp else NT
        self.tok0 = i * NT
        n, t0 = self.n, self.tok0
        if self.samp:
            P.dma("sp", self.cst[:], I["cst_s"][:, :, :, :], reads=[], writes=[("cst",)])
        P.dma("sp", self.xf[:, :, :n], I["xT"].ap().rearrange("(kc p) t -> p kc t", p=128)[:, :, t0:t0 + n], writes=R("xf", range(16)))
        P.copy(self.xb[:, :, :n], self.xf[:, :, :n], R("xf", range(16)), R("xb", range(16)))
        for l in range(self.NL):
            self.layer(l)
        P.dma("pool", self.O["yT"].ap().rearrange("(kc p) t -> p kc t", p=128)[:, :, t0:t0 + n], self.xf[:, :, :n], reads=R("xf", range(16)), writes=[("o_y", i)])
        if self.samp:
            P.dma("pool", self.O["conv_s"][:, :, :, :], self.cst[:], reads=[("cst",)], writes=[("o_convs",)])
            P.op("dve", lambda g: g.memset(self.cst[:], 0.0), [], [("cst",)])
            P.barrier()

    def proj_fm(self, W3, wk, cj, nk, src, srckey, n):
        P = self.P
        pb, pk = self.bank(0, 4)
        for kc in range(nk):
            P.mm(pb[:, :n], W3[:, kc, cj * 128:(cj + 1) * 128], src(kc), kc == 0, kc == nk - 1, [wk, (srckey, kc)], [pk])
        return pb, pk

    def rope(self, s, npart, perm_off, tabkey):
        P, n = self.P, self.n
        stt_ = self.st[s]
        pb, pk = self.bank(4, 6)
        P.mm(pb[0:npart, :n], self.cf[0:npart, perm_off:perm_off + npart], stt_[0:npart, :n], True, True, [("cf",), ("st", s)], [pk])
        t = self.rot("tm", 4)
        P.tt(self.tm[t][0:npart, :n], pb[0:npart, :n], self.tab[0:npart, 1, :n], ALU.mult, [pk, tabkey], [("tm", t)])
        P.tt(stt_[0:npart, :n], stt_[0:npart, :n], self.tab[0:npart, 0, :n], ALU.mult, [("st", s), tabkey], [("st", s)])
        P.tt(stt_[0:npart, :n], stt_[0:npart, :n], self.tm[t][0:npart, :n], ALU.add, [("st", s), ("tm", t)], [("st", s)])

    def kv_out_names(self, l):
        return self.O["kT%d" % l], self.O["v%d" % l]

    def layer(self, l):
        P, n, t0, i = self.P, self.n, self.tok0, self.i
        m = l % 4
        g = 1 if self.samp else 0
        P.dma("sp", self.mk[:], self.S["mK"][g, l], reads=[("mK", g, l)], writes=[("mk",)])
        P.dma("sp", self.mv[:], self.S["mV"][g, l], reads=[("mV", g, l)], writes=[("mv",)])
        if m == 2:
            self.mla_proj(l)
        else:
            self.qkv_proj(l)
        if m == 0:
            self.attn_diff(l)
        elif m == 1:
            self.attn_band(l)
        elif m == 2:
            self.attn_mla(l)
        else:
            self.attn_stick(l)
        self.attn_mem(l)
        self.out_proj_ffn(l)

    def qkv_proj(self, l):
        P, n, t0, i = self.P, self.n, self.tok0, self.i
        xsrc = lambda kc: self.xb[:, kc, :n]
        OK, OV = self.kv_out_names(l)
        if l == 0:
            P.dma("sp", self.tab[0:32, :, :n], self.I["tabA"][:, :, t0:t0 + n], writes=[("tab",)])
        for part in range(2):
            for jt in range(3):
                W, wk = self.wnext()
                W3 = W[:, :].rearrange("p (k c) -> p k c", c=512)
                for cj in range(4):
                    j = jt * 4 + cj
                    pb, pk = self.proj_fm(W3, wk, cj, 16, xsrc, "xb", n)
                    dst = self.qT(j) if part == 0 else self.kT(j)
                    dkey = ("qT", j) if part == 0 else ("kT", j)
                    if l == 0 or part == 1:
                        s = self.rot("st", 3)
                        P.copy(self.st[s][:, :n], pb[:, :n], [pk], [("st", s)], e="act")
                        if l == 0:
                            self.rope(s, 32, C_PERMA, ("tab",))
                        if part == 1:
                            P.dma("pool", OK[j * 128:(j + 1) * 128, t0:t0 + n], self.st[s][:, :n], reads=[("st", s)], writes=[("o_k", l, j, i)])
                            if l == 1 and self.samp:
                                P.dma("pool", self.O["bkT_s"][j * 128:(j + 1) * 128, 448:512], self.st[s][:, :n], reads=[("st", s)], writes=[("o_bks", j)])
                        P.copy(dst[:, :n], self.st[s][:, :n], [("st", s)], [dkey], e="pool")
                    else:
                        P.copy(dst[:, :n], pb[:, :n], [pk], [dkey], e="act")
        if not self.samp:
            P.dma("pool", self.S["KT%d" % l].ap().rearrange("j p t -> p j t")[:, :, t0:t0 + n],
                  self.reg[:, 12 * NT:24 * NT].rearrange("p (j t) -> p j t", t=NT), reads=R("kT", range(12)), writes=[("KTs", l, i)])
        ntb = (n + 127) // 128
        for jt in range(3):
            W, wk = self.wnext()
            W3 = W[:, :].rearrange("p (k c) -> p k c", c=512)
            for tb in range(ntb):
                np_ = min(128, n - tb * 128)
                pb, pk = self.bank(0, 4)
                for kc in range(16):
                    P.mm(pb[:np_, :], self.xb[:, kc, tb * 128:tb * 128 + np_], W3[:, kc, :], kc == 0, kc == 15, [wk, ("xb", kc)], [pk])
                s = self.rot("st", 3)
                P.copy(self.st[s][:np_, :], pb[:np_, :], [pk], [("st", s)], e="act")
                r0 = t0 + tb * 128
                P.dma("pool", OV[r0:r0 + np_, jt * 512:(jt + 1) * 512], self.st[s][:np_, :], reads=[("st", s)], writes=[("o_v", l, jt, tb, i)])
                if l == 1 and self.samp:
                    P.dma("pool", self.O["bv_s"][448:512, jt * 512:(jt + 1) * 512], self.st[s][:np_, :], reads=[("st", s)], writes=[("o_bvs", jt)])
                P.copy(self.vv(tb)[:np_, jt * 512:(jt + 1) * 512], self.st[s][:np_, :], [("st", s)], [("v", tb)], e="pool")
        if not self.samp:
            P.dma("pool", self.S["V%d" % l].ap()[t0:t0 + n, :].rearrange("(tb p) c -> p tb c", p=128),
                  self.reg[:, 24 * NT:24 * NT + 4 * 1536].rearrange("p (tb c) -> p tb c", c=1536), reads=R("v", range(4)), writes=[("Vs", l, i)])
        if l == 1 and self.samp:
            P.dma("pool", self.O["bkT_s"][:, 0:448], self.I["skt_b"].ap().rearrange("j p t -> (j p) t")[:, 64:512], writes=[("o_bks2",)])
            P.dma("pool", self.O["bv_s"][0:448, :], self.I["sv_b"][64:512, :], writes=[("o_bvs2",)])
        W, wk = self.wnext()
        W3 = W[:, :].rearrange("p (k c) -> p k c", c=512)
        for cj in range(4):
            pb, pk = self.proj_fm(W3, wk, cj, 16, xsrc, "xb", n)
            P.copy(self.qm(cj)[:, :n], pb[:, :n], [pk], [("qm", cj)], e="act")

    def kv_groups(self, l):
        S = self.S
        if self.samp:
            ng = 1 if l == 1 else 2
            return [(S["sKT%d" % l], S["sV%d" % l], S["sKR"], g, [("sKT%d" % l,), ("sV%d" % l,), ("sKR",)]) for g in range(ng)]
        return [(S["KT%d" % l], S["V%d" % l], S["KR"], g, [("KTs", l, g), ("Vs", l, g), ("KRs", g)]) for g in range(self.i)]

    def load_group(self, grp, j, vc0, dv, with_kr=False):
        P = self.P
        KT, V, KR, g, keys = grp
        s = self.rot("kvs", 2)
        P.dma("sp", self.ks[s][:, 0, :], KT[j, :, g * 512:(g + 1) * 512], reads=[keys[0]], writes=[("ks", s)])
        if with_kr:
            P.dma("sp", self.ks[s][:, 1, :], KR[:, g * 512:(g + 1) * 512], reads=[keys[2]], writes=[("ks", s)])
        P.dma("sp", self.vs[s][:, :, 0:dv], V.ap()[g * 512:(g + 1) * 512, vc0:vc0 + dv].rearrange("(kb p) c -> p kb c", p=128),
              reads=[keys[1]], writes=[("vs", s)])
        return s

    def accset(self):
        k = self.rot("accset", 2)
        base = 2 + 3 * k
        return [(self.ps[:, base + c, :], ("ps", base + c)) for c in range(3)]

    def pipe_push(self, s1, s2):
        s1()
        if self.pipe_pending is not None:
            self.pipe_pending()
        self.pipe_pending = s2

    def pipe_flush(self):
        if getattr(self, "pipe_pending", None) is not None:
            self.pipe_pending()
        self.pipe_pending = None

    def softmax_heads(self, l, heads, scale, masked):
        P, n, i = self.P, self.n, self.i
        groups = self.kv_groups(l)
        items = [(hd, grp) for hd in heads for grp in groups]
        slots = {}

        def issue(k):
            if k < len(items) and k not in slots:
                hd, grp = items[k]
                slots[k] = self.load_group(grp, hd["kj"], hd["vc0"], hd["dv"], with_kr=hd.get("qr") is not None)

        k = 0
        ones_b = self.cb[:, C_ONES:C_ONES + 128]
        ntb = (n + 127) // 128
        self.pipe_pending = None
        for hd in heads:
            q, qk = hd["q"]
            ndv = hd["dv"] // 128
            acc = self.accset()
            oacc = acc[:ndv]
            sacc = acc[2]
            nblocks = 4 * len(groups) + ntb
            st = dict(bi=0)

            def block(klhs, kkeys, krlhs, vfn, vkey, nk, q0, diag, hd=hd, q=q, qk=qk, oacc=oacc, sacc=sacc, ndv=ndv, st=st, nblocks=nblocks):
                bi = st["bi"]
                st["bi"] += 1
                first, last = (bi == 0), (bi == nblocks - 1)
                hold = {}

                def s1():
                    sb_, sk = self.bank(0, 2)
                    if krlhs is None:
                        P.mm(sb_[:nk, q0:n], klhs, q[:, q0:n], True, True, kkeys + [qk], [sk])
                    else:
                        qr, qrk, hp = hd["qr"]
                        P.mm(sb_[:nk, q0:n], klhs, q[:, q0:n], True, False, kkeys + [qk], [sk])
                        P.mm(sb_[:nk, q0:n], krlhs, qr[hp:hp + 64, q0:n], False, True, kkeys + [qrk], [sk])
                    p = self.rot("pt", 3)
                    pt = self.pt[p]
                    P.act(pt[:nk, q0:n], sb_[:nk, q0:n], AF.Exp, [sk], [("pt", p)], scale=scale)
                    if diag and masked:
                        w = min(128, n - q0)
                        P.tt(pt[:nk, q0:q0 + w], pt[:nk, q0:q0 + w], self.cb[:nk, C_MCC:C_MCC + w], ALU.mult, [("pt", p), ("cb",)], [("pt", p)])
                    hold["p"] = p

                def s2():
                    p = hold["p"]
                    pt = self.pt[p]
                    for c in range(ndv):
                        P.mm(oacc[c][0][:, q0:n], vfn(c), pt[:nk, q0:n], first, last, [vkey, ("pt", p)], [oacc[c][1]])
                    P.mm(sacc[0][:, q0:n], ones_b[:nk, :], pt[:nk, q0:n], first, last, [("cb",), ("pt", p)], [sacc[1]])
                    if last:
                        t = self.rot("tm", 4)
                        P.op("dve", lambda g_: g_.reciprocal(self.tm[t][:, :n], sacc[0][:, :n]), [sacc[1]], [("tm", t)])
                        hd["fin"](oacc, (self.tm[t], ("tm", t)))

                self.pipe_push(s1, s2)

            for grp in groups:
                issue(k)
                s = slots[k]
                k += 1
                for kb in range(4):
                    krl = None
                    if hd.get("qr") is not None:
                        hp = hd["qr"][2]
                        krl = self.ks[s][hp:hp + 64, 1, kb * 128:(kb + 1) * 128]
                    block(self.ks[s][:, 0, kb * 128:(kb + 1) * 128], [("ks", s)], krl,
                          lambda c, s=s, kb=kb: self.vs[s][:, kb, c * 128:(c + 1) * 128], ("vs", s), 128, 0, False)
                    if kb == 0:
                        issue(k)
            kown, kownkey = hd["kown"]
            for kb in range(ntb):
                nk = min(128, n - kb * 128)
                krl = None
                if hd.get("qr") is not None:
                    hp = hd["qr"][2]
                    krl = self.krT2[hp:hp + 64, kb * 128:kb * 128 + nk]
                block(kown[:, kb * 128:kb * 128 + nk], [kownkey] + ([("krT2",)] if krl is not None else []), krl,
                      lambda c, kb=kb, nk=nk, hd=hd: self.vv(kb)[:nk, hd["vc0"] + c * 128: hd["vc0"] + (c + 1) * 128], ("v", kb), nk, kb * 128, True)
        self.pipe_flush()

    def attn_diff(self, l):
        P, n = self.P, self.n
        heads = []
        for h in range(6):
            for m in range(2):
                j = 2 * h + m

                def fin(oacc, rec, h=h, m=m):
                    for c in range(2):
                        P.tt(self.dtmv(2 * m + c)[:, :n], oacc[c][0][:, :n], rec[0][:, :n], ALU.mult, [oacc[c][1], rec[1]], [("dtm", 2 * m + c)])
                    if m == 1:
                        for c in range(2):
                            P.stt(self.dtmv(c)[:, :n], self.dtmv(2 + c)[:, :n], self.sm[:, 1:2], self.dtmv(c)[:, :n], ALU.mult, ALU.add,
                                  [("dtm", 2 + c), ("dtm", c), ("sm",)], [("dtm", c)])
                        pb, pk = self.bank(0, 2)
                        for c in range(2):
                            t = self.rot("tm", 4)
                            P.act(self.tm[t][:, :n], self.dtmv(c)[:, :n], AF.Square, [("dtm", c)], [("tm", t)])
                            P.mm(pb[:, :n], self.cf[:, C_ONES:C_ONES + 128], self.tm[t][:, :n], c == 0, c == 1, [("cf",), ("tm", t)], [pk])
                        t = self.rot("tm", 4)
                        P.act(self.tm[t][:, :n], pb[:, :n], AF.Sqrt, [pk, ("sm",)], [("tm", t)], bias=self.sm[:, 0:1], scale=1.0 / 256)
                        P.op("dve", lambda g_: g_.reciprocal(self.tm[t][:, :n], self.tm[t][:, :n]), [("tm", t)], [("tm", t)])
                        for c in range(2):
                            P.tt(self.dtmv(c)[:, :n], self.dtmv(c)[:, :n], self.tm[t][:, :n], ALU.mult, [("dtm", c), ("tm", t)], [("dtm", c)])
                            P.ts(self.oT[:, 2 * h + c, :n], self.dtmv(c)[:, :n], self.sm[:, 2 + c:3 + c], None, ALU.mult, None,
                                 [("dtm", c), ("sm",)], [("oT", 2 * h + c)], e="pool")

                heads.append(dict(q=(self.qT(j), ("qT", j)), kj=j, kown=(self.kT(j), ("kT", j)), vc0=256 * h, dv=256, fin=fin))
        self.softmax_heads(l, heads, 128 ** -0.5, True)

    def attn_mem(self, l):
        P, n = self.P, self.n
        ones_b = self.cb[:, C_ONES:C_ONES + 128]
        self.pipe_pending = None
        for h in range(4):
            acc = self.accset()
            oacc, sacc = acc[0], acc[2]
            for kb in range(2):
                hold = {}

                def s1(h=h, kb=kb, hold=hold):
                    sb_, sk = self.bank(0, 2)
                    P.mm(sb_[:, :n], self.mk[:, h, kb * 128:(kb + 1) * 128], self.qm(h)[:, :n], True, True, [("mk",), ("qm", h)], [sk])
                    p = self.rot("pt", 3)
                    P.act(self.pt[p][:, :n], sb_[:, :n], AF.Exp, [sk], [("pt", p)], scale=128 ** -0.5)
                    hold["p"] = p

                def s2(h=h, kb=kb, hold=hold, oacc=oacc, sacc=sacc):
                    p = hold["p"]
                    P.mm(oacc[0][:, :n], self.mv[:, kb, h * 128:(h + 1) * 128], self.pt[p][:, :n], kb == 0, kb == 1, [("mv",), ("pt", p)], [oacc[1]])
                    P.mm(sacc[0][:, :n], ones_b, self.pt[p][:, :n], kb == 0, kb == 1, [("cb",), ("pt", p)], [sacc[1]])
                    if kb == 1:
                        t = self.rot("tm", 4)
                        P.op("dve", lambda g_: g_.reciprocal(self.tm[t][:, :n], sacc[0][:, :n]), [sacc[1]], [("tm", t)])
                        P.tt(self.oT[:, 12 + h, :n], oacc[0][:, :n], self.tm[t][:, :n], ALU.mult, [oacc[1], ("tm", t)], [("oT", 12 + h)])

                self.pipe_push(s1, s2)
        self.pipe_flush()

    def attn_band(self, l):
        P, n, i = self.P, self.n, self.i
        ones_b = self.cb[:, C_ONES:C_ONES + 128]
        has_prev = self.samp or i > 0
        npb = (n + 127) // 128
        self.pipe_pending = None
        for h in range(12):
            ebf = self.ebf
            ekey = ("ebf", 0)
            P.dma("sp", ebf[:], self.I["bbias"][:, h], writes=[ekey])
            P.act(ebf[:], ebf[:], AF.Exp, [ekey], [ekey])
            s = None
            if has_prev:
                s = self.rot("kvs", 2)
                if self.samp:
                    KT, V, g0, kk = self.S["sKT1"], self.S["sV1"], 0, [("sKT1",), ("sV1",)]
                else:
                    KT, V, g0, kk = self.S["KT1"], self.S["V1"], (i - 1) * 512, [("KTs", 1, i - 1), ("Vs", 1, i - 1)]
                P.dma("sp", self.ks[s][:, 0, :], KT[h, :, g0:g0 + 512], reads=[kk[0]], writes=[("ks", s)])
                P.dma("sp", self.vs[s][:, :, 0:128], V.ap()[g0:g0 + 512, h * 128:(h + 1) * 128].rearrange("(kb p) c -> p kb c", p=128),
                      reads=[kk[1]], writes=[("vs", s)])
            acc = self.accset()
            oacc, sacc = acc[0], acc[2]
            rs = [4, 3, 2, 1, 0]
            valid_r = [r for r in rs if has_prev or r == 4 or (npb - 1) - 4 + r >= 0]
            for ri, r in enumerate(valid_r):
                pbs = [pb_ for pb_ in range(npb) if (pb_ - 4 + r >= 0) or has_prev]
                hold = {}

                def s1(h=h, r=r, pbs=pbs, s=s, ebf=ebf, ekey=ekey, hold=hold):
                    pb0 = pbs[0]
                    sb_, sk = self.bank(0, 2)
                    nk = 128
                    for pb_ in pbs:
                        lb = pb_ - 4 + r
                        nq = min(128, n - pb_ * 128)
                        if lb < 0:
                            klhs, kkey, nk = self.ks[s][:, 0, (4 + lb) * 128:(5 + lb) * 128], ("ks", s), 128
                        else:
                            nk = min(128, n - lb * 128)
                            klhs, kkey = self.kT(h)[:, lb * 128:lb * 128 + nk], ("kT", h)
                        P.mm(sb_[:nk, pb_ * 128:pb_ * 128 + nq], klhs, self.qT(h)[:, pb_ * 128:pb_ * 128 + nq], True, True, [kkey, ("qT", h)], [sk])
                    c0, c1 = pb0 * 128, n
                    p = self.rot("pt", 3)
                    pt = self.pt[p]
                    P.act(pt[:nk, c0:c1], sb_[:nk, c0:c1], AF.Exp, [sk], [("pt", p)], scale=128 ** -0.5)
                    nqq = min(128, n)
                    ptv = pt[:nk, c0:c1].rearrange("p (a b) -> p a b", b=nqq)
                    ebv = ebf[:nk, r:r + 1, 0:nqq].broadcast_to([nk, len(pbs), nqq])
                    P.tt(ptv, ptv, ebv, ALU.mult, [("pt", p), ekey], [("pt", p)])
                    hold["p"], hold["nk"] = p, nk

                def s2(h=h, r=r, ri=ri, pbs=pbs, s=s, hold=hold, oacc=oacc, sacc=sacc, nvr=len(valid_r)):
                    p, nk_all = hold["p"], hold["nk"]
                    pt = self.pt[p]
                    last = (ri == nvr - 1)
                    c0, c1 = pbs[0] * 128, n
                    for pb_ in pbs:
                        lb = pb_ - 4 + r
                        nq = min(128, n - pb_ * 128)
                        if lb < 0:
                            vl, vkey, nk = self.vs[s][:, 4 + lb, 0:128], ("vs", s), 128
                        else:
                            nk = min(128, n - lb * 128)
                            vl, vkey = self.vv(lb)[:nk, h * 128:(h + 1) * 128], ("v", lb)
                        P.mm(oacc[0][:, pb_ * 128:pb_ * 128 + nq], vl, pt[:nk, pb_ * 128:pb_ * 128 + nq], ri == 0 and pb_ == pbs[0],
                             last and pb_ == pbs[-1], [vkey, ("pt", p)], [oacc[1]])
                    P.mm(sacc[0][:, c0:c1], ones_b[:nk_all, :], pt[:nk_all, c0:c1], ri == 0, last, [("cb",), ("pt", p)], [sacc[1]])
                    if last:
                        t = self.rot("tm", 4)
                        P.op("dve", lambda g_: g_.reciprocal(self.tm[t][:, :n], sacc[0][:, :n]), [sacc[1]], [("tm", t)])
                        P.tt(self.oT[:, h, :n], oacc[0][:, :n], self.tm[t][:, :n], ALU.mult, [oacc[1], ("tm", t)], [("oT", h)])

                self.pipe_push(s1, s2)
        self.pipe_flush()

    def attn_stick(self, l):
        P, n, i = self.P, self.n, self.i
        groups = self.kv_groups(l)
        rgroups = list(reversed(groups))
        items = [(h, grp) for h in range(12) for grp in rgroups]
        slots = {}

        def issue(k):
            if k < len(items) and k not in slots:
                h, grp = items[k]
                slots[k] = self.load_group(grp, h, h * 128, 128)

        k = 0
        scale = 128 ** -0.5
        onesf = self.cf[:, C_ONES:C_ONES + 128]
        ntb = (n + 127) // 128
        pend = []

        def push(s1, s2, s3):
            s1()
            if len(pend) >= 1:
                pend[-1][0]()
            if len(pend) >= 2:
                pend[-2][1]()
                pend.pop(0)
            pend.append([s2, s3])

        def flush():
            if len(pend) == 2:
                pend[1][0]()
                pend[0][1]()
                pend[1][1]()
            elif len(pend) == 1:
                pend[0][0]()
                pend[0][1]()
            del pend[:]

        for h in range(12):
            ob = 5 + self.rot("stick_o", 2)
            oacc = (self.ps[:, ob, :], ("ps", ob))
            ebv_ = self.ebf[:, :, :].rearrange("p a b -> p (a b)")
            spb = [(self.hb[0], ("hb", 0)), (self.hb[1], ("hb", 1)), (ebv_, ("ebf", 0))]
            for b_, bk_ in spb:
                P.op("pool", lambda g_, b_=b_: g_.memset(b_[:, :n], 0.0), [], [bk_])
            state = dict(bi=0, cur=0)
            nblocks = 4 * len(groups) + ntb

            def block(klhs, kkey, vl, vlkey, nk, q0, diag, h=h, oacc=oacc, state=state, nblocks=nblocks, spb=spb):
                bi = state["bi"]
                state["bi"] += 1
                first, last = (bi == 0), (bi == nblocks - 1)
                vcur, vkey = spb[bi % 3]
                vnew, vnkey = spb[(bi + 1) % 3]
                hold = {}

                def s1():
                    sb_, sk = self.bank(0, 3)
                    P.mm(sb_[:nk, q0:n], klhs, self.qT(h)[:, q0:n], True, True, [kkey, ("qT", h)], [sk])
                    te = self.rot("tm", 4)
                    P.act(self.tm[te][:nk, q0:n], sb_[:nk, q0:n], AF.Exp, [sk], [("tm", te)], scale=scale)
                    tsp = self.rot("st", 3)
                    sp = self.st[tsp]
                    P.act(sp[:nk, q0:n], self.tm[te][:nk, q0:n], AF.Ln, [("tm", te)], [("st", tsp)], bias=1.0)
                    if diag:
                        w = min(128, n - q0)
                        P.tt(sp[:nk, q0:q0 + w], sp[:nk, q0:q0 + w], self.cf[:nk, C_MSC:C_MSC + w], ALU.mult, [("st", tsp), ("cf",)], [("st", tsp)])
                    tz = self.rot("tm", 4)
                    P.stt(self.tm[tz][:nk, q0:n], sb_[:nk, q0:n], scale, sp[:nk, q0:n], ALU.mult, ALU.subtract, [sk, ("st", tsp)], [("tm", tz)])
                    P.tt(vnew[:nk, q0:n], vcur[:nk, q0:n], sp[:nk, q0:n], ALU.add, [vkey, ("st", tsp)], [vnkey], e="pool")
                    hold["tsp"], hold["tz"] = tsp, tz

                def s2():
                    tsp, tz = hold["tsp"], hold["tz"]
                    sp, arg = self.st[tsp], self.tm[tz]
                    tb_, tk = self.bank(3, 5)
                    P.mm(tb_[:nk, q0:n], self.cf[:nk, C_U:C_U + nk], sp[:nk, q0:n], True, first, [("cf",), ("st", tsp)], [tk])
                    if not first:
                        P.mm(tb_[:nk, q0:n], onesf[:, :nk], vcur[:, q0:n], False, True, [("cf",), vkey], [tk])
                    P.tt(arg[:nk, q0:n], arg[:nk, q0:n], tb_[:nk, q0:n], ALU.subtract, [("tm", tz), tk], [("tm", tz)])
                    p = self.rot("pt", 3)
                    pt = self.pt[p]
                    P.act(pt[:nk, q0:n], arg[:nk, q0:n], AF.Exp, [("tm", tz)], [("pt", p)])
                    if diag:
                        w = min(128, n - q0)
                        P.tt(pt[:nk, q0:q0 + w], pt[:nk, q0:q0 + w], self.cb[:nk, C_MSC:C_MSC + w], ALU.mult, [("pt", p), ("cb",)], [("pt", p)])
                    hold["p"] = p

                def s3():
                    p = hold["p"]
                    P.mm(oacc[0][:, q0:n], vl, self.pt[p][:nk, q0:n], first, last, [vlkey, ("pt", p)], [oacc[1]])
                    if last:
                        P.copy(self.oT[:, h, :n], oacc[0][:, :n], [oacc[1]], [("oT", h)], e="act")

                push(s1, s2, s3)

            issue(k)
            for kb in reversed(range(ntb)):
                nk = min(128, n - kb * 128)
                block(self.kT(h)[:, kb * 128:kb * 128 + nk], ("kT", h), self.vv(kb)[:nk, h * 128:(h + 1) * 128], ("v", kb), nk, kb * 128, True)
            for grp in rgroups:
                issue(k)
                s = slots[k]
                k += 1
                for bi_, kb in enumerate(reversed(range(4))):
                    block(self.ks[s][:, 0, kb * 128:(kb + 1) * 128], ("ks", s), self.vs[s][:, kb, 0:128], ("vs", s), 128, 0, False)
                    if bi_ == 1:
                        issue(k)
            flush()

    def rms_fm(self, src, srckey, nch, dim, gofs, outs):
        P, n = self.P, self.n
        pb, pk = self.bank(6, 8)
        for c in range(nch):
            t = self.rot("tm", 4)
            P.act(self.tm[t][:, :n], src(c)[:, :n], AF.Square, [(srckey, c)], [("tm", t)])
            P.mm(pb[:, :n], self.cf[:, C_ONES:C_ONES + 128], self.tm[t][:, :n], c == 0, c == nch - 1, [("cf",), ("tm", t)], [pk])
        tr = self.rot("hb", 2)
        rs = self.hb[tr]
        P.act(rs[:, :n], pb[:, :n], AF.Sqrt, [pk, ("sm",)], [("hb", tr)], bias=self.sm[:, 0:1], scale=1.0 / dim)
        P.op("dve", lambda g_: g_.reciprocal(rs[:, :n], rs[:, :n]), [("hb", tr)], [("hb", tr)])
        for c in range(nch):
            t = self.rot("tm", 4)
            P.tt(self.tm[t][:, :n], src(c)[:, :n], rs[:, :n], ALU.mult, [(srckey, c), ("hb", tr)], [("tm", t)])
            for (fn, key) in outs:
                P.ts(fn(c)[:, :n], self.tm[t][:, :n], self.vecs[:, gofs + c:gofs + c + 1], None, ALU.mult, None, [("tm", t), ("vecs",)], [(key, c)], e="pool")

    def mla_proj(self, l):
        P, n, t0, i = self.P, self.n, self.tok0, self.i
        xsrc = lambda kc: self.xb[:, kc, :n]
        regf = self.reg[:, 0:20 * NT].bitcast(F32)
        cqf = lambda c: regf[:, c * NT:(c + 1) * NT]
        lf = lambda c: regf[:, (6 + c) * NT:(7 + c) * NT]
        cqn = lambda c: self.oT[:, c, :]
        latT = lambda c: self.oT[:, 6 + c, :]
        self.krT2 = self.reg[:, 43 * NT:44 * NT]
        P.dma("sp", self.tab[:, :, :n], self.I["tabC"][:, :, t0:t0 + n], writes=[("tab",)])
        W, wk = self.wnext(); W3 = W[:, :].rearrange("p (k c) -> p k c", c=512)
        for cj in range(4):
            pb, pk = self.proj_fm(W3, wk, cj, 16, xsrc, "xb", n)
            P.copy(cqf(cj)[:, :n], pb[:, :n], [pk], [("cqf", cj)], e="act")
        W, wk = self.wnext(); W3 = W[:, :].rearrange("p (k c) -> p k c", c=512)
        for cj in range(2):
            pb, pk = self.proj_fm(W3, wk, cj, 16, xsrc, "xb", n)
            P.copy(cqf(4 + cj)[:, :n], pb[:, :n], [pk], [("cqf", 4 + cj)], e="act")
        pb, pk = self.bank(0, 4)
        for kc in range(16):
            P.mm(pb[0:64, :n], W3[:, kc, 256:320], xsrc(kc), kc == 0, kc == 15, [wk, ("xb", kc)], [pk])
        s = self.rot("st", 3)
        P.copy(self.st[s][0:64, :n], pb[0:64, :n], [pk], [("st", s)], e="act")
        pb2, pk2 = self.bank(4, 6)
        P.mm(pb2[:, :n], self.cf[0:64, C_DUP:C_DUP + 128], self.st[s][0:64, :n], True, True, [("cf",), ("st", s)], [pk2])
        s2 = self.rot("st", 3)
        P.copy(self.st[s2][:, :n], pb2[:, :n], [pk2], [("st", s2)], e="act")
        self.rope(s2, 128, C_PERMC, ("tab",))
        P.dma("pool", self.O["krT"][:, t0:t0 + n], self.st[s2][0:64, :n], reads=[("st", s2)], writes=[("o_kr", i)])
        P.copy(self.krT2[:, :n], self.st[s2][:, :n], [("st", s2)], [("krT2",)], e="pool")
        if not self.samp:
            P.dma("pool", self.S["KR"][:, t0:t0 + n], self.krT2[:, :n], reads=[("krT2",)], writes=[("KRs", i)])
        W, wk = self.wnext(); W3 = W[:, :].rearrange("p (k c) -> p k c", c=512)
        for cj in range(4):
            pb, pk = self.proj_fm(W3, wk, cj, 16, xsrc, "xb", n)
            P.copy(lf(cj)[:, :n], pb[:, :n], [pk], [("lf", cj)], e="act")
        W, wk = self.wnext(); W3 = W[:, :].rearrange("p (k c) -> p k c", c=512)
        for cj in range(4):
            pb, pk = self.proj_fm(W3, wk, cj, 16, xsrc, "xb", n)
            P.copy(self.qm(cj)[:, :n], pb[:, :n], [pk], [("qm", cj)], e="act")
        self.rms_fm(cqf, "cqf", 6, 768, V_QG, [(cqn, "cqn")])
        self.rms_fm(lf, "lf", 4, 512, V_KG, [(lf, "lf2"), (latT, "latT")])
        for c in range(4):
            P.dma("pool", self.O["latT"][c * 128:(c + 1) * 128, t0:t0 + n], lf(c)[:, :n], reads=[("lf2", c)], writes=[("o_lat", c, i)])
        P.barrier()
        csrc = lambda kc: cqn(kc)[:, :n]
        for jt in range(5):
            W, wk = self.wnext(); W3 = W[:, :].rearrange("p (k c) -> p k c", c=512)
            for cj in range(4):
                jglob = jt * 4 + cj
                if jglob >= 18:
                    break
                pb, pk = self.proj_fm(W3, wk, cj, 6, csrc, "cqn", n)
                if jglob < 12:
                    P.copy(self.qT(jglob)[:, :n], pb[:, :n], [pk], [("qT", jglob)], e="act")
                else:
                    jj = jglob - 12
                    s = self.rot("st", 3)
                    P.copy(self.st[s][:, :n], pb[:, :n], [pk], [("st", s)], e="act")
                    self.rope(s, 128, C_PERMC, ("tab",))
                    P.copy(self.qrb(jj)[:, :n], self.st[s][:, :n], [("st", s)], [("qrb", jj)], e="pool")
        sets = [("past", 0), ("past", 1)] if self.samp else []
        sets.append(("own", None))
        vsf = [self.vs[k_][:, :, :].rearrange("p a b -> p (a b)") for k_ in range(2)]
        plat = lambda kc: vsf[kc // 2][:, (kc % 2) * 512:(kc % 2 + 1) * 512]
        for jt in range(6):
            W, wk = self.wnext(); W3 = W[:, :].rearrange("p (k c) -> p k c", c=512)
            for kind, g in sets:
                if kind == "past":
                    for kc in range(4):
                        P.dma("sp", plat(kc), self.S["sLat"][kc, :, g * 512:(g + 1) * 512], reads=[("sLat",)], writes=[("vs", kc // 2)])
                    src = lambda kc: plat(kc)
                    rkey = lambda kc: ("vs", kc // 2)
                    nn = 512
                else:
                    src = lambda kc: latT(kc)[:, :n]
                    rkey = lambda kc: ("latT", kc)
                    nn = n
                if jt < 3:
                    for cj in range(4):
                        j = jt * 4 + cj
                        pb, pk = self.bank(0, 4)
                        for kc in range(4):
                            P.mm(pb[:, :nn], W3[:, kc, cj * 128:(cj + 1) * 128], src(kc), kc == 0, kc == 3, [wk, rkey(kc)], [pk])
                        if kind == "own":
                            P.copy(self.kT(j)[:, :n], pb[:, :n], [pk], [("kT", j)], e="act")
                        else:
                            p = self.rot("pt", 3)
                            P.copy(self.pt[p][:, :], pb[:, :], [pk], [("pt", p)], e="act")
                            P.dma("pool", self.S["sKT2"][j, :, g * 512:(g + 1) * 512], self.pt[p][:, :], reads=[("pt", p)], writes=[("sKT2",)])
                else:
                    c0 = (jt - 3) * 512
                    for tb in range((nn + 127) // 128):
                        np_ = min(128, nn - tb * 128)
                        pb, pk = self.bank(0, 4)
                        for kc in range(4):
                            P.mm(pb[:np_, :], src(kc)[:, tb * 128:tb * 128 + np_], W3[:, kc, :], kc == 0, kc == 3, [wk, rkey(kc)], [pk])
                        if kind == "own":
                            P.copy(self.vv(tb)[:np_, c0:c0 + 512], pb[:np_, :], [pk], [("v", tb)], e="act")
                        else:
                            p = self.rot("pt", 3)
                            P.copy(self.pt[p][:, :], pb[:, :], [pk], [("pt", p)], e="act")
                            r0 = g * 512 + tb * 128
                            P.dma("pool", self.S["sV2"][r0:r0 + 128, c0:c0 + 512], self.pt[p][:, :], reads=[("pt", p)], writes=[("sV2",)])
        if not self.samp:
            P.dma("pool", self.S["KT2"].ap().rearrange("j p t -> p j t")[:, :, t0:t0 + n],
                  self.reg[:, 12 * NT:24 * NT].rearrange("p (j t) -> p j t", t=NT), reads=R("kT", range(12)), writes=[("KTs", 2, i)])
            P.dma("pool", self.S["V2"].ap()[t0:t0 + n, :].rearrange("(tb p) c -> p tb c", p=128),
                  self.reg[:, 24 * NT:24 * NT + 4 * 1536].rearrange("p (tb c) -> p tb c", c=1536), reads=R("v", range(4)), writes=[("Vs", 2, i)])

    def attn_mla(self, l):
        P, n = self.P, self.n
        P.barrier()
        heads = []
        for h in range(12):
            def fin(oacc, rec, h=h):
                P.tt(self.oT[:, h, :n], oacc[0][0][:, :n], rec[0][:, :n], ALU.mult, [oacc[0][1], rec[1]], [("oT", h)])
            heads.append(dict(q=(self.qT(h), ("qT", h)), qr=(self.qrb(h // 2), ("qrb", h // 2), 64 * (h % 2)), kj=h,
                              kown=(self.kT(h), ("kT", h)), vc0=128 * h, dv=128, fin=fin))
        self.softmax_heads(l, heads, 192 ** -0.5, True)

    def ln_stats(self, c):
        P, n = self.P, self.n
        onesf = self.cf[:, C_ONES:C_ONES + 128]
        pa, pak = self.ps[:, 6, :], ("ps", 6)
        pq, pqk = self.ps[:, 7, :], ("ps", 7)
        P.mm(pa[:, :n], onesf, self.xf[:, c, :n], c == 0, c == 15, [("cf",), ("xf", c)], [pak])
        t = self.rot("tm", 4)
        P.act(self.tm[t][:, :n], self.xf[:, c, :n], AF.Square, [("xf", c)], [("tm", t)])
        P.mm(pq[:, :n], onesf, self.tm[t][:, :n], c == 0, c == 15, [("cf",), ("tm", t)], [pqk])

    def layer_norm(self, l, which):
        P, n = self.P, self.n
        go = V_LN + ((2 * which) * 4 + l) * 16
        bo = V_LN + ((2 * which + 1) * 4 + l) * 16
        onesf = self.cf[:, C_ONES:C_ONES + 128]
        pa, pak = self.ps[:, 6, :], ("ps", 6)
        pq, pqk = self.ps[:, 7, :], ("ps", 7)
        mu, rstd = self.hb[0], self.hb[1]
        P.act(mu[:, :n], pa[:, :n], AF.Identity, [pak], [("hb", 0)], scale=1.0 / D)
        t = self.rot("tm", 4)
        P.tt(self.tm[t][:, :n], mu[:, :n], mu[:, :n], ALU.mult, [("hb", 0)], [("tm", t)])
        P.stt(rstd[:, :n], pq[:, :n], 1.0 / D, self.tm[t][:, :n], ALU.mult, ALU.subtract, [pqk, ("tm", t)], [("hb", 1)])
        P.act(rstd[:, :n], rstd[:, :n], AF.Sqrt, [("hb", 1), ("sm",)], [("hb", 1)], bias=self.sm[:, 0:1])
        P.op("dve", lambda g_: g_.reciprocal(rstd[:, :n], rstd[:, :n]), [("hb", 1)], [("hb", 1)])
        P.stt(mu[:, :n], mu[:, :n], -1.0, rstd[:, :n], ALU.mult, ALU.mult, [("hb", 0), ("hb", 1)], [("hb", 0)])
        for c in range(16):
            t = self.rot("tm", 4)
            P.tt(self.tm[t][:, :n], self.xf[:, c, :n], rstd[:, :n], ALU.mult, [("xf", c), ("hb", 1)], [("tm", t)])
            P.tt(self.tm[t][:, :n], self.tm[t][:, :n], mu[:, :n], ALU.add, [("tm", t), ("hb", 0)], [("tm", t)], e="pool")
            P.act(self.xf[:, c, :n], self.tm[t][:, :n], AF.Identity, [("tm", t), ("vecs",)], [("xf", c)],
                  bias=self.vecs[:, bo + c:bo + c + 1], scale=self.vecs[:, go + c:go + c + 1])
            P.copy(self.xb[:, c, :n], self.xf[:, c, :n], [("xf", c)], [("xb", c)], e="pool")

    def out_proj_ffn(self, l):
        P, n, i = self.P, self.n, self.i
        for jt in range(4):
            W, wk = self.wnext(); W3 = W[:, :].rearrange("p (k c) -> p k c", c=512)
            for cj in range(4):
                oc = jt * 4 + cj
                pb, pk = self.bank(0, 4)
                for kc in range(16):
                    P.mm(pb[:, :n], W3[:, kc, cj * 128:(cj + 1) * 128], self.oT[:, kc, :n], kc == 0, kc == 15, [wk, ("oT", kc)], [pk])
                P.stt(self.xf[:, oc, :n], self.xf[:, oc, :n], ALPHA, pb[:, :n], ALU.mult, ALU.add, [("xf", oc), pk], [("xf", oc)])
                if oc >= 1:
                    self.ln_stats(oc - 1)
        self.ln_stats(15)
        self.layer_norm(l, 0)
        P.barrier()
        cst = self.cst
        ckey = ("cst",)
        v = self.vecs
        for t in range(22):
            W, wk = self.wnext(); W3 = W[:, :].rearrange("p (k c) -> p k c", c=512)
            hbufs = {}
            for cj in range(4):
                g = 2 * t + (cj % 2)
                ch = g if cj < 2 else 44 + g
                pb, pk = self.bank(0, 6)
                for kc in range(16):
                    P.mm(pb[:, :n], W3[:, kc, cj * 128:(cj + 1) * 128], self.xb[:, kc, :n], kc == 0, kc == 15, [wk, ("xb", kc)], [pk])
                u = self.rot("ue", 2)
                ue = self.ue[u]
                P.copy(ue[:, 0:2], cst[:, l, ch, :], [ckey], [("ue", u)], e="pool")
                P.copy(ue[:, 2:2 + n], pb[:, :n], [pk, ("ue", u)], [("ue", u)], e="act")
                P.copy(cst[:, l, ch, :], ue[:, n:n + 2], [("ue", u)], [ckey], e="pool")
                hbk = self.rot("hbc", 4)
                hb = self.hcv[hbk]
                cw = lambda j_: v[:, V_CW + (l * 3 + j_) * 88 + ch: V_CW + (l * 3 + j_) * 88 + ch + 1]
                cbias = v[:, V_CB + l * 88 + ch: V_CB + l * 88 + ch + 1]
                P.act(hb[:, :n], pb[:, :n], AF.Identity, [pk, ("vecs",)], [("hcv", hbk)], bias=cbias, scale=cw(2))
                P.stt(hb[:, :n], ue[:, 1:1 + n], cw(1), hb[:, :n], ALU.mult, ALU.add, [("ue", u), ("vecs",), ("hcv", hbk)], [("hcv", hbk)])
                P.stt(hb[:, :n], ue[:, 0:n], cw(0), hb[:, :n], ALU.mult, ALU.add, [("ue", u), ("vecs",), ("hcv", hbk)], [("hcv", hbk)])
                hbufs[cj] = hbk
            for gg in range(2):
                g = 2 * t + gg
                a, b_ = hbufs[gg], hbufs[2 + gg]
                P.act(self.hcv[a][:, :n], self.hcv[a][:, :n], AF.Silu, [("hcv", a)], [("hcv", a)])
                P.tt(self.hT(g)[:, :n], self.hcv[a][:, :n], self.hcv[b_][:, :n], ALU.mult, [("hcv", a), ("hcv", b_)], [("hT", g)], e="pool")
        for oc in range(16):
            W, wk = self.wnext(); Wd = W[:, 0:44 * 128].rearrange("p (k c) -> p k c", c=128)
            pb, pk = self.bank(0, 6)
            for fc in range(44):
                P.mm(pb[:, :n], Wd[:, fc, :], self.hT(fc)[:, :n], fc == 0, fc == 43, [wk, ("hT", fc)], [pk])
            P.stt(self.xf[:, oc, :n], self.xf[:, oc, :n], ALPHA, pb[:, :n], ALU.mult, ALU.add, [("xf", oc), pk], [("xf", oc)])
            if oc >= 1:
                self.ln_stats(oc - 1)
        self.ln_stats(15)
        self.layer_norm(l, 1)


def run(inputs, cfg):
    k = Kern(cfg)
    nc = k.build()
    wt, nels, pro_seq, layer_seq = build_weight_tiles(inputs)
    shared = build_shared(inputs)
    in_maps = []
    for c in range(8):
        d = dict(shared)
        for gi, grp in enumerate([pro_seq] + layer_seq):
            d["wt%d" % gi] = wt[grp[0]:grp[-1] + 1] if gi <= k.NL else wt[grp[0]:grp[0] + 1]
        d.update(build_core_inputs(inputs, c))
        in_maps.append(d)
    import time as _t
    t0_ = _t.time()
    res = run_bass_kernel_spmd(nc, in_maps, core_ids=list(range(8)))
    print("spmd run seconds", _t.time() - t0_, flush=True)
    return res.results


def assemble(R_):
    f = np.float32
    def P2(fn):
        return np.stack([fn(R_[b]) for b in range(2)])
    def S8(fn):
        return np.stack([fn(R_[c]) for c in range(8)])
    y_p = P2(lambda r: r["yT"][:, :TP].T)
    y_s = S8(lambda r: r["yT"][:, TP:].T)
    def kfm(r, name, sl, H, dd):
        a = r[name][:, sl].T
        return a.reshape(a.shape[0], H, dd)
    pa, sa = slice(0, TP), slice(TP, TOT)
    outs = [y_p, y_s]
    outs.append(P2(lambda r: kfm(r, "kT0", pa, 6, 256))[None])
    outs.append(P2(lambda r: r["v0"][pa].reshape(TP, 6, 256))[None])
    outs.append(P2(lambda r: kfm(r, "kT1", slice(TP - 512, TP), 12, 128))[None])
    outs.append(P2(lambda r: r["v1"][TP - 512:TP].reshape(512, 12, 128))[None])
    outs.append(P2(lambda r: r["latT"][:, pa].T)[None])
    outs.append(P2(lambda r: r["krT"][:, pa].T)[None])
    outs.append(P2(lambda r: kfm(r, "kT3", pa, 12, 128))[None])
    outs.append(P2(lambda r: r["v3"][pa].reshape(TP, 12, 128))[None])
    outs.append(np.stack([np.stack([R_[b]["memkT"][l].T.reshape(256, 4, 128) for b in range(2)]) for l in range(4)]))
    outs.append(np.stack([np.stack([R_[b]["memv"][l].reshape(256, 4, 128) for b in range(2)]) for l in range(4)]))
    def conv(a):
        return a.transpose(1, 3, 2, 0).reshape(4, 2, 88 * 128)
    outs.append(np.stack([conv(R_[b]["conv_p"]) for b in range(2)], axis=1))
    outs.append(S8(lambda r: kfm(r, "kT0", sa, 6, 256))[None])
    outs.append(S8(lambda r: r["v0"][sa].reshape(NS, 6, 256))[None])
    outs.append(S8(lambda r: r["bkT_s"].T.reshape(512, 12, 128))[None])
    outs.append(S8(lambda r: r["bv_s"].reshape(512, 12, 128))[None])
    outs.append(S8(lambda r: r["latT"][:, sa].T)[None])
    outs.append(S8(lambda r: r["krT"][:, sa].T)[None])
    outs.append(S8(lambda r: kfm(r, "kT3", sa, 12, 128))[None])
    outs.append(S8(lambda r: r["v3"][sa].reshape(NS, 12, 128))[None])
    outs.append(np.stack([conv(R_[c]["conv_s"]) for c in range(8)], axis=1))
    return tuple(np.ascontiguousarray(o.astype(f)) for o in outs)


def kernel(**inputs):
    inputs = {k: np.asarray(v) for k, v in inputs.items()}
    R_ = run(inputs, {})
    return assemble(R_)
```

```python
import contextlib
import math
import numpy as np
import concourse.bass as bass
import concourse.mybir as mybir
from concourse.bass_utils import run_bass_kernel_spmd

F32 = mybir.dt.float32
BF16 = mybir.dt.bfloat16
AF = mybir.ActivationFunctionType
ALU = mybir.AluOpType

D = 2048
NT = 512
NPT = 8
TP = 4096
NS = 64
TOT = TP + NS
PAST = 1024
DFF = 5632
ALPHA = 8 ** 0.25
EPS = 1e-5
WEL = 8192
SAME_ENG_SYNC = True
NDS = 40
MASKV = -1.0e4

V_LN = 0
V_CW = V_LN + 4 * 4 * 16
V_CB = V_CW + 4 * 3 * 88
V_DG = V_CB + 4 * 88
V_QG = V_DG + 2
V_KG = V_QG + 6
V_LAM = V_KG + 4
NV = V_LAM + 4
C_ONES = 0
C_PERMA = 128
C_PERMC = 160
C_U = 288
C_MCC = 416
C_MSC = 544
C_DUP = 672
NC = 800


def lam_init(i):
    return 0.8 - 0.6 * math.exp(-0.3 * i)


def _colblock(w, cols):
    kc = w.shape[0] // 128
    t = np.zeros((128, 16, 512), np.float32)
    t[:, :kc, :len(cols)] = w[:, cols].reshape(kc, 128, len(cols)).transpose(1, 0, 2)
    return t.reshape(128, WEL), kc * 512


def _downblock(w, oc):
    t = np.zeros((128, WEL), np.float32)
    t[:, :44 * 128] = w[:, oc * 128:(oc + 1) * 128].reshape(44, 128, 128).transpose(1, 0, 2).reshape(128, 44 * 128)
    return t, 44 * 128


def build_weight_tiles(inp):
    tiles, nels = [], []
    layer_seq = [[] for _ in range(4)]
    pro_seq = []

    def add(t, lst):
        lst.append(len(tiles))
        tiles.append(t[0])
        nels.append(t[1])

    for l in range(4):
        wm = inp["w_mem_kv"][l]
        add(_colblock(wm, np.arange(0, 512)), pro_seq)
        add(_colblock(wm, np.arange(512, 1024)), pro_seq)
    w_in = [inp["w_in_a"][0], inp["w_in_b"][0], inp["w_in_c"][0], inp["w_in_d"][0]]
    for l in range(4):
        seq = layer_seq[l]
        w = w_in[l]
        if l != 2:
            for j in range(10):
                add(_colblock(w, np.arange(j * 512, (j + 1) * 512)), seq)
        else:
            add(_colblock(w, np.arange(0, 512)), seq)
            add(_colblock(w, np.concatenate([np.arange(512, 768), np.arange(1280, 1344)])), seq)
            add(_colblock(w, np.arange(768, 1280)), seq)
            add(_colblock(w, np.arange(1344, 1856)), seq)
            wq = inp["mla_w_uq"][0]
            nope = np.concatenate([np.arange(h * 192, h * 192 + 128) for h in range(12)])
            rope = np.concatenate([np.arange(h * 192 + 128, h * 192 + 192) for h in range(12)])
            qcols = np.concatenate([nope, rope])
            for j in range(5):
                add(_colblock(wq, qcols[j * 512:(j + 1) * 512]), seq)
            wkv = inp["mla_w_ukv"][0]
            kn = np.concatenate([np.arange(h * 256, h * 256 + 128) for h in range(12)])
            vv = np.concatenate([np.arange(h * 256 + 128, h * 256 + 256) for h in range(12)])
            kvcols = np.concatenate([kn, vv])
            for j in range(6):
                add(_colblock(wkv, kvcols[j * 512:(j + 1) * 512]), seq)
        wo = inp["w_o"][l]
        for j in range(4):
            add(_colblock(wo, np.arange(j * 512, (j + 1) * 512)), seq)
        wu = inp["w_up"][l]
        for t in range(22):
            cols = np.concatenate([np.arange(2 * t * 128, (2 * t + 2) * 128), DFF + np.arange(2 * t * 128, (2 * t + 2) * 128)])
            add(_colblock(wu, cols), seq)
        wd = inp["w_down"][l]
        for oc in range(16):
            add(_downblock(wd, oc), seq)
    return np.stack(tiles), nels, pro_seq, layer_seq


def weight_tile_meta():
    nels, layer_seq, pro_seq = [], [[] for _ in range(4)], []

    def add(nel, lst):
        lst.append(len(nels))
        nels.append(nel)

    for l in range(4):
        add(WEL, pro_seq)
        add(WEL, pro_seq)
    for l in range(4):
        seq = layer_seq[l]
        if l != 2:
            for j in range(10):
                add(WEL, seq)
        else:
            for j in range(4):
                add(WEL, seq)
            for j in range(5):
                add(6 * 512, seq)
            for j in range(6):
                add(4 * 512, seq)
        for j in range(4):
            add(WEL, seq)
        for t in range(22):
            add(WEL, seq)
        for oc in range(16):
            add(44 * 128, seq)
    return nels, pro_seq, layer_seq


def fm(v):
    return np.ascontiguousarray(v.reshape(-1, 128).T)


def build_shared(inp):
    vecs = np.zeros((128, NV), np.float32)
    for k, name in enumerate(["ln1_g", "ln1_b", "ln2_g", "ln2_b"]):
        for l in range(4):
            o = V_LN + (k * 4 + l) * 16
            vecs[:, o:o + 16] = fm(inp[name][l])
    for l in range(4):
        for j in range(3):
            o = V_CW + (l * 3 + j) * 88
            vecs[:, o:o + 88] = fm(inp["conv_ffn_w"][l, j])
        o = V_CB + l * 88
        vecs[:, o:o + 88] = fm(inp["conv_ffn_b"][l])
    vecs[:, V_DG:V_DG + 2] = fm(inp["diff_norm_g"][0])
    vecs[:, V_QG:V_QG + 6] = fm(inp["mla_q_norm_g"][0])
    vecs[:, V_KG:V_KG + 4] = fm(inp["mla_kv_norm_g"][0])
    for k, name in enumerate(["diff_lambda_q1", "diff_lambda_k1", "diff_lambda_q2", "diff_lambda_k2"]):
        vecs[:, V_LAM + k] = inp[name][0]
    c = np.zeros((128, NC), np.float32)
    c[:, C_ONES:C_ONES + 128] = 1.0
    for p in range(32):
        c[p, C_PERMA + (p + 16) % 32] = 1.0
    for p in range(128):
        g = (p // 64) * 64
        c[p, C_PERMC + g + ((p - g) + 32) % 64] = 1.0
    j = np.arange(128)[:, None]
    k = np.arange(128)[None, :]
    c[:, C_U:C_U + 128] = (j > k)
    c[:, C_MCC:C_MCC + 128] = (j // 64 <= k // 64)
    c[:, C_MSC:C_MSC + 128] = (j < k)
    for p in range(64):
        c[p, C_DUP + p] = 1.0
        c[p, C_DUP + 64 + p] = 1.0
    pos = np.concatenate([np.arange(TP), PAST + np.arange(NS)]).astype(np.float32)
    tabA = np.zeros((32, 2, TOT), np.float32)
    invA = (500000.0 ** (-np.arange(16, dtype=np.float32) / 16)).astype(np.float32)
    angA = pos[None, :] * invA[:, None]
    tabA[:16, 0] = np.cos(angA); tabA[16:, 0] = np.cos(angA)
    tabA[:16, 1] = -np.sin(angA); tabA[16:, 1] = np.sin(angA)
    tabC = np.zeros((128, 2, TOT), np.float32)
    invC = (10000.0 ** (-np.arange(32, dtype=np.float32) / 32)).astype(np.float32)
    angC = pos[None, :] * invC[:, None]
    for g in range(2):
        tabC[g * 64:g * 64 + 32, 0] = np.cos(angC); tabC[g * 64 + 32:g * 64 + 64, 0] = np.cos(angC)
        tabC[g * 64:g * 64 + 32, 1] = -np.sin(angC); tabC[g * 64 + 32:g * 64 + 64, 1] = np.sin(angC)
    rb = inp["band_rel_bias"][0]
    ki = np.arange(128)[:, None]
    qi = np.arange(128)[None, :]
    bias = np.zeros((128, 12, 5, 128), np.float32)
    for r in range(5):
        rel = 128 * (4 - r) + qi - ki
        idx = np.clip(rel, -128, 128) + 128
        dch = (8 - 2 * r) + qi // 64 - ki // 64
        ok = (dch >= 0) & (dch <= 8)
        for h in range(12):
            b = rb[h][idx]
            bias[:, h, r, :] = np.where(ok, b, np.float32(MASKV))
    return dict(vecs=vecs, consts=c, tabA=tabA, tabC=tabC, bbias=bias)


def build_core_inputs(inp, c):
    b, s = c % 2, c
    d = {}
    xT = np.empty((D, TOT), np.float32)
    xT[:, :TP] = inp["x_prompt"][b].T
    xT[:, TP:] = inp["x_sample"][s].T
    d["xT"] = xT
    d["memT"] = np.ascontiguousarray(inp["mem_prompt"][b].T)
    d["skt_a"] = np.ascontiguousarray(inp["cache_a_k"][0, s].reshape(PAST, 12, 128).transpose(1, 2, 0))
    d["sv_a"] = np.ascontiguousarray(inp["cache_a_v"][0, s].reshape(PAST, 1536))
    d["skt_b"] = np.ascontiguousarray(inp["cache_b_k"][0, s].transpose(1, 2, 0))
    d["sv_b"] = np.ascontiguousarray(inp["cache_b_v"][0, s].reshape(512, 1536))
    d["slat"] = np.ascontiguousarray(inp["cache_c_latent"][0, s].T.reshape(4, 128, PAST))
    kr = inp["cache_c_krope"][0, s].T
    d["skr"] = np.ascontiguousarray(np.concatenate([kr, kr], 0))
    d["skt_d"] = np.ascontiguousarray(inp["cache_d_k"][0, s].transpose(1, 2, 0))
    d["sv_d"] = np.ascontiguousarray(inp["cache_d_v"][0, s].reshape(PAST, 1536))
    d["smk"] = np.ascontiguousarray(inp["cache_mem_k"][:, s].transpose(0, 3, 2, 1))
    d["smv"] = np.ascontiguousarray(inp["cache_mem_v"][:, s].reshape(4, 2, 128, 512).transpose(0, 2, 1, 3))
    st = inp["state_ffn_conv"][:, s]
    d["cst_s"] = np.ascontiguousarray(st.reshape(4, 2, 88, 128).transpose(3, 0, 2, 1))
    return d


class Prog:
    def __init__(self, nc, es):
        self.nc, self.es = nc, es
        self.eng = {"pe": nc.tensor, "act": nc.scalar, "dve": nc.vector, "pool": nc.gpsimd, "sp": nc.sync}
        self.csem = {e: es.enter_context(nc.semaphore("c_" + e)) for e in ["pe", "act", "dve", "pool"]}
        self.ccnt = {e: 0 for e in self.csem}
        self.dsem = [es.enter_context(nc.semaphore("d%d" % i)) for i in range(NDS)]
        self.dcnt = [0] * NDS
        self.dnext = 0
        self.seen = {e: {} for e in self.eng}
        self.lw, self.rd = {}, {}
        self.ninst = 0

    def _sem(self, k):
        return self.csem[k[1]] if k[0] == "c" else self.dsem[k[1]]

    def _wait(self, e, deps):
        need = {}
        for k, v in deps:
            if v > need.get(k, 0):
                need[k] = v
        for k, v in need.items():
            if k == ("c", e) and (e == "pe" or not SAME_ENG_SYNC):
                continue
            if self.seen[e].get(k, 0) >= v:
                continue
            self.eng[e].wait_ge(self._sem(k), v)
            self.seen[e][k] = v
            self.ninst += 1

    def _deps(self, reads, writes):
        d = []
        for r in reads:
            if r in self.lw:
                d.append(self.lw[r])
        for w in writes:
            if w in self.lw:
                d.append(self.lw[w])
            d.extend(self.rd.get(w, {}).items())
        return d

    def _record(self, iid, reads, writes):
        for r in reads:
            rr = self.rd.setdefault(r, {})
            if iid[1] > rr.get(iid[0], 0):
                rr[iid[0]] = iid[1]
        for w in writes:
            self.lw[w] = iid
            self.rd[w] = {}

    def op(self, e, fn, reads=(), writes=()):
        self._wait(e, self._deps(reads, writes))
        self.ccnt[e] += 1
        iid = (("c", e), self.ccnt[e])
        fn(self.eng[e]).then_inc(self.csem[e], 1)
        self.ninst += 1
        self._record(iid, reads, writes)

    def dma(self, q, out, in_, reads=(), writes=()):
        k = self.dnext
        self.dnext = (k + 1) % NDS
        deps = self._deps(reads, writes)
        if self.dcnt[k] > 0:
            deps.append((("d", k), self.dcnt[k]))
        self._wait(q, deps)
        self.dcnt[k] += 16
        iid = (("d", k), self.dcnt[k])
        self.eng[q].dma_start(out=out, in_=in_).then_inc(self.dsem[k], 16)
        self.ninst += 1
        self._record(iid, reads, writes)

    def barrier(self):
        deps = [(("c", e), self.ccnt[e]) for e in self.ccnt if self.ccnt[e] > 0]
        skip = set()
        for key in (("w", 0), ("w", 1)):
            if key in self.lw and self.lw[key][0][0] == "d":
                skip.add(self.lw[key])
        for k in range(NDS):
            if self.dcnt[k] > 0:
                iid = (("d", k), self.dcnt[k])
                if iid in skip:
                    if self.dcnt[k] > 16:
                        deps.append((("d", k), self.dcnt[k] - 16))
                else:
                    deps.append(iid)
        for e in ["pe", "act", "dve", "pool"]:
            self._wait(e, [d for d in deps if d[0] != ("c", e)])

    def finish(self):
        deps = [(("c", e), self.ccnt[e]) for e in self.ccnt if self.ccnt[e] > 0]
        deps += [(("d", k), self.dcnt[k]) for k in range(NDS) if self.dcnt[k] > 0]
        self._wait("sp", deps)

    def mm(self, out, lhsT, rhs, start, stop, reads, writes):
        self.op("pe", lambda e: e.matmul(out, lhsT, rhs, start=start, stop=stop), reads, writes)

    def act(self, out, in_, func, reads, writes, bias=None, scale=None):
        kw = {}
        if bias is not None:
            kw["bias"] = bias
        if scale is not None:
            kw["scale"] = scale
        self.op("act", lambda e: e.activation(out, in_, func, **kw), reads, writes)

    def tt(self, out, in0, in1, op, reads, writes, e="dve"):
        self.op(e, lambda g: g.tensor_tensor(out, in0, in1, op), reads, writes)

    def ts(self, out, in0, s1, s2, op0, op1, reads, writes, e="dve"):
        if op1 is None:
            self.op(e, lambda g: g.tensor_scalar(out, in0, s1, None, op0), reads, writes)
        else:
            self.op(e, lambda g: g.tensor_scalar(out, in0, s1, s2, op0, op1), reads, writes)

    def stt(self, out, in0, sc, in1, op0, op1, reads, writes):
        self.op("dve", lambda g: g.scalar_tensor_tensor(out, in0, sc, in1, op0, op1), reads, writes)

    def copy(self, out, in_, reads, writes, e="dve"):
        if e == "act":
            self.act(out, in_, AF.Copy, reads, writes)
        else:
            self.op(e, lambda g: g.tensor_copy(out, in_), reads, writes)


def R(name, idx):
    return [(name, i) for i in idx]


class Kern:
    def __init__(self, cfg):
        self.cfg = cfg
        self.NL = cfg.get("NL", 4)
        self.tiles = cfg.get("tiles", list(range(9)))

    def build(self):
        nc = bass.Bass("TRN2", target_bir_lowering=False)
        self.nc = nc
        nels, pro_seq, layer_seq = weight_tile_meta()
        self.nels, self.pro_seq, self.layer_seq = nels, pro_seq, layer_seq
        NWT = len(nels)
        dt = nc.dram_tensor

        def inp(name, shape):
            return dt(name, list(shape), F32, kind="ExternalInput")

        def outp(name, shape):
            return dt(name, list(shape), F32, kind="ExternalOutput")

        I = {}
        self.wgroups = [pro_seq] + layer_seq
        self.wloc = {}
        for gi, grp in enumerate(self.wgroups):
            for li, ti in enumerate(grp):
                self.wloc[ti] = (gi, li)
        for gi, grp in enumerate(self.wgroups):
            ng = len(grp) if gi <= self.NL else 1
            I["wt%d" % gi] = inp("wt%d" % gi, [ng, 128, WEL])
        I["vecs"] = inp("vecs", [128, NV])
        I["consts"] = inp("consts", [128, NC])
        I["tabA"] = inp("tabA", [32, 2, TOT])
        I["tabC"] = inp("tabC", [128, 2, TOT])
        I["bbias"] = inp("bbias", [128, 12, 5, 128])
        I["xT"] = inp("xT", [D, TOT])
        I["memT"] = inp("memT", [D, 256])
        I["skt_a"] = inp("skt_a", [12, 128, PAST]); I["sv_a"] = inp("sv_a", [PAST, 1536])
        I["skt_b"] = inp("skt_b", [12, 128, 512]); I["sv_b"] = inp("sv_b", [512, 1536])
        I["slat"] = inp("slat", [4, 128, PAST]); I["skr"] = inp("skr", [128, PAST])
        I["skt_d"] = inp("skt_d", [12, 128, PAST]); I["sv_d"] = inp("sv_d", [PAST, 1536])
        I["smk"] = inp("smk", [4, 128, 4, 256]); I["smv"] = inp("smv", [4, 128, 2, 512])
        I["cst_s"] = inp("cst_s", [128, 4, 88, 2])
        self.I = I
        O = {}
        O["yT"] = outp("yT", [D, TOT])
        for l in (0, 1, 3):
            O["kT%d" % l] = outp("kT%d" % l, [1536, TOT])
            O["v%d" % l] = outp("v%d" % l, [TOT, 1536])
        O["latT"] = outp("latT", [512, TOT])
        O["krT"] = outp("krT", [64, TOT])
        O["bkT_s"] = outp("bkT_s", [1536, 512])
        O["bv_s"] = outp("bv_s", [512, 1536])
        O["memkT"] = outp("memkT", [4, 512, 256])
        O["memv"] = outp("memv", [4, 256, 512])
        O["conv_p"] = outp("conv_p", [128, 4, 88, 2])
        O["conv_s"] = outp("conv_s", [128, 4, 88, 2])
        self.O = O
        S = {}
        for gi, grp in enumerate(self.wgroups):
            ng = len(grp) if gi <= self.NL else 1
            S["wb%d" % gi] = dt("wb%d" % gi, [ng, 128, WEL], BF16)
        for l in range(4):
            S["KT%d" % l] = dt("KTs%d" % l, [12, 128, TP], BF16)
            S["V%d" % l] = dt("Vs%d" % l, [TP, 1536], BF16)
        S["KR"] = dt("KRs", [128, TP], BF16)
        S["sKT0"] = dt("sKT0", [12, 128, PAST], BF16); S["sV0"] = dt("sV0", [PAST, 1536], BF16)
        S["sKT1"] = dt("sKT1", [12, 128, 512], BF16); S["sV1"] = dt("sV1", [512, 1536], BF16)
        S["sKT2"] = dt("sKT2", [12, 128, PAST], BF16); S["sV2"] = dt("sV2", [PAST, 1536], BF16)
        S["sKT3"] = dt("sKT3", [12, 128, PAST], BF16); S["sV3"] = dt("sV3", [PAST, 1536], BF16)
        S["sLat"] = dt("sLat", [4, 128, PAST], BF16); S["sKR"] = dt("sKR", [128, PAST], BF16)
        S["mK"] = dt("mK", [2, 4, 128, 4, 256], BF16); S["mV"] = dt("mV", [2, 4, 128, 2, 512], BF16)
        self.S = S

        with contextlib.ExitStack() as es:
            P = Prog(nc, es)
            self.P = P
            sb = lambda name, shape, dty: es.enter_context(nc.sbuf_tensor("s_" + name, list(shape), dty))
            self.xf = sb("xf", [128, 16, NT], F32)
            self.xb = sb("xb", [128, 16, NT], BF16)
            self.wsl = [sb("w%d" % i, [128, WEL], BF16) for i in range(2)]
            self.reg = sb("reg", [128, 44 * NT], BF16)
            self.oT = sb("oT", [128, 16, NT], BF16)
            self.ks = [sb("ks%d" % i, [128, 2, NT], BF16) for i in range(2)]
            self.vs = [sb("vs%d" % i, [128, 4, 256], BF16) for i in range(2)]
            self.pt = [sb("pt%d" % i, [128, NT], BF16) for i in range(3)]
            self.st = [sb("st%d" % i, [128, NT], F32) for i in range(3)]
            self.ue = [sb("ue%d" % i, [128, NT + 4], F32) for i in range(2)]
            self.hb = [sb("hb%d" % i, [128, NT], F32) for i in range(2)]
            self.tm = [sb("tm%d" % i, [128, NT], F32) for i in range(4)]
            self.dtm = sb("dtm", [128, 4 * NT], F32)
            self.dtmv = lambda c: self.dtm[:, c * NT:(c + 1) * NT]
            self.hcv = [self.dtm[:, c * NT:(c + 1) * NT] for c in range(4)]
            dtb = self.dtm[:, :].bitcast(BF16)
            self.qrb = lambda jj: dtb[:, jj * NT:(jj + 1) * NT]
            self.vecs = sb("vecs", [128, NV], F32)
            self.cf = sb("cf", [128, NC], F32)
            self.cb = sb("cbf", [128, NC], BF16)
            self.cst = sb("cst", [128, 4, 88, 2], F32)
            self.mk = sb("mk", [128, 4, 256], BF16)
            self.mv = sb("mv", [128, 2, 512], BF16)
            self.tab = sb("tab", [128, 2, NT], F32)
            self.ebf = sb("ebf", [128, 5, 128], F32)
            self.sm = sb("sm", [128, 16], F32)
            self.ps = es.enter_context(nc.psum_tensor("ps", [128, 8, NT], F32))
            self.bank_rr = 0
            self.rr = {}
            reg = self.reg
            self.qT = lambda j: reg[:, j * NT:(j + 1) * NT]
            self.kT = lambda j: reg[:, (12 + j) * NT:(13 + j) * NT]
            self.vv = lambda tb: reg[:, 24 * NT + tb * 1536: 24 * NT + (tb + 1) * 1536]
            self.qm = lambda j: reg[:, 36 * NT + j * NT: 36 * NT + (j + 1) * NT]
            self.hT = lambda j: reg[:, j * NT:(j + 1) * NT]

            self.prologue()
            for i in self.tiles:
                self.do_tile(i)
            if NPT in self.tiles:
                P.dma("pool", self.O["conv_s"][:, :, :, :], self.cst[:], reads=[("cst",)], writes=[("o_convs",)])
            else:
                P.dma("pool", self.O["conv_p"][:, :, :, :], self.cst[:], reads=[("cst",)], writes=[("o_convp",)])
            P.finish()
        self.ninst = P.ninst
        return nc

    def rot(self, name, n):
        k = self.rr.get(name, 0)
        self.rr[name] = (k + 1) % n
        return k

    def bank(self, lo=0, hi=8):
        k = lo + self.rot(("bank", lo, hi), hi - lo)
        return self.ps[:, k, :], ("ps", k)

    def wnext(self):
        ws = self.wstate
        seq = ws["seq"]
        while ws["issued"] < min(len(seq), ws["pos"] + 2):
            k = ws["issued"]
            self.cast_upto(k + 6)
            ti = seq[k]
            nel = self.nels[ti]
            gi, li = self.wloc[ti]
            self.P.dma("sp", self.wsl[k % 2][:, :nel], self.S["wb%d" % gi][li][:, :nel], reads=[("wb", ti)], writes=[("w", k % 2)])
            ws["issued"] += 1
        k = ws["pos"]
        ws["pos"] += 1
        return self.wsl[k % 2], ("w", k % 2)

    def cast_upto(self, k):
        ws = self.wstate
        seq = ws["seq"]
        while ws["cpos"] < min(len(seq), k + 1):
            ti = seq[ws["cpos"]]
            ws["cpos"] += 1
            if ti in ws["casted"]:
                continue
            ws["casted"].add(ti)
            nel = self.nels[ti]
            gi, li = self.wloc[ti]
            self.P.dma("pool", self.S["wb%d" % gi][li][:, :nel], self.I["wt%d" % gi][li][:, :nel], writes=[("wb", ti)])

    def prologue(self):
        P, I, S = self.P, self.I, self.S
        seq = list(self.pro_seq)
        for i in self.tiles:
            for l in range(self.NL):
                seq += self.layer_seq[l]
        self.wstate = dict(seq=seq, pos=0, issued=0, cpos=0, casted=set())
        used = sorted(set(seq))
        P.dma("sp", self.vecs[:], I["vecs"][:, :], writes=[("vecs",)])
        P.dma("sp", self.cf[:], I["consts"][:, :], writes=[("cf",)])
        for a, b_, key in [("skt_a", "sKT0", "sKT0"), ("sv_a", "sV0", "sV0"), ("skt_b", "sKT1", "sKT1"), ("sv_b", "sV1", "sV1"),
                           ("slat", "sLat", "sLat"), ("skr", "sKR", "sKR"), ("skt_d", "sKT3", "sKT3"), ("sv_d", "sV3", "sV3")]:
            src, dst = I[a].ap(), S[b_].ap()
            P.dma("pool", dst, src, writes=[(key,)])
        for l in range(4):
            P.dma("pool", S["mK"][1, l], I["smk"][l], writes=[("mK", 1, l)])
            P.dma("pool", S["mV"][1, l], I["smv"][l], writes=[("mV", 1, l)])
        P.copy(self.cb[:], self.cf[:], reads=[("cf",)], writes=[("cb",)])
        P.op("dve", lambda g: g.memset(self.cst[:], 0.0), writes=[("cst",)])
        P.op("dve", lambda g: g.memset(self.sm[:], 0.0), writes=[("sm",)])
        P.op("dve", lambda g: g.memset(self.sm[:, 0:1], EPS), reads=[], writes=[("sm",)])
        li = lam_init(0)
        v = self.vecs
        P.tt(self.sm[:, 4:5], v[:, V_LAM:V_LAM + 1], v[:, V_LAM + 1:V_LAM + 2], ALU.mult, [("vecs",), ("sm",)], [("sm",)])
        P.tt(self.sm[:, 5:6], v[:, V_LAM + 2:V_LAM + 3], v[:, V_LAM + 3:V_LAM + 4], ALU.mult, [("vecs",), ("sm",)], [("sm",)])
        pb, pk = self.bank()
        P.mm(pb[:, 0:2], self.cf[:, C_ONES:C_ONES + 128], self.sm[:, 4:6], True, True, [("cf",), ("sm",)], [pk])
        P.act(self.sm[:, 6:8], pb[:, 0:2], AF.Exp, [pk, ("sm",)], [("sm",)])
        P.tt(self.sm[:, 8:9], self.sm[:, 7:8], self.sm[:, 6:7], ALU.subtract, [("sm",)], [("sm",)])
        P.ts(self.sm[:, 1:2], self.sm[:, 8:9], -li, None, ALU.add, None, [("sm",)], [("sm",)])
        P.ts(self.sm[:, 2:4], v[:, V_DG:V_DG + 2], 1.0 - li, None, ALU.mult, None, [("vecs",), ("sm",)], [("sm",)])
        P.dma("sp", self.xf[:, :, 0:256], I["memT"].ap().rearrange("(kc p) t -> p kc t", p=128), writes=R("xf", range(16)))
        P.copy(self.xb[:, :, 0:256], self.xf[:, :, 0:256], R("xf", range(16)), R("xb", range(16)))
        for l in range(4):
            W, wk = self.wnext()
            W3 = W[:, :].rearrange("p (k c) -> p k c", c=512)
            for j in range(4):
                pb, pk = self.bank()
                for kc in range(16):
                    P.mm(pb[:, 0:256], W3[:, kc, j * 128:(j + 1) * 128], self.xb[:, kc, 0:256], kc == 0, kc == 15, [wk] + R("xb", [kc]), [pk])
                s = self.rot("st", 3)
                P.copy(self.st[s][:, 0:256], pb[:, 0:256], [pk], [("st", s)], e="act")
                P.dma("pool", self.O["memkT"][l, j * 128:(j + 1) * 128, :], self.st[s][:, 0:256], reads=[("st", s)], writes=[("o_mk", l, j)])
                P.copy(self.mk[:, j, :], self.st[s][:, 0:256], [("st", s)], [("mk",)])
            P.dma("pool", S["mK"][0, l], self.mk[:], reads=[("mk",)], writes=[("mK", 0, l)])
            W, wk = self.wnext()
            W3 = W[:, :].rearrange("p (k c) -> p k c", c=512)
            for tb in range(2):
                pb, pk = self.bank()
                for kc in range(16):
                    P.mm(pb[:, :], self.xb[:, kc, tb * 128:(tb + 1) * 128], W3[:, kc, :], kc == 0, kc == 15, [wk] + R("xb", [kc]), [pk])
                s = self.rot("st", 3)
                P.copy(self.st[s][:, :], pb[:, :], [pk], [("st", s)], e="act")
                P.dma("pool", self.O["memv"][l, tb * 128:(tb + 1) * 128, :], self.st[s][:, :], reads=[("st", s)], writes=[("o_mv", l, tb)])
                P.copy(self.mv[:, tb, :], self.st[s][:, :], [("st", s)], [("mv",)])
            P.dma("pool", S["mV"][0, l], self.mv[:], reads=[("mv",)], writes=[("mV", 0, l)])

    def do_tile(self, i):
        P, I = self.P, self.I
        self.i = i
        self.samp = (i == NPT)
        self.n = NS if self.samp else NT
        self.tok0 = i * NT
        n, t0 = self.n, self.tok0
        if self.samp:
            P.barrier()
            P.dma("pool", self.O["conv_p"][:, :, :, :], self.cst[:], reads=[("cst",)], writes=[("o_convp",)])
            P.dma("sp", self.cst[:], I["cst_s"][:, :, :, :], reads=[], writes=[("cst",)])
        P.dma("sp", self.xf[:, :, :n], I["xT"].ap().rearrange("(kc p) t -> p kc t", p=128)[:, :, t0:t0 + n], writes=R("xf", range(16)))
        P.copy(self.xb[:, :, :n], self.xf[:, :, :n], R("xf", range(16)), R("xb", range(16)))
        for l in range(self.NL):
            self.layer(l)
        P.dma("pool", self.O["yT"].ap().rearrange("(kc p) t -> p kc t", p=128)[:, :, t0:t0 + n], self.xf[:, :, :n], reads=R("xf", range(16)), writes=[("o_y", i)])

    def proj_fm(self, W3, wk, cj, nk, src, srckey, n):
        P = self.P
        pb, pk = self.bank(0, 4)
        for kc in range(nk):
            P.mm(pb[:, :n], W3[:, kc, cj * 128:(cj + 1) * 128], src(kc), kc == 0, kc == nk - 1, [wk, (srckey, kc)], [pk])
        return pb, pk

    def rope(self, s, npart, perm_off, tabkey):
        P, n = self.P, self.n
        stt_ = self.st[s]
        pb, pk = self.bank(4, 6)
        P.mm(pb[0:npart, :n], self.cf[0:npart, perm_off:perm_off + npart], stt_[0:npart, :n], True, True, [("cf",), ("st", s)], [pk])
        t = self.rot("tm", 4)
        P.tt(self.tm[t][0:npart, :n], pb[0:npart, :n], self.tab[0:npart, 1, :n], ALU.mult, [pk, tabkey], [("tm", t)])
        P.tt(stt_[0:npart, :n], stt_[0:npart, :n], self.tab[0:npart, 0, :n], ALU.mult, [("st", s), tabkey], [("st", s)])
        P.tt(stt_[0:npart, :n], stt_[0:npart, :n], self.tm[t][0:npart, :n], ALU.add, [("st", s), ("tm", t)], [("st", s)])

    def kv_out_names(self, l):
        return self.O["kT%d" % l], self.O["v%d" % l]

    def layer(self, l):
        P, n, t0, i = self.P, self.n, self.tok0, self.i
        m = l % 4
        g = 1 if self.samp else 0
        P.dma("sp", self.mk[:], self.S["mK"][g, l], reads=[("mK", g, l)], writes=[("mk",)])
        P.dma("sp", self.mv[:], self.S["mV"][g, l], reads=[("mV", g, l)], writes=[("mv",)])
        if m == 2:
            self.mla_proj(l)
        else:
            self.qkv_proj(l)
        if m == 0:
            self.attn_diff(l)
        elif m == 1:
            self.attn_band(l)
        elif m == 2:
            self.attn_mla(l)
        else:
            self.attn_stick(l)
        self.attn_mem(l)
        self.out_proj_ffn(l)

    def qkv_proj(self, l):
        P, n, t0, i = self.P, self.n, self.tok0, self.i
        xsrc = lambda kc: self.xb[:, kc, :n]
        OK, OV = self.kv_out_names(l)
        if l == 0:
            P.dma("sp", self.tab[0:32, :, :n], self.I["tabA"][:, :, t0:t0 + n], writes=[("tab",)])
        for part in range(2):
            for jt in range(3):
                W, wk = self.wnext()
                W3 = W[:, :].rearrange("p (k c) -> p k c", c=512)
                for cj in range(4):
                    j = jt * 4 + cj
                    pb, pk = self.proj_fm(W3, wk, cj, 16, xsrc, "xb", n)
                    dst = self.qT(j) if part == 0 else self.kT(j)
                    dkey = ("qT", j) if part == 0 else ("kT", j)
                    if l == 0 or part == 1:
                        s = self.rot("st", 3)
                        P.copy(self.st[s][:, :n], pb[:, :n], [pk], [("st", s)], e="act")
                        if l == 0:
                            self.rope(s, 32, C_PERMA, ("tab",))
                        if part == 1:
                            P.dma("pool", OK[j * 128:(j + 1) * 128, t0:t0 + n], self.st[s][:, :n], reads=[("st", s)], writes=[("o_k", l, j, i)])
                            if l == 1 and self.samp:
                                P.dma("pool", self.O["bkT_s"][j * 128:(j + 1) * 128, 448:512], self.st[s][:, :n], reads=[("st", s)], writes=[("o_bks", j)])
                        P.copy(dst[:, :n], self.st[s][:, :n], [("st", s)], [dkey], e="pool")
                    else:
                        P.copy(dst[:, :n], pb[:, :n], [pk], [dkey], e="act")
        if not self.samp:
            P.dma("pool", self.S["KT%d" % l].ap().rearrange("j p t -> p j t")[:, :, t0:t0 + n],
                  self.reg[:, 12 * NT:24 * NT].rearrange("p (j t) -> p j t", t=NT), reads=R("kT", range(12)), writes=[("KTs", l, i)])
        ntb = (n + 127) // 128
        for jt in range(3):
            W, wk = self.wnext()
            W3 = W[:, :].rearrange("p (k c) -> p k c", c=512)
            for tb in range(ntb):
                np_ = min(128, n - tb * 128)
                pb, pk = self.bank(0, 4)
                for kc in range(16):
                    P.mm(pb[:np_, :], self.xb[:, kc, tb * 128:tb * 128 + np_], W3[:, kc, :], kc == 0, kc == 15, [wk, ("xb", kc)], [pk])
                s = self.rot("st", 3)
                P.copy(self.st[s][:np_, :], pb[:np_, :], [pk], [("st", s)], e="act")
                r0 = t0 + tb * 128
                P.dma("pool", OV[r0:r0 + np_, jt * 512:(jt + 1) * 512], self.st[s][:np_, :], reads=[("st", s)], writes=[("o_v", l, jt, tb, i)])
                if l == 1 and self.samp:
                    P.dma("pool", self.O["bv_s"][448:512, jt * 512:(jt + 1) * 512], self.st[s][:np_, :], reads=[("st", s)], writes=[("o_bvs", jt)])
                P.copy(self.vv(tb)[:np_, jt * 512:(jt + 1) * 512], self.st[s][:np_, :], [("st", s)], [("v", tb)], e="pool")
        if not self.samp:
            P.dma("pool", self.S["V%d" % l].ap()[t0:t0 + n, :].rearrange("(tb p) c -> p tb c", p=128),
                  self.reg[:, 24 * NT:24 * NT + 4 * 1536].rearrange("p (tb c) -> p tb c", c=1536), reads=R("v", range(4)), writes=[("Vs", l, i)])
        if l == 1 and self.samp:
            P.dma("pool", self.O["bkT_s"][:, 0:448], self.I["skt_b"].ap().rearrange("j p t -> (j p) t")[:, 64:512], writes=[("o_bks2",)])
            P.dma("pool", self.O["bv_s"][0:448, :], self.I["sv_b"][64:512, :], writes=[("o_bvs2",)])
        W, wk = self.wnext()
        W3 = W[:, :].rearrange("p (k c) -> p k c", c=512)
        for cj in range(4):
            pb, pk = self.proj_fm(W3, wk, cj, 16, xsrc, "xb", n)
            P.copy(self.qm(cj)[:, :n], pb[:, :n], [pk], [("qm", cj)], e="act")

    def kv_groups(self, l):
        S = self.S
        if self.samp:
            ng = 1 if l == 1 else 2
            return [(S["sKT%d" % l], S["sV%d" % l], S["sKR"], g, [("sKT%d" % l,), ("sV%d" % l,), ("sKR",)]) for g in range(ng)]
        return [(S["KT%d" % l], S["V%d" % l], S["KR"], g, [("KTs", l, g), ("Vs", l, g), ("KRs", g)]) for g in range(self.i)]

    def load_group(self, grp, j, vc0, dv, with_kr=False):
        P = self.P
        KT, V, KR, g, keys = grp
        s = self.rot("kvs", 2)
        P.dma("sp", self.ks[s][:, 0, :], KT[j, :, g * 512:(g + 1) * 512], reads=[keys[0]], writes=[("ks", s)])
        if with_kr:
            P.dma("sp", self.ks[s][:, 1, :], KR[:, g * 512:(g + 1) * 512], reads=[keys[2]], writes=[("ks", s)])
        P.dma("sp", self.vs[s][:, :, 0:dv], V.ap()[g * 512:(g + 1) * 512, vc0:vc0 + dv].rearrange("(kb p) c -> p kb c", p=128),
              reads=[keys[1]], writes=[("vs", s)])
        return s

    def accset(self):
        k = self.rot("accset", 2)
        base = 2 + 3 * k
        return [(self.ps[:, base + c, :], ("ps", base + c)) for c in range(3)]

    def pipe_push(self, s1, s2):
        s1()
        if self.pipe_pending is not None:
            self.pipe_pending()
        self.pipe_pending = s2

    def pipe_flush(self):
        if getattr(self, "pipe_pending", None) is not None:
            self.pipe_pending()
        self.pipe_pending = None

    def softmax_heads(self, l, heads, scale, masked):
        P, n, i = self.P, self.n, self.i
        groups = self.kv_groups(l)
        items = [(hd, grp) for hd in heads for grp in groups]
        slots = {}

        def issue(k):
            if k < len(items) and k not in slots:
                hd, grp = items[k]
                slots[k] = self.load_group(grp, hd["kj"], hd["vc0"], hd["dv"], with_kr=hd.get("qr") is not None)

        k = 0
        ones_b = self.cb[:, C_ONES:C_ONES + 128]
        ntb = (n + 127) // 128
        self.pipe_pending = None
        for hd in heads:
            q, qk = hd["q"]
            ndv = hd["dv"] // 128
            acc = self.accset()
            oacc = acc[:ndv]
            sacc = acc[2]
            nblocks = 4 * len(groups) + ntb
            st = dict(bi=0)

            def block(klhs, kkeys, krlhs, vfn, vkey, nk, q0, diag, hd=hd, q=q, qk=qk, oacc=oacc, sacc=sacc, ndv=ndv, st=st, nblocks=nblocks):
                bi = st["bi"]
                st["bi"] += 1
                first, last = (bi == 0), (bi == nblocks - 1)
                hold = {}

                def s1():
                    sb_, sk = self.bank(0, 2)
                    if krlhs is None:
                        P.mm(sb_[:nk, q0:n], klhs, q[:, q0:n], True, True, kkeys + [qk], [sk])
                    else:
                        qr, qrk, hp = hd["qr"]
                        P.mm(sb_[:nk, q0:n], klhs, q[:, q0:n], True, False, kkeys + [qk], [sk])
                        P.mm(sb_[:nk, q0:n], krlhs, qr[hp:hp + 64, q0:n], False, True, kkeys + [qrk], [sk])
                    p = self.rot("pt", 3)
                    pt = self.pt[p]
                    P.act(pt[:nk, q0:n], sb_[:nk, q0:n], AF.Exp, [sk], [("pt", p)], scale=scale)
                    if diag and masked:
                        w = min(128, n - q0)
                        P.tt(pt[:nk, q0:q0 + w], pt[:nk, q0:q0 + w], self.cb[:nk, C_MCC:C_MCC + w], ALU.mult, [("pt", p), ("cb",)], [("pt", p)])
                    hold["p"] = p

                def s2():
                    p = hold["p"]
                    pt = self.pt[p]
                    for c in range(ndv):
                        P.mm(oacc[c][0][:, q0:n], vfn(c), pt[:nk, q0:n], first, last, [vkey, ("pt", p)], [oacc[c][1]])
                    P.mm(sacc[0][:, q0:n], ones_b[:nk, :], pt[:nk, q0:n], first, last, [("cb",), ("pt", p)], [sacc[1]])
                    if last:
                        t = self.rot("tm", 4)
                        P.op("dve", lambda g_: g_.reciprocal(self.tm[t][:, :n], sacc[0][:, :n]), [sacc[1]], [("tm", t)])
                        hd["fin"](oacc, (self.tm[t], ("tm", t)))

                self.pipe_push(s1, s2)

            for grp in groups:
                issue(k)
                s = slots[k]
                k += 1
                for kb in range(4):
                    krl = None
                    if hd.get("qr") is not None:
                        hp = hd["qr"][2]
                        krl = self.ks[s][hp:hp + 64, 1, kb * 128:(kb + 1) * 128]
                    block(self.ks[s][:, 0, kb * 128:(kb + 1) * 128], [("ks", s)], krl,
                          lambda c, s=s, kb=kb: self.vs[s][:, kb, c * 128:(c + 1) * 128], ("vs", s), 128, 0, False)
                    if kb == 0:
                        issue(k)
            kown, kownkey = hd["kown"]
            for kb in range(ntb):
                nk = min(128, n - kb * 128)
                krl = None
                if hd.get("qr") is not None:
                    hp = hd["qr"][2]
                    krl = self.krT2[hp:hp + 64, kb * 128:kb * 128 + nk]
                block(kown[:, kb * 128:kb * 128 + nk], [kownkey] + ([("krT2",)] if krl is not None else []), krl,
                      lambda c, kb=kb, nk=nk, hd=hd: self.vv(kb)[:nk, hd["vc0"] + c * 128: hd["vc0"] + (c + 1) * 128], ("v", kb), nk, kb * 128, True)
        self.pipe_flush()

    def attn_diff(self, l):
        P, n = self.P, self.n
        heads = []
        for h in range(6):
            for m in range(2):
                j = 2 * h + m

                def fin(oacc, rec, h=h, m=m):
                    for c in range(2):
                        P.tt(self.dtmv(2 * m + c)[:, :n], oacc[c][0][:, :n], rec[0][:, :n], ALU.mult, [oacc[c][1], rec[1]], [("dtm", 2 * m + c)])
                    if m == 1:
                        for c in range(2):
                            P.stt(self.dtmv(c)[:, :n], self.dtmv(2 + c)[:, :n], self.sm[:, 1:2], self.dtmv(c)[:, :n], ALU.mult, ALU.add,
                                  [("dtm", 2 + c), ("dtm", c), ("sm",)], [("dtm", c)])
                        pb, pk = self.bank(0, 2)
                        for c in range(2):
                            t = self.rot("tm", 4)
                            P.act(self.tm[t][:, :n], self.dtmv(c)[:, :n], AF.Square, [("dtm", c)], [("tm", t)])
                            P.mm(pb[:, :n], self.cf[:, C_ONES:C_ONES + 128], self.tm[t][:, :n], c == 0, c == 1, [("cf",), ("tm", t)], [pk])
                        t = self.rot("tm", 4)
                        P.act(self.tm[t][:, :n], pb[:, :n], AF.Sqrt, [pk, ("sm",)], [("tm", t)], bias=self.sm[:, 0:1], scale=1.0 / 256)
                        P.op("dve", lambda g_: g_.reciprocal(self.tm[t][:, :n], self.tm[t][:, :n]), [("tm", t)], [("tm", t)])
                        for c in range(2):
                            P.tt(self.dtmv(c)[:, :n], self.dtmv(c)[:, :n], self.tm[t][:, :n], ALU.mult, [("dtm", c), ("tm", t)], [("dtm", c)])
                            P.ts(self.oT[:, 2 * h + c, :n], self.dtmv(c)[:, :n], self.sm[:, 2 + c:3 + c], None, ALU.mult, None,
                                 [("dtm", c), ("sm",)], [("oT", 2 * h + c)], e="pool")

                heads.append(dict(q=(self.qT(j), ("qT", j)), kj=j, kown=(self.kT(j), ("kT", j)), vc0=256 * h, dv=256, fin=fin))
        self.softmax_heads(l, heads, 128 ** -0.5, True)

    def attn_mem(self, l):
        P, n = self.P, self.n
        ones_b = self.cb[:, C_ONES:C_ONES + 128]
        self.pipe_pending = None
        for h in range(4):
            acc = self.accset()
            oacc, sacc = acc[0], acc[2]
            for kb in range(2):
                hold = {}

                def s1(h=h, kb=kb, hold=hold):
                    sb_, sk = self.bank(0, 2)
                    P.mm(sb_[:, :n], self.mk[:, h, kb * 128:(kb + 1) * 128], self.qm(h)[:, :n], True, True, [("mk",), ("qm", h)], [sk])
                    p = self.rot("pt", 3)
                    P.act(self.pt[p][:, :n], sb_[:, :n], AF.Exp, [sk], [("pt", p)], scale=128 ** -0.5)
                    hold["p"] = p

                def s2(h=h, kb=kb, hold=hold, oacc=oacc, sacc=sacc):
                    p = hold["p"]
                    P.mm(oacc[0][:, :n], self.mv[:, kb, h * 128:(h + 1) * 128], self.pt[p][:, :n], kb == 0, kb == 1, [("mv",), ("pt", p)], [oacc[1]])
                    P.mm(sacc[0][:, :n], ones_b, self.pt[p][:, :n], kb == 0, kb == 1, [("cb",), ("pt", p)], [sacc[1]])
                    if kb == 1:
                        t = self.rot("tm", 4)
                        P.op("dve", lambda g_: g_.reciprocal(self.tm[t][:, :n], sacc[0][:, :n]), [sacc[1]], [("tm", t)])
                        P.tt(self.oT[:, 12 + h, :n], oacc[0][:, :n], self.tm[t][:, :n], ALU.mult, [oacc[1], ("tm", t)], [("oT", 12 + h)])

                self.pipe_push(s1, s2)
        self.pipe_flush()

    def attn_band(self, l):
        P, n, i = self.P, self.n, self.i
        ones_b = self.cb[:, C_ONES:C_ONES + 128]
        has_prev = self.samp or i > 0
        npb = (n + 127) // 128
        self.pipe_pending = None
        for h in range(12):
            ebf = self.ebf
            ekey = ("ebf", 0)
            P.dma("sp", ebf[:], self.I["bbias"][:, h], writes=[ekey])
            P.act(ebf[:], ebf[:], AF.Exp, [ekey], [ekey])
            s = None
            if has_prev:
                s = self.rot("kvs", 2)
                if self.samp:
                    KT, V, g0, kk = self.S["sKT1"], self.S["sV1"], 0, [("sKT1",), ("sV1",)]
                else:
                    KT, V, g0, kk = self.S["KT1"], self.S["V1"], (i - 1) * 512, [("KTs", 1, i - 1), ("Vs", 1, i - 1)]
                P.dma("sp", self.ks[s][:, 0, :], KT[h, :, g0:g0 + 512], reads=[kk[0]], writes=[("ks", s)])
                P.dma("sp", self.vs[s][:, :, 0:128], V.ap()[g0:g0 + 512, h * 128:(h + 1) * 128].rearrange("(kb p) c -> p kb c", p=128),
                      reads=[kk[1]], writes=[("vs", s)])
            acc = self.accset()
            oacc, sacc = acc[0], acc[2]
            rs = [4, 3, 2, 1, 0]
            valid_r = [r for r in rs if has_prev or r == 4 or (npb - 1) - 4 + r >= 0]
            for ri, r in enumerate(valid_r):
                pbs = [pb_ for pb_ in range(npb) if (pb_ - 4 + r >= 0) or has_prev]
                hold = {}

                def s1(h=h, r=r, pbs=pbs, s=s, ebf=ebf, ekey=ekey, hold=hold):
                    pb0 = pbs[0]
                    sb_, sk = self.bank(0, 2)
                    nk = 128
                    for pb_ in pbs:
                        lb = pb_ - 4 + r
                        nq = min(128, n - pb_ * 128)
                        if lb < 0:
                            klhs, kkey, nk = self.ks[s][:, 0, (4 + lb) * 128:(5 + lb) * 128], ("ks", s), 128
                        else:
                            nk = min(128, n - lb * 128)
                            klhs, kkey = self.kT(h)[:, lb * 128:lb * 128 + nk], ("kT", h)
                        P.mm(sb_[:nk, pb_ * 128:pb_ * 128 + nq], klhs, self.qT(h)[:, pb_ * 128:pb_ * 128 + nq], True, True, [kkey, ("qT", h)], [sk])
                    c0, c1 = pb0 * 128, n
                    p = self.rot("pt", 3)
                    pt = self.pt[p]
                    P.act(pt[:nk, c0:c1], sb_[:nk, c0:c1], AF.Exp, [sk], [("pt", p)], scale=128 ** -0.5)
                    nqq = min(128, n)
                    ptv = pt[:nk, c0:c1].rearrange("p (a b) -> p a b", b=nqq)
                    ebv = ebf[:nk, r:r + 1, 0:nqq].broadcast_to([nk, len(pbs), nqq])
                    P.tt(ptv, ptv, ebv, ALU.mult, [("pt", p), ekey], [("pt", p)])
                    hold["p"], hold["nk"] = p, nk

                def s2(h=h, r=r, ri=ri, pbs=pbs, s=s, hold=hold, oacc=oacc, sacc=sacc, nvr=len(valid_r)):
                    p, nk_all = hold["p"], hold["nk"]
                    pt = self.pt[p]
                    last = (ri == nvr - 1)
                    c0, c1 = pbs[0] * 128, n
                    for pb_ in pbs:
                        lb = pb_ - 4 + r
                        nq = min(128, n - pb_ * 128)
                        if lb < 0:
                            vl, vkey, nk = self.vs[s][:, 4 + lb, 0:128], ("vs", s), 128
                        else:
                            nk = min(128, n - lb * 128)
                            vl, vkey = self.vv(lb)[:nk, h * 128:(h + 1) * 128], ("v", lb)
                        P.mm(oacc[0][:, pb_ * 128:pb_ * 128 + nq], vl, pt[:nk, pb_ * 128:pb_ * 128 + nq], ri == 0 and pb_ == pbs[0],
                             last and pb_ == pbs[-1], [vkey, ("pt", p)], [oacc[1]])
                    P.mm(sacc[0][:, c0:c1], ones_b[:nk_all, :], pt[:nk_all, c0:c1], ri == 0, last, [("cb",), ("pt", p)], [sacc[1]])
                    if last:
                        t = self.rot("tm", 4)
                        P.op("dve", lambda g_: g_.reciprocal(self.tm[t][:, :n], sacc[0][:, :n]), [sacc[1]], [("tm", t)])
                        P.tt(self.oT[:, h, :n], oacc[0][:, :n], self.tm[t][:, :n], ALU.mult, [oacc[1], ("tm", t)], [("oT", h)])

                self.pipe_push(s1, s2)
        self.pipe_flush()

    def attn_stick(self, l):
        P, n, i = self.P, self.n, self.i
        groups = self.kv_groups(l)
        rgroups = list(reversed(groups))
        items = [(h, grp) for h in range(12) for grp in rgroups]
        slots = {}

        def issue(k):
            if k < len(items) and k not in slots:
                h, grp = items[k]
                slots[k] = self.load_group(grp, h, h * 128, 128)

        k = 0
        scale = 128 ** -0.5
        onesf = self.cf[:, C_ONES:C_ONES + 128]
        ntb = (n + 127) // 128
        pend = []

        def push(s1, s2, s3):
            s1()
            if len(pend) >= 1:
                pend[-1][0]()
            if len(pend) >= 2:
                pend[-2][1]()
                pend.pop(0)
            pend.append([s2, s3])

        def flush():
            if len(pend) == 2:
                pend[1][0]()
                pend[0][1]()
                pend[1][1]()
            elif len(pend) == 1:
                pend[0][0]()
                pend[0][1]()
            del pend[:]

        for h in range(12):
            ob = 5 + self.rot("stick_o", 2)
            oacc = (self.ps[:, ob, :], ("ps", ob))
            ebv_ = self.ebf[:, :, :].rearrange("p a b -> p (a b)")
            spb = [(self.hb[0], ("hb", 0)), (self.hb[1], ("hb", 1)), (ebv_, ("ebf", 0))]
            for b_, bk_ in spb:
                P.op("pool", lambda g_, b_=b_: g_.memset(b_[:, :n], 0.0), [], [bk_])
            state = dict(bi=0, cur=0)
            nblocks = 4 * len(groups) + ntb

            def block(klhs, kkey, vl, vlkey, nk, q0, diag, h=h, oacc=oacc, state=state, nblocks=nblocks, spb=spb):
                bi = state["bi"]
                state["bi"] += 1
                first, last = (bi == 0), (bi == nblocks - 1)
                vcur, vkey = spb[bi % 3]
                vnew, vnkey = spb[(bi + 1) % 3]
                hold = {}

                def s1():
                    sb_, sk = self.bank(0, 3)
                    P.mm(sb_[:nk, q0:n], klhs, self.qT(h)[:, q0:n], True, True, [kkey, ("qT", h)], [sk])
                    te = self.rot("tm", 4)
                    P.act(self.tm[te][:nk, q0:n], sb_[:nk, q0:n], AF.Exp, [sk], [("tm", te)], scale=scale)
                    tsp = self.rot("st", 3)
                    sp = self.st[tsp]
                    P.act(sp[:nk, q0:n], self.tm[te][:nk, q0:n], AF.Ln, [("tm", te)], [("st", tsp)], bias=1.0)
                    if diag:
                        w = min(128, n - q0)
                        P.tt(sp[:nk, q0:q0 + w], sp[:nk, q0:q0 + w], self.cf[:nk, C_MSC:C_MSC + w], ALU.mult, [("st", tsp), ("cf",)], [("st", tsp)])
                    tz = self.rot("tm", 4)
                    P.stt(self.tm[tz][:nk, q0:n], sb_[:nk, q0:n], scale, sp[:nk, q0:n], ALU.mult, ALU.subtract, [sk, ("st", tsp)], [("tm", tz)])
                    P.tt(vnew[:nk, q0:n], vcur[:nk, q0:n], sp[:nk, q0:n], ALU.add, [vkey, ("st", tsp)], [vnkey], e="pool")
                    hold["tsp"], hold["tz"] = tsp, tz

                def s2():
                    tsp, tz = hold["tsp"], hold["tz"]
                    sp, arg = self.st[tsp], self.tm[tz]
                    tb_, tk = self.bank(3, 5)
                    P.mm(tb_[:nk, q0:n], self.cf[:nk, C_U:C_U + nk], sp[:nk, q0:n], True, first, [("cf",), ("st", tsp)], [tk])
                    if not first:
                        P.mm(tb_[:nk, q0:n], onesf[:, :nk], vcur[:, q0:n], False, True, [("cf",), vkey], [tk])
                    P.tt(arg[:nk, q0:n], arg[:nk, q0:n], tb_[:nk, q0:n], ALU.subtract, [("tm", tz), tk], [("tm", tz)])
                    p = self.rot("pt", 3)
                    pt = self.pt[p]
                    P.act(pt[:nk, q0:n], arg[:nk, q0:n], AF.Exp, [("tm", tz)], [("pt", p)])
                    if diag:
                        w = min(128, n - q0)
                        P.tt(pt[:nk, q0:q0 + w], pt[:nk, q0:q0 + w], self.cb[:nk, C_MSC:C_MSC + w], ALU.mult, [("pt", p), ("cb",)], [("pt", p)])
                    hold["p"] = p

                def s3():
                    p = hold["p"]
                    P.mm(oacc[0][:, q0:n], vl, self.pt[p][:nk, q0:n], first, last, [vlkey, ("pt", p)], [oacc[1]])
                    if last:
                        P.copy(self.oT[:, h, :n], oacc[0][:, :n], [oacc[1]], [("oT", h)], e="act")

                push(s1, s2, s3)

            issue(k)
            for kb in reversed(range(ntb)):
                nk = min(128, n - kb * 128)
                block(self.kT(h)[:, kb * 128:kb * 128 + nk], ("kT", h), self.vv(kb)[:nk, h * 128:(h + 1) * 128], ("v", kb), nk, kb * 128, True)
            for grp in rgroups:
                issue(k)
                s = slots[k]
                k += 1
                for bi_, kb in enumerate(reversed(range(4))):
                    block(self.ks[s][:, 0, kb * 128:(kb + 1) * 128], ("ks", s), self.vs[s][:, kb, 0:128], ("vs", s), 128, 0, False)
                    if bi_ == 1:
                        issue(k)
            flush()

    def rms_fm(self, src, srckey, nch, dim, gofs, outs):
        P, n = self.P, self.n
        pb, pk = self.bank(6, 8)
        for c in range(nch):
            t = self.rot("tm", 4)
            P.act(self.tm[t][:, :n], src(c)[:, :n], AF.Square, [(srckey, c)], [("tm", t)])
            P.mm(pb[:, :n], self.cf[:, C_ONES:C_ONES + 128], self.tm[t][:, :n], c == 0, c == nch - 1, [("cf",), ("tm", t)], [pk])
        tr = self.rot("hb", 2)
        rs = self.hb[tr]
        P.act(rs[:, :n], pb[:, :n], AF.Sqrt, [pk, ("sm",)], [("hb", tr)], bias=self.sm[:, 0:1], scale=1.0 / dim)
        P.op("dve", lambda g_: g_.reciprocal(rs[:, :n], rs[:, :n]), [("hb", tr)], [("hb", tr)])
        for c in range(nch):
            t = self.rot("tm", 4)
            P.tt(self.tm[t][:, :n], src(c)[:, :n], rs[:, :n], ALU.mult, [(srckey, c), ("hb", tr)], [("tm", t)])
            for (fn, key) in outs:
                P.ts(fn(c)[:, :n], self.tm[t][:, :n], self.vecs[:, gofs + c:gofs + c + 1], None, ALU.mult, None, [("tm", t), ("vecs",)], [(key, c)], e="pool")

    def mla_proj(self, l):
        P, n, t0, i = self.P, self.n, self.tok0, self.i
        xsrc = lambda kc: self.xb[:, kc, :n]
        regf = self.reg[:, 0:20 * NT].bitcast(F32)
        cqf = lambda c: regf[:, c * NT:(c + 1) * NT]
        lf = lambda c: regf[:, (6 + c) * NT:(7 + c) * NT]
        cqn = lambda c: self.oT[:, c, :]
        latT = lambda c: self.oT[:, 6 + c, :]
        self.krT2 = self.reg[:, 43 * NT:44 * NT]
        P.dma("sp", self.tab[:, :, :n], self.I["tabC"][:, :, t0:t0 + n], writes=[("tab",)])
        W, wk = self.wnext(); W3 = W[:, :].rearrange("p (k c) -> p k c", c=512)
        for cj in range(4):
            pb, pk = self.proj_fm(W3, wk, cj, 16, xsrc, "xb", n)
            P.copy(cqf(cj)[:, :n], pb[:, :n], [pk], [("cqf", cj)], e="act")
        W, wk = self.wnext(); W3 = W[:, :].rearrange("p (k c) -> p k c", c=512)
        for cj in range(2):
            pb, pk = self.proj_fm(W3, wk, cj, 16, xsrc, "xb", n)
            P.copy(cqf(4 + cj)[:, :n], pb[:, :n], [pk], [("cqf", 4 + cj)], e="act")
        pb, pk = self.bank(0, 4)
        for kc in range(16):
            P.mm(pb[0:64, :n], W3[:, kc, 256:320], xsrc(kc), kc == 0, kc == 15, [wk, ("xb", kc)], [pk])
        s = self.rot("st", 3)
        P.copy(self.st[s][0:64, :n], pb[0:64, :n], [pk], [("st", s)], e="act")
        pb2, pk2 = self.bank(4, 6)
        P.mm(pb2[:, :n], self.cf[0:64, C_DUP:C_DUP + 128], self.st[s][0:64, :n], True, True, [("cf",), ("st", s)], [pk2])
        s2 = self.rot("st", 3)
        P.copy(self.st[s2][:, :n], pb2[:, :n], [pk2], [("st", s2)], e="act")
        self.rope(s2, 128, C_PERMC, ("tab",))
        P.dma("pool", self.O["krT"][:, t0:t0 + n], self.st[s2][0:64, :n], reads=[("st", s2)], writes=[("o_kr", i)])
        P.copy(self.krT2[:, :n], self.st[s2][:, :n], [("st", s2)], [("krT2",)], e="pool")
        if not self.samp:
            P.dma("pool", self.S["KR"][:, t0:t0 + n], self.krT2[:, :n], reads=[("krT2",)], writes=[("KRs", i)])
        W, wk = self.wnext(); W3 = W[:, :].rearrange("p (k c) -> p k c", c=512)
        for cj in range(4):
            pb, pk = self.proj_fm(W3, wk, cj, 16, xsrc, "xb", n)
            P.copy(lf(cj)[:, :n], pb[:, :n], [pk], [("lf", cj)], e="act")
        W, wk = self.wnext(); W3 = W[:, :].rearrange("p (k c) -> p k c", c=512)
        for cj in range(4):
            pb, pk = self.proj_fm(W3, wk, cj, 16, xsrc, "xb", n)
            P.copy(self.qm(cj)[:, :n], pb[:, :n], [pk], [("qm", cj)], e="act")
        self.rms_fm(cqf, "cqf", 6, 768, V_QG, [(cqn, "cqn")])
        self.rms_fm(lf, "lf", 4, 512, V_KG, [(lf, "lf2"), (latT, "latT")])
        for c in range(4):
            P.dma("pool", self.O["latT"][c * 128:(c + 1) * 128, t0:t0 + n], lf(c)[:, :n], reads=[("lf2", c)], writes=[("o_lat", c, i)])
        P.barrier()
        csrc = lambda kc: cqn(kc)[:, :n]
        for jt in range(5):
            W, wk = self.wnext(); W3 = W[:, :].rearrange("p (k c) -> p k c", c=512)
            for cj in range(4):
                jglob = jt * 4 + cj
                if jglob >= 18:
                    break
                pb, pk = self.proj_fm(W3, wk, cj, 6, csrc, "cqn", n)
                if jglob < 12:
                    P.copy(self.qT(jglob)[:, :n], pb[:, :n], [pk], [("qT", jglob)], e="act")
                else:
                    jj = jglob - 12
                    s = self.rot("st", 3)
                    P.copy(self.st[s][:, :n], pb[:, :n], [pk], [("st", s)], e="act")
                    self.rope(s, 128, C_PERMC, ("tab",))
                    P.copy(self.qrb(jj)[:, :n], self.st[s][:, :n], [("st", s)], [("qrb", jj)], e="pool")
        sets = [("past", 0), ("past", 1)] if self.samp else []
        sets.append(("own", None))
        vsf = [self.vs[k_][:, :, :].rearrange("p a b -> p (a b)") for k_ in range(2)]
        plat = lambda kc: vsf[kc // 2][:, (kc % 2) * 512:(kc % 2 + 1) * 512]
        for jt in range(6):
            W, wk = self.wnext(); W3 = W[:, :].rearrange("p (k c) -> p k c", c=512)
            for kind, g in sets:
                if kind == "past":
                    for kc in range(4):
                        P.dma("sp", plat(kc), self.S["sLat"][kc, :, g * 512:(g + 1) * 512], reads=[("sLat",)], writes=[("vs", kc // 2)])
                    src = lambda kc: plat(kc)
                    rkey = lambda kc: ("vs", kc // 2)
                    nn = 512
                else:
                    src = lambda kc: latT(kc)[:, :n]
                    rkey = lambda kc: ("latT", kc)
                    nn = n
                if jt < 3:
                    for cj in range(4):
                        j = jt * 4 + cj
                        pb, pk = self.bank(0, 4)
                        for kc in range(4):
                            P.mm(pb[:, :nn], W3[:, kc, cj * 128:(cj + 1) * 128], src(kc), kc == 0, kc == 3, [wk, rkey(kc)], [pk])
                        if kind == "own":
                            P.copy(self.kT(j)[:, :n], pb[:, :n], [pk], [("kT", j)], e="act")
                        else:
                            p = self.rot("pt", 3)
                            P.copy(self.pt[p][:, :], pb[:, :], [pk], [("pt", p)], e="act")
                            P.dma("pool", self.S["sKT2"][j, :, g * 512:(g + 1) * 512], self.pt[p][:, :], reads=[("pt", p)], writes=[("sKT2",)])
                else:
                    c0 = (jt - 3) * 512
                    for tb in range((nn + 127) // 128):
                        np_ = min(128, nn - tb * 128)
                        pb, pk = self.bank(0, 4)
                        for kc in range(4):
                            P.mm(pb[:np_, :], src(kc)[:, tb * 128:tb * 128 + np_], W3[:, kc, :], kc == 0, kc == 3, [wk, rkey(kc)], [pk])
                        if kind == "own":
                            P.copy(self.vv(tb)[:np_, c0:c0 + 512], pb[:np_, :], [pk], [("v", tb)], e="act")
                        else:
                            p = self.rot("pt", 3)
                            P.copy(self.pt[p][:, :], pb[:, :], [pk], [("pt", p)], e="act")
                            r0 = g * 512 + tb * 128
                            P.dma("pool", self.S["sV2"][r0:r0 + 128, c0:c0 + 512], self.pt[p][:, :], reads=[("pt", p)], writes=[("sV2",)])
        if not self.samp:
            P.dma("pool", self.S["KT2"].ap().rearrange("j p t -> p j t")[:, :, t0:t0 + n],
                  self.reg[:, 12 * NT:24 * NT].rearrange("p (j t) -> p j t", t=NT), reads=R("kT", range(12)), writes=[("KTs", 2, i)])
            P.dma("pool", self.S["V2"].ap()[t0:t0 + n, :].rearrange("(tb p) c -> p tb c", p=128),
                  self.reg[:, 24 * NT:24 * NT + 4 * 1536].rearrange("p (tb c) -> p tb c", c=1536), reads=R("v", range(4)), writes=[("Vs", 2, i)])

    def attn_mla(self, l):
        P, n = self.P, self.n
        P.barrier()
        heads = []
        for h in range(12):
            def fin(oacc, rec, h=h):
                P.tt(self.oT[:, h, :n], oacc[0][0][:, :n], rec[0][:, :n], ALU.mult, [oacc[0][1], rec[1]], [("oT", h)])
            heads.append(dict(q=(self.qT(h), ("qT", h)), qr=(self.qrb(h // 2), ("qrb", h // 2), 64 * (h % 2)), kj=h,
                              kown=(self.kT(h), ("kT", h)), vc0=128 * h, dv=128, fin=fin))
        self.softmax_heads(l, heads, 192 ** -0.5, True)

    def layer_norm(self, l, which):
        P, n = self.P, self.n
        go = V_LN + ((2 * which) * 4 + l) * 16
        bo = V_LN + ((2 * which + 1) * 4 + l) * 16
        onesf = self.cf[:, C_ONES:C_ONES + 128]
        onesb = self.cb[:, C_ONES:C_ONES + 128]
        pa, pak = self.ps[:, 6, :], ("ps", 6)
        pq, pqk = self.ps[:, 7, :], ("ps", 7)
        for c in range(16):
            P.act(self.oT[:, c, :n], self.xf[:, c, :n], AF.Square, [("xf", c)], [("oT", c)])
        for c in range(16):
            P.mm(pa[:, :n], onesf, self.xf[:, c, :n], c == 0, c == 15, [("cf",), ("xf", c)], [pak])
        for c in range(16):
            P.mm(pq[:, :n], onesb, self.oT[:, c, :n], c == 0, c == 15, [("cb",), ("oT", c)], [pqk])
        mu, rstd = self.hb[0], self.hb[1]
        P.act(mu[:, :n], pa[:, :n], AF.Identity, [pak], [("hb", 0)], scale=1.0 / D)
        t = self.rot("tm", 4)
        P.tt(self.tm[t][:, :n], mu[:, :n], mu[:, :n], ALU.mult, [("hb", 0)], [("tm", t)])
        P.stt(rstd[:, :n], pq[:, :n], 1.0 / D, self.tm[t][:, :n], ALU.mult, ALU.subtract, [pqk, ("tm", t)], [("hb", 1)])
        P.act(rstd[:, :n], rstd[:, :n], AF.Sqrt, [("hb", 1), ("sm",)], [("hb", 1)], bias=self.sm[:, 0:1])
        P.op("dve", lambda g_: g_.reciprocal(rstd[:, :n], rstd[:, :n]), [("hb", 1)], [("hb", 1)])
        P.stt(mu[:, :n], mu[:, :n], -1.0, rstd[:, :n], ALU.mult, ALU.mult, [("hb", 0), ("hb", 1)], [("hb", 0)])
        for c in range(16):
            t = self.rot("tm", 4)
            e_ = "dve" if c % 3 != 2 else "pool"
            P.tt(self.tm[t][:, :n], self.xf[:, c, :n], rstd[:, :n], ALU.mult, [("xf", c), ("hb", 1)], [("tm", t)], e=e_)
            P.tt(self.tm[t][:, :n], self.tm[t][:, :n], mu[:, :n], ALU.add, [("tm", t), ("hb", 0)], [("tm", t)], e=e_)
            P.act(self.xf[:, c, :n], self.tm[t][:, :n], AF.Identity, [("tm", t), ("vecs",)], [("xf", c)],
                  bias=self.vecs[:, bo + c:bo + c + 1], scale=self.vecs[:, go + c:go + c + 1])
            P.act(self.xb[:, c, :n], self.tm[t][:, :n], AF.Identity, [("tm", t), ("vecs",)], [("xb", c)],
                  bias=self.vecs[:, bo + c:bo + c + 1], scale=self.vecs[:, go + c:go + c + 1])

    def out_proj_ffn(self, l):
        P, n, i = self.P, self.n, self.i
        for jt in range(4):
            W, wk = self.wnext(); W3 = W[:, :].rearrange("p (k c) -> p k c", c=512)
            for cj in range(4):
                oc = jt * 4 + cj
                pb, pk = self.bank(0, 4)
                for kc in range(16):
                    P.mm(pb[:, :n], W3[:, kc, cj * 128:(cj + 1) * 128], self.oT[:, kc, :n], kc == 0, kc == 15, [wk, ("oT", kc)], [pk])
                P.stt(self.xf[:, oc, :n], self.xf[:, oc, :n], ALPHA, pb[:, :n], ALU.mult, ALU.add, [("xf", oc), pk], [("xf", oc)])
        self.layer_norm(l, 0)
        P.barrier()
        cst = self.cst
        ckey = ("cst",)
        v = self.vecs
        for t in range(22):
            W, wk = self.wnext(); W3 = W[:, :].rearrange("p (k c) -> p k c", c=512)
            hbufs = {}
            for cj in range(4):
                g = 2 * t + (cj % 2)
                ch = g if cj < 2 else 44 + g
                pb, pk = self.bank(0, 6)
                for kc in range(16):
                    P.mm(pb[:, :n], W3[:, kc, cj * 128:(cj + 1) * 128], self.xb[:, kc, :n], kc == 0, kc == 15, [wk, ("xb", kc)], [pk])
                u = self.rot("ue", 2)
                ue = self.ue[u]
                P.copy(ue[:, 0:2], cst[:, l, ch, :], [ckey], [("ue", u)], e="pool")
                P.copy(ue[:, 2:2 + n], pb[:, :n], [pk, ("ue", u)], [("ue", u)], e="act")
                P.copy(cst[:, l, ch, :], ue[:, n:n + 2], [("ue", u)], [ckey], e="pool")
                hbk = self.rot("hbc", 4)
                hb = self.hcv[hbk]
                cw = lambda j_: v[:, V_CW + (l * 3 + j_) * 88 + ch: V_CW + (l * 3 + j_) * 88 + ch + 1]
                cbias = v[:, V_CB + l * 88 + ch: V_CB + l * 88 + ch + 1]
                P.act(hb[:, :n], pb[:, :n], AF.Identity, [pk, ("vecs",)], [("hcv", hbk)], bias=cbias, scale=cw(2))
                P.stt(hb[:, :n], ue[:, 1:1 + n], cw(1), hb[:, :n], ALU.mult, ALU.add, [("ue", u), ("vecs",), ("hcv", hbk)], [("hcv", hbk)])
                P.stt(hb[:, :n], ue[:, 0:n], cw(0), hb[:, :n], ALU.mult, ALU.add, [("ue", u), ("vecs",), ("hcv", hbk)], [("hcv", hbk)])
                hbufs[cj] = hbk
            for gg in range(2):
                g = 2 * t + gg
                a, b_ = hbufs[gg], hbufs[2 + gg]
                P.act(self.hcv[a][:, :n], self.hcv[a][:, :n], AF.Silu, [("hcv", a)], [("hcv", a)])
                P.tt(self.hT(g)[:, :n], self.hcv[a][:, :n], self.hcv[b_][:, :n], ALU.mult, [("hcv", a), ("hcv", b_)], [("hT", g)], e="pool")
        for oc in range(16):
            W, wk = self.wnext(); Wd = W[:, 0:44 * 128].rearrange("p (k c) -> p k c", c=128)
            pb, pk = self.bank(0, 6)
            for fc in range(44):
                P.mm(pb[:, :n], Wd[:, fc, :], self.hT(fc)[:, :n], fc == 0, fc == 43, [wk, ("hT", fc)], [pk])
            P.stt(self.xf[:, oc, :n], self.xf[:, oc, :n], ALPHA, pb[:, :n], ALU.mult, ALU.add, [("xf", oc), pk], [("xf", oc)])
        self.layer_norm(l, 1)


def run(inputs, cfg):
    k = Kern(cfg)
    nc = k.build()
    wt, nels, pro_seq, layer_seq = build_weight_tiles(inputs)
    shared = build_shared(inputs)
    in_maps = []
    for c in range(8):
        d = dict(shared)
        for gi, grp in enumerate([pro_seq] + layer_seq):
            d["wt%d" % gi] = wt[grp[0]:grp[-1] + 1] if gi <= k.NL else wt[grp[0]:grp[0] + 1]
        d.update(build_core_inputs(inputs, c))
        in_maps.append(d)
    import time as _t
    t0_ = _t.time()
    res = run_bass_kernel_spmd(nc, in_maps, core_ids=list(range(8)))
    print("spmd run seconds", _t.time() - t0_, flush=True)
    return res.results


def assemble(R_):
    f = np.float32
    def P2(fn):
        return np.stack([fn(R_[b]) for b in range(2)])
    def S8(fn):
        return np.stack([fn(R_[c]) for c in range(8)])
    y_p = P2(lambda r: r["yT"][:, :TP].T)
    y_s = S8(lambda r: r["yT"][:, TP:].T)
    def kfm(r, name, sl, H, dd):
        a = r[name][:, sl].T
        return a.reshape(a.shape[0], H, dd)
    pa, sa = slice(0, TP), slice(TP, TOT)
    outs = [y_p, y_s]
    outs.append(P2(lambda r: kfm(r, "kT0", pa, 6, 256))[None])
    outs.append(P2(lambda r: r["v0"][pa].reshape(TP, 6, 256))[None])
    outs.append(P2(lambda r: kfm(r, "kT1", slice(TP - 512, TP), 12, 128))[None])
    outs.append(P2(lambda r: r["v1"][TP - 512:TP].reshape(512, 12, 128))[None])
    outs.append(P2(lambda r: r["latT"][:, pa].T)[None])
    outs.append(P2(lambda r: r["krT"][:, pa].T)[None])
    outs.append(P2(lambda r: kfm(r, "kT3", pa, 12, 128))[None])
    outs.append(P2(lambda r: r["v3"][pa].reshape(TP, 12, 128))[None])
    outs.append(np.stack([np.stack([R_[b]["memkT"][l].T.reshape(256, 4, 128) for b in range(2)]) for l in range(4)]))
    outs.append(np.stack([np.stack([R_[b]["memv"][l].reshape(256, 4, 128) for b in range(2)]) for l in range(4)]))
    def conv(a):
        return a.transpose(1, 3, 2, 0).reshape(4, 2, 88 * 128)
    outs.append(np.stack([conv(R_[b]["conv_p"]) for b in range(2)], axis=1))
    outs.append(S8(lambda r: kfm(r, "kT0", sa, 6, 256))[None])
    outs.append(S8(lambda r: r["v0"][sa].reshape(NS, 6, 256))[None])
    outs.append(S8(lambda r: r["bkT_s"].T.reshape(512, 12, 128))[None])
    outs.append(S8(lambda r: r["bv_s"].reshape(512, 12, 128))[None])
    outs.append(S8(lambda r: r["latT"][:, sa].T)[None])
    outs.append(S8(lambda r: r["krT"][:, sa].T)[None])
    outs.append(S8(lambda r: kfm(r, "kT3", sa, 12, 128))[None])
    outs.append(S8(lambda r: r["v3"][sa].reshape(NS, 12, 128))[None])
    outs.append(np.stack([conv(R_[c]["conv_s"]) for c in range(8)], axis=1))
    return tuple(np.ascontiguousarray(o.astype(f)) for o in outs)


def kernel(**inputs):
    inputs = {k: np.asarray(v) for k, v in inputs.items()}
    R_ = run(inputs, {})
    return assemble(R_)
```

```python
import contextlib
import math
import numpy as np
import concourse.bass as bass
import concourse.mybir as mybir
from concourse.bass_utils import run_bass_kernel_spmd

F32 = mybir.dt.float32
BF16 = mybir.dt.bfloat16
AF = mybir.ActivationFunctionType
ALU = mybir.AluOpType

D = 2048
NT = 512
NPT = 8
TP = 4096
NS = 64
TOT = TP + NS
PAST = 1024
DFF = 5632
ALPHA = 8 ** 0.25
EPS = 1e-5
WEL = 8192
SAME_ENG_SYNC = True
NDS = 40
MASKV = -1.0e4

V_LN = 0
V_CW = V_LN + 4 * 4 * 16
V_CB = V_CW + 4 * 3 * 88
V_DG = V_CB + 4 * 88
V_QG = V_DG + 2
V_KG = V_QG + 6
V_LAM = V_KG + 4
NV = V_LAM + 4
C_ONES = 0
C_PERMA = 128
C_PERMC = 160
C_U = 288
C_MCC = 416
C_MSC = 544
C_DUP = 672
NC = 800


def lam_init(i):
    return 0.8 - 0.6 * math.exp(-0.3 * i)


def _colblock(w, cols):
    kc = w.shape[0] // 128
    t = np.zeros((128, 16, 512), np.float32)
    t[:, :kc, :len(cols)] = w[:, cols].reshape(kc, 128, len(cols)).transpose(1, 0, 2)
    return t.reshape(128, WEL), kc * 512


def _downblock(w, oc):
    t = np.zeros((128, WEL), np.float32)
    t[:, :44 * 128] = w[:, oc * 128:(oc + 1) * 128].reshape(44, 128, 128).transpose(1, 0, 2).reshape(128, 44 * 128)
    return t, 44 * 128


def build_weight_tiles(inp):
    tiles, nels = [], []
    layer_seq = [[] for _ in range(4)]
    pro_seq = []

    def add(t, lst):
        lst.append(len(tiles))
        tiles.append(t[0])
        nels.append(t[1])

    for l in range(4):
        wm = inp["w_mem_kv"][l]
        add(_colblock(wm, np.arange(0, 512)), pro_seq)
        add(_colblock(wm, np.arange(512, 1024)), pro_seq)
    w_in = [inp["w_in_a"][0], inp["w_in_b"][0], inp["w_in_c"][0], inp["w_in_d"][0]]
    for l in range(4):
        seq = layer_seq[l]
        w = w_in[l]
        if l != 2:
            for j in range(10):
                add(_colblock(w, np.arange(j * 512, (j + 1) * 512)), seq)
        else:
            add(_colblock(w, np.arange(0, 512)), seq)
            add(_colblock(w, np.concatenate([np.arange(512, 768), np.arange(1280, 1344)])), seq)
            add(_colblock(w, np.arange(768, 1280)), seq)
            add(_colblock(w, np.arange(1344, 1856)), seq)
            wq = inp["mla_w_uq"][0]
            nope = np.concatenate([np.arange(h * 192, h * 192 + 128) for h in range(12)])
            rope = np.concatenate([np.arange(h * 192 + 128, h * 192 + 192) for h in range(12)])
            qcols = np.concatenate([nope, rope])
            for j in range(5):
                add(_colblock(wq, qcols[j * 512:(j + 1) * 512]), seq)
            wkv = inp["mla_w_ukv"][0]
            kn = np.concatenate([np.arange(h * 256, h * 256 + 128) for h in range(12)])
            vv = np.concatenate([np.arange(h * 256 + 128, h * 256 + 256) for h in range(12)])
            kvcols = np.concatenate([kn, vv])
            for j in range(6):
                add(_colblock(wkv, kvcols[j * 512:(j + 1) * 512]), seq)
        wo = inp["w_o"][l]
        for j in range(4):
            add(_colblock(wo, np.arange(j * 512, (j + 1) * 512)), seq)
        wu = inp["w_up"][l]
        for t in range(22):
            cols = np.concatenate([np.arange(2 * t * 128, (2 * t + 2) * 128), DFF + np.arange(2 * t * 128, (2 * t + 2) * 128)])
            add(_colblock(wu, cols), seq)
        wd = inp["w_down"][l]
        for oc in range(16):
            add(_downblock(wd, oc), seq)
    return np.stack(tiles), nels, pro_seq, layer_seq


def weight_tile_meta():
    nels, layer_seq, pro_seq = [], [[] for _ in range(4)], []

    def add(nel, lst):
        lst.append(len(nels))
        nels.append(nel)

    for l in range(4):
        add(WEL, pro_seq)
        add(WEL, pro_seq)
    for l in range(4):
        seq = layer_seq[l]
        if l != 2:
            for j in range(10):
                add(WEL, seq)
        else:
            for j in range(4):
                add(WEL, seq)
            for j in range(5):
                add(6 * 512, seq)
            for j in range(6):
                add(4 * 512, seq)
        for j in range(4):
            add(WEL, seq)
        for t in range(22):
            add(WEL, seq)
        for oc in range(16):
            add(44 * 128, seq)
    return nels, pro_seq, layer_seq


def fm(v):
    return np.ascontiguousarray(v.reshape(-1, 128).T)


def build_shared(inp):
    vecs = np.zeros((128, NV), np.float32)
    for k, name in enumerate(["ln1_g", "ln1_b", "ln2_g", "ln2_b"]):
        for l in range(4):
            o = V_LN + (k * 4 + l) * 16
            vecs[:, o:o + 16] = fm(inp[name][l])
    for l in range(4):
        for j in range(3):
            o = V_CW + (l * 3 + j) * 88
            vecs[:, o:o + 88] = fm(inp["conv_ffn_w"][l, j])
        o = V_CB + l * 88
        vecs[:, o:o + 88] = fm(inp["conv_ffn_b"][l])
    vecs[:, V_DG:V_DG + 2] = fm(inp["diff_norm_g"][0])
    vecs[:, V_QG:V_QG + 6] = fm(inp["mla_q_norm_g"][0])
    vecs[:, V_KG:V_KG + 4] = fm(inp["mla_kv_norm_g"][0])
    for k, name in enumerate(["diff_lambda_q1", "diff_lambda_k1", "diff_lambda_q2", "diff_lambda_k2"]):
        vecs[:, V_LAM + k] = inp[name][0]
    c = np.zeros((128, NC), np.float32)
    c[:, C_ONES:C_ONES + 128] = 1.0
    for p in range(32):
        c[p, C_PERMA + (p + 16) % 32] = 1.0
    for p in range(128):
        g = (p // 64) * 64
        c[p, C_PERMC + g + ((p - g) + 32) % 64] = 1.0
    j = np.arange(128)[:, None]
    k = np.arange(128)[None, :]
    c[:, C_U:C_U + 128] = (j > k)
    c[:, C_MCC:C_MCC + 128] = (j // 64 <= k // 64)
    c[:, C_MSC:C_MSC + 128] = (j < k)
    for p in range(64):
        c[p, C_DUP + p] = 1.0
        c[p, C_DUP + 64 + p] = 1.0
    pos = np.concatenate([np.arange(TP), PAST + np.arange(NS)]).astype(np.float32)
    tabA = np.zeros((32, 2, TOT), np.float32)
    invA = (500000.0 ** (-np.arange(16, dtype=np.float32) / 16)).astype(np.float32)
    angA = pos[None, :] * invA[:, None]
    tabA[:16, 0] = np.cos(angA); tabA[16:, 0] = np.cos(angA)
    tabA[:16, 1] = -np.sin(angA); tabA[16:, 1] = np.sin(angA)
    tabC = np.zeros((128, 2, TOT), np.float32)
    invC = (10000.0 ** (-np.arange(32, dtype=np.float32) / 32)).astype(np.float32)
    angC = pos[None, :] * invC[:, None]
    for g in range(2):
        tabC[g * 64:g * 64 + 32, 0] = np.cos(angC); tabC[g * 64 + 32:g * 64 + 64, 0] = np.cos(angC)
        tabC[g * 64:g * 64 + 32, 1] = -np.sin(angC); tabC[g * 64 + 32:g * 64 + 64, 1] = np.sin(angC)
    rb = inp["band_rel_bias"][0]
    ki = np.arange(128)[:, None]
    qi = np.arange(128)[None, :]
    bias = np.zeros((128, 12, 5, 128), np.float32)
    for r in range(5):
        rel = 128 * (4 - r) + qi - ki
        idx = np.clip(rel, -128, 128) + 128
        dch = (8 - 2 * r) + qi // 64 - ki // 64
        ok = (dch >= 0) & (dch <= 8)
        for h in range(12):
            b = rb[h][idx]
            bias[:, h, r, :] = np.where(ok, b, np.float32(MASKV))
    return dict(vecs=vecs, consts=c, tabA=tabA, tabC=tabC, bbias=bias)


def build_core_inputs(inp, c):
    b, s = c % 2, c
    d = {}
    xT = np.empty((D, TOT), np.float32)
    xT[:, :TP] = inp["x_prompt"][b].T
    xT[:, TP:] = inp["x_sample"][s].T
    d["xT"] = xT
    d["memT"] = np.ascontiguousarray(inp["mem_prompt"][b].T)
    d["skt_a"] = np.ascontiguousarray(inp["cache_a_k"][0, s].reshape(PAST, 12, 128).transpose(1, 2, 0))
    d["sv_a"] = np.ascontiguousarray(inp["cache_a_v"][0, s].reshape(PAST, 1536))
    d["skt_b"] = np.ascontiguousarray(inp["cache_b_k"][0, s].transpose(1, 2, 0))
    d["sv_b"] = np.ascontiguousarray(inp["cache_b_v"][0, s].reshape(512, 1536))
    d["slat"] = np.ascontiguousarray(inp["cache_c_latent"][0, s].T.reshape(4, 128, PAST))
    kr = inp["cache_c_krope"][0, s].T
    d["skr"] = np.ascontiguousarray(np.concatenate([kr, kr], 0))
    d["skt_d"] = np.ascontiguousarray(inp["cache_d_k"][0, s].transpose(1, 2, 0))
    d["sv_d"] = np.ascontiguousarray(inp["cache_d_v"][0, s].reshape(PAST, 1536))
    d["smk"] = np.ascontiguousarray(inp["cache_mem_k"][:, s].transpose(0, 3, 2, 1))
    d["smv"] = np.ascontiguousarray(inp["cache_mem_v"][:, s].reshape(4, 2, 128, 512).transpose(0, 2, 1, 3))
    st = inp["state_ffn_conv"][:, s]
    d["cst_s"] = np.ascontiguousarray(st.reshape(4, 2, 88, 128).transpose(3, 0, 2, 1))
    return d


class Prog:
    def __init__(self, nc, es):
        self.nc, self.es = nc, es
        self.eng = {"pe": nc.tensor, "act": nc.scalar, "dve": nc.vector, "pool": nc.gpsimd, "sp": nc.sync}
        self.csem = {e: es.enter_context(nc.semaphore("c_" + e)) for e in ["pe", "act", "dve", "pool"]}
        self.ccnt = {e: 0 for e in self.csem}
        self.dsem = [es.enter_context(nc.semaphore("d%d" % i)) for i in range(NDS)]
        self.dcnt = [0] * NDS
        self.dnext = 0
        self.seen = {e: {} for e in self.eng}
        self.lw, self.rd = {}, {}
        self.ninst = 0

    def _sem(self, k):
        return self.csem[k[1]] if k[0] == "c" else self.dsem[k[1]]

    def _wait(self, e, deps):
        need = {}
        for k, v in deps:
            if v > need.get(k, 0):
                need[k] = v
        for k, v in need.items():
            if k == ("c", e) and (e == "pe" or not SAME_ENG_SYNC):
                continue
            if self.seen[e].get(k, 0) >= v:
                continue
            self.eng[e].wait_ge(self._sem(k), v)
            self.seen[e][k] = v
            self.ninst += 1

    def _deps(self, reads, writes):
        d = []
        for r in reads:
            if r in self.lw:
                d.append(self.lw[r])
        for w in writes:
            if w in self.lw:
                d.append(self.lw[w])
            d.extend(self.rd.get(w, {}).items())
        return d

    def _record(self, iid, reads, writes):
        for r in reads:
            rr = self.rd.setdefault(r, {})
            if iid[1] > rr.get(iid[0], 0):
                rr[iid[0]] = iid[1]
        for w in writes:
            self.lw[w] = iid
            self.rd[w] = {}

    def op(self, e, fn, reads=(), writes=()):
        self._wait(e, self._deps(reads, writes))
        self.ccnt[e] += 1
        iid = (("c", e), self.ccnt[e])
        fn(self.eng[e]).then_inc(self.csem[e], 1)
        self.ninst += 1
        self._record(iid, reads, writes)

    def dma(self, q, out, in_, reads=(), writes=()):
        k = self.dnext
        self.dnext = (k + 1) % NDS
        deps = self._deps(reads, writes)
        if self.dcnt[k] > 0:
            deps.append((("d", k), self.dcnt[k]))
        self._wait(q, deps)
        self.dcnt[k] += 16
        iid = (("d", k), self.dcnt[k])
        self.eng[q].dma_start(out=out, in_=in_).then_inc(self.dsem[k], 16)
        self.ninst += 1
        self._record(iid, reads, writes)

    def barrier(self):
        deps = [(("c", e), self.ccnt[e]) for e in self.ccnt if self.ccnt[e] > 0]
        skip = set()
        for key in (("w", 0), ("w", 1)):
            if key in self.lw and self.lw[key][0][0] == "d":
                skip.add(self.lw[key])
        for k in range(NDS):
            if self.dcnt[k] > 0:
                iid = (("d", k), self.dcnt[k])
                if iid in skip:
                    if self.dcnt[k] > 16:
                        deps.append((("d", k), self.dcnt[k] - 16))
                else:
                    deps.append(iid)
        for e in ["pe", "act", "dve", "pool"]:
            self._wait(e, [d for d in deps if d[0] != ("c", e)])

    def finish(self):
        deps = [(("c", e), self.ccnt[e]) for e in self.ccnt if self.ccnt[e] > 0]
        deps += [(("d", k), self.dcnt[k]) for k in range(NDS) if self.dcnt[k] > 0]
        self._wait("sp", deps)

    def mm(self, out, lhsT, rhs, start, stop, reads, writes):
        self.op("pe", lambda e: e.matmul(out, lhsT, rhs, start=start, stop=stop), reads, writes)

    def act(self, out, in_, func, reads, writes, bias=None, scale=None):
        kw = {}
        if bias is not None:
            kw["bias"] = bias
        if scale is not None:
            kw["scale"] = scale
        self.op("act", lambda e: e.activation(out, in_, func, **kw), reads, writes)

    def tt(self, out, in0, in1, op, reads, writes, e="dve"):
        self.op(e, lambda g: g.tensor_tensor(out, in0, in1, op), reads, writes)

    def ts(self, out, in0, s1, s2, op0, op1, reads, writes, e="dve"):
        if op1 is None:
            self.op(e, lambda g: g.tensor_scalar(out, in0, s1, None, op0), reads, writes)
        else:
            self.op(e, lambda g: g.tensor_scalar(out, in0, s1, s2, op0, op1), reads, writes)

    def stt(self, out, in0, sc, in1, op0, op1, reads, writes):
        self.op("dve", lambda g: g.scalar_tensor_tensor(out, in0, sc, in1, op0, op1), reads, writes)

    def copy(self, out, in_, reads, writes, e="dve"):
        if e == "act":
            self.act(out, in_, AF.Copy, reads, writes)
        else:
            self.op(e, lambda g: g.tensor_copy(out, in_), reads, writes)


def R(name, idx):
    return [(name, i) for i in idx]


class Kern:
    def __init__(self, cfg):
        self.cfg = cfg
        self.NL = cfg.get("NL", 4)
        self.tiles = cfg.get("tiles", list(range(9)))

    def build(self):
        nc = bass.Bass("TRN2", target_bir_lowering=False)
        self.nc = nc
        nels, pro_seq, layer_seq = weight_tile_meta()
        self.nels, self.pro_seq, self.layer_seq = nels, pro_seq, layer_seq
        NWT = len(nels)
        dt = nc.dram_tensor

        def inp(name, shape):
            return dt(name, list(shape), F32, kind="ExternalInput")

        def outp(name, shape):
            return dt(name, list(shape), F32, kind="ExternalOutput")

        I = {}
        self.wgroups = [pro_seq] + layer_seq
        self.wloc = {}
        for gi, grp in enumerate(self.wgroups):
            for li, ti in enumerate(grp):
                self.wloc[ti] = (gi, li)
        for gi, grp in enumerate(self.wgroups):
            ng = len(grp) if gi <= self.NL else 1
            I["wt%d" % gi] = inp("wt%d" % gi, [ng, 128, WEL])
        I["vecs"] = inp("vecs", [128, NV])
        I["consts"] = inp("consts", [128, NC])
        I["tabA"] = inp("tabA", [32, 2, TOT])
        I["tabC"] = inp("tabC", [128, 2, TOT])
        I["bbias"] = inp("bbias", [128, 12, 5, 128])
        I["xT"] = inp("xT", [D, TOT])
        I["memT"] = inp("memT", [D, 256])
        I["skt_a"] = inp("skt_a", [12, 128, PAST]); I["sv_a"] = inp("sv_a", [PAST, 1536])
        I["skt_b"] = inp("skt_b", [12, 128, 512]); I["sv_b"] = inp("sv_b", [512, 1536])
        I["slat"] = inp("slat", [4, 128, PAST]); I["skr"] = inp("skr", [128, PAST])
        I["skt_d"] = inp("skt_d", [12, 128, PAST]); I["sv_d"] = inp("sv_d", [PAST, 1536])
        I["smk"] = inp("smk", [4, 128, 4, 256]); I["smv"] = inp("smv", [4, 128, 2, 512])
        I["cst_s"] = inp("cst_s", [128, 4, 88, 2])
        self.I = I
        O = {}
        O["yT"] = outp("yT", [D, TOT])
        for l in (0, 1, 3):
            O["kT%d" % l] = outp("kT%d" % l, [1536, TOT])
            O["v%d" % l] = outp("v%d" % l, [TOT, 1536])
        O["latT"] = outp("latT", [512, TOT])
        O["krT"] = outp("krT", [64, TOT])
        O["bkT_s"] = outp("bkT_s", [1536, 512])
        O["bv_s"] = outp("bv_s", [512, 1536])
        O["memkT"] = outp("memkT", [4, 512, 256])
        O["memv"] = outp("memv", [4, 256, 512])
        O["conv_p"] = outp("conv_p", [128, 4, 88, 2])
        O["conv_s"] = outp("conv_s", [128, 4, 88, 2])
        self.O = O
        S = {}
        for gi, grp in enumerate(self.wgroups):
            ng = len(grp) if gi <= self.NL else 1
            S["wb%d" % gi] = dt("wb%d" % gi, [ng, 128, WEL], BF16)
        for l in range(4):
            S["KT%d" % l] = dt("KTs%d" % l, [12, 128, TP], BF16)
            S["V%d" % l] = dt("Vs%d" % l, [TP, 1536], BF16)
        S["KR"] = dt("KRs", [128, TP], BF16)
        S["sKT0"] = dt("sKT0", [12, 128, PAST], BF16); S["sV0"] = dt("sV0", [PAST, 1536], BF16)
        S["sKT1"] = dt("sKT1", [12, 128, 512], BF16); S["sV1"] = dt("sV1", [512, 1536], BF16)
        S["sKT2"] = dt("sKT2", [12, 128, PAST], BF16); S["sV2"] = dt("sV2", [PAST, 1536], BF16)
        S["sKT3"] = dt("sKT3", [12, 128, PAST], BF16); S["sV3"] = dt("sV3", [PAST, 1536], BF16)
        S["sLat"] = dt("sLat", [4, 128, PAST], BF16); S["sKR"] = dt("sKR", [128, PAST], BF16)
        S["mK"] = dt("mK", [2, 4, 128, 4, 256], BF16); S["mV"] = dt("mV", [2, 4, 128, 2, 512], BF16)
        self.S = S

        with contextlib.ExitStack() as es:
            P = Prog(nc, es)
            self.P = P
            sb = lambda name, shape, dty: es.enter_context(nc.sbuf_tensor("s_" + name, list(shape), dty))
            self.xf = sb("xf", [128, 16, NT], F32)
            self.xb = sb("xb", [128, 16, NT], BF16)
            self.wsl = [sb("w%d" % i, [128, WEL], BF16) for i in range(2)]
            self.reg = sb("reg", [128, 44 * NT], BF16)
            self.oT = sb("oT", [128, 16, NT], BF16)
            self.ks = [sb("ks%d" % i, [128, 2, NT], BF16) for i in range(2)]
            self.vs = [sb("vs%d" % i, [128, 4, 256], BF16) for i in range(2)]
            self.pt = [sb("pt%d" % i, [128, NT], BF16) for i in range(3)]
            self.st = [sb("st%d" % i, [128, NT], F32) for i in range(3)]
            self.ue = [sb("ue%d" % i, [128, NT + 4], F32) for i in range(2)]
            self.hb = [sb("hb%d" % i, [128, NT], F32) for i in range(2)]
            self.tm = [sb("tm%d" % i, [128, NT], F32) for i in range(4)]
            self.dtm = sb("dtm", [128, 4 * NT], F32)
            self.dtmv = lambda c: self.dtm[:, c * NT:(c + 1) * NT]
            self.hcv = [self.dtm[:, c * NT:(c + 1) * NT] for c in range(4)]
            dtb = self.dtm[:, :].bitcast(BF16)
            self.qrb = lambda jj: dtb[:, jj * NT:(jj + 1) * NT]
            self.vecs = sb("vecs", [128, NV], F32)
            self.cf = sb("cf", [128, NC], F32)
            self.cb = sb("cbf", [128, NC], BF16)
            self.cst = sb("cst", [128, 4, 88, 2], F32)
            self.mk = sb("mk", [128, 4, 256], BF16)
            self.mv = sb("mv", [128, 2, 512], BF16)
            self.tab = sb("tab", [128, 2, NT], F32)
            self.ebf = sb("ebf", [128, 5, 128], F32)
            self.sm = sb("sm", [128, 16], F32)
            self.ps = es.enter_context(nc.psum_tensor("ps", [128, 8, NT], F32))
            self.bank_rr = 0
            self.rr = {}
            reg = self.reg
            self.qT = lambda j: reg[:, j * NT:(j + 1) * NT]
            self.kT = lambda j: reg[:, (12 + j) * NT:(13 + j) * NT]
            self.vv = lambda tb: reg[:, 24 * NT + tb * 1536: 24 * NT + (tb + 1) * 1536]
            self.qm = lambda j: reg[:, 36 * NT + j * NT: 36 * NT + (j + 1) * NT]
            self.hT = lambda j: reg[:, j * NT:(j + 1) * NT]

            self.prologue()
            for i in self.tiles:
                self.do_tile(i)
            if NPT in self.tiles:
                P.dma("pool", self.O["conv_s"][:, :, :, :], self.cst[:], reads=[("cst",)], writes=[("o_convs",)])
            else:
                P.dma("pool", self.O["conv_p"][:, :, :, :], self.cst[:], reads=[("cst",)], writes=[("o_convp",)])
            P.finish()
        self.ninst = P.ninst
        return nc

    def rot(self, name, n):
        k = self.rr.get(name, 0)
        self.rr[name] = (k + 1) % n
        return k

    def bank(self, lo=0, hi=8):
        k = lo + self.rot(("bank", lo, hi), hi - lo)
        return self.ps[:, k, :], ("ps", k)

    def wnext(self):
        ws = self.wstate
        seq = ws["seq"]
        while ws["issued"] < min(len(seq), ws["pos"] + 2):
            k = ws["issued"]
            self.cast_upto(k + 6)
            ti = seq[k]
            nel = self.nels[ti]
            gi, li = self.wloc[ti]
            self.P.dma("sp", self.wsl[k % 2][:, :nel], self.S["wb%d" % gi][li][:, :nel], reads=[("wb", ti)], writes=[("w", k % 2)])
            ws["issued"] += 1
        k = ws["pos"]
        ws["pos"] += 1
        return self.wsl[k % 2], ("w", k % 2)

    def cast_upto(self, k):
        ws = self.wstate
        seq = ws["seq"]
        while ws["cpos"] < min(len(seq), k + 1):
            ti = seq[ws["cpos"]]
            ws["cpos"] += 1
            if ti in ws["casted"]:
                continue
            ws["casted"].add(ti)
            nel = self.nels[ti]
            gi, li = self.wloc[ti]
            self.P.dma("pool", self.S["wb%d" % gi][li][:, :nel], self.I["wt%d" % gi][li][:, :nel], writes=[("wb", ti)])

    def prologue(self):
        P, I, S = self.P, self.I, self.S
        seq = list(self.pro_seq)
        for i in self.tiles:
            for l in range(self.NL):
                seq += self.layer_seq[l]
        self.wstate = dict(seq=seq, pos=0, issued=0, cpos=0, casted=set())
        used = sorted(set(seq))
        P.dma("sp", self.vecs[:], I["vecs"][:, :], writes=[("vecs",)])
        P.dma("sp", self.cf[:], I["consts"][:, :], writes=[("cf",)])
        for a, b_, key in [("skt_a", "sKT0", "sKT0"), ("sv_a", "sV0", "sV0"), ("skt_b", "sKT1", "sKT1"), ("sv_b", "sV1", "sV1"),
                           ("slat", "sLat", "sLat"), ("skr", "sKR", "sKR"), ("skt_d", "sKT3", "sKT3"), ("sv_d", "sV3", "sV3")]:
            src, dst = I[a].ap(), S[b_].ap()
            P.dma("pool", dst, src, writes=[(key,)])
        for l in range(4):
            P.dma("pool", S["mK"][1, l], I["smk"][l], writes=[("mK", 1, l)])
            P.dma("pool", S["mV"][1, l], I["smv"][l], writes=[("mV", 1, l)])
        P.copy(self.cb[:], self.cf[:], reads=[("cf",)], writes=[("cb",)])
        P.op("dve", lambda g: g.memset(self.cst[:], 0.0), writes=[("cst",)])
        P.op("dve", lambda g: g.memset(self.sm[:], 0.0), writes=[("sm",)])
        P.op("dve", lambda g: g.memset(self.sm[:, 0:1], EPS), reads=[], writes=[("sm",)])
        li = lam_init(0)
        v = self.vecs
        P.tt(self.sm[:, 4:5], v[:, V_LAM:V_LAM + 1], v[:, V_LAM + 1:V_LAM + 2], ALU.mult, [("vecs",), ("sm",)], [("sm",)])
        P.tt(self.sm[:, 5:6], v[:, V_LAM + 2:V_LAM + 3], v[:, V_LAM + 3:V_LAM + 4], ALU.mult, [("vecs",), ("sm",)], [("sm",)])
        pb, pk = self.bank()
        P.mm(pb[:, 0:2], self.cf[:, C_ONES:C_ONES + 128], self.sm[:, 4:6], True, True, [("cf",), ("sm",)], [pk])
        P.act(self.sm[:, 6:8], pb[:, 0:2], AF.Exp, [pk, ("sm",)], [("sm",)])
        P.tt(self.sm[:, 8:9], self.sm[:, 7:8], self.sm[:, 6:7], ALU.subtract, [("sm",)], [("sm",)])
        P.ts(self.sm[:, 1:2], self.sm[:, 8:9], -li, None, ALU.add, None, [("sm",)], [("sm",)])
        P.ts(self.sm[:, 2:4], v[:, V_DG:V_DG + 2], 1.0 - li, None, ALU.mult, None, [("vecs",), ("sm",)], [("sm",)])
        P.dma("sp", self.xf[:, :, 0:256], I["memT"].ap().rearrange("(kc p) t -> p kc t", p=128), writes=R("xf", range(16)))
        P.copy(self.xb[:, :, 0:256], self.xf[:, :, 0:256], R("xf", range(16)), R("xb", range(16)))
        for l in range(4):
            W, wk = self.wnext()
            W3 = W[:, :].rearrange("p (k c) -> p k c", c=512)
            for j in range(4):
                pb, pk = self.bank()
                for kc in range(16):
                    P.mm(pb[:, 0:256], W3[:, kc, j * 128:(j + 1) * 128], self.xb[:, kc, 0:256], kc == 0, kc == 15, [wk] + R("xb", [kc]), [pk])
                s = self.rot("st", 3)
                P.copy(self.st[s][:, 0:256], pb[:, 0:256], [pk], [("st", s)], e="act")
                P.dma("pool", self.O["memkT"][l, j * 128:(j + 1) * 128, :], self.st[s][:, 0:256], reads=[("st", s)], writes=[("o_mk", l, j)])
                P.copy(self.mk[:, j, :], self.st[s][:, 0:256], [("st", s)], [("mk",)])
            P.dma("pool", S["mK"][0, l], self.mk[:], reads=[("mk",)], writes=[("mK", 0, l)])
            W, wk = self.wnext()
            W3 = W[:, :].rearrange("p (k c) -> p k c", c=512)
            for tb in range(2):
                pb, pk = self.bank()
                for kc in range(16):
                    P.mm(pb[:, :], self.xb[:, kc, tb * 128:(tb + 1) * 128], W3[:, kc, :], kc == 0, kc == 15, [wk] + R("xb", [kc]), [pk])
                s = self.rot("st", 3)
                P.copy(self.st[s][:, :], pb[:, :], [pk], [("st", s)], e="act")
                P.dma("pool", self.O["memv"][l, tb * 128:(tb + 1) * 128, :], self.st[s][:, :], reads=[("st", s)], writes=[("o_mv", l, tb)])
                P.copy(self.mv[:, tb, :], self.st[s][:, :], [("st", s)], [("mv",)])
            P.dma("pool", S["mV"][0, l], self.mv[:], reads=[("mv",)], writes=[("mV", 0, l)])

    def do_tile(self, i):
        P, I = self.P, self.I
        self.i = i
        self.samp = (i == NPT)
        self.n = NS if self.samp else NT
        self.tok0 = i * NT
        n, t0 = self.n, self.tok0
        if self.samp:
            P.barrier()
            P.dma("pool", self.O["conv_p"][:, :, :, :], self.cst[:], reads=[("cst",)], writes=[("o_convp",)])
            P.dma("sp", self.cst[:], I["cst_s"][:, :, :, :], reads=[], writes=[("cst",)])
        P.dma("sp", self.xf[:, :, :n], I["xT"].ap().rearrange("(kc p) t -> p kc t", p=128)[:, :, t0:t0 + n], writes=R("xf", range(16)))
        P.copy(self.xb[:, :, :n], self.xf[:, :, :n], R("xf", range(16)), R("xb", range(16)))
        for l in range(self.NL):
            self.layer(l)
        P.dma("pool", self.O["yT"].ap().rearrange("(kc p) t -> p kc t", p=128)[:, :, t0:t0 + n], self.xf[:, :, :n], reads=R("xf", range(16)), writes=[("o_y", i)])

    def proj_fm(self, W3, wk, cj, nk, src, srckey, n):
        P = self.P
        pb, pk = self.bank(0, 4)
        for kc in range(nk):
            P.mm(pb[:, :n], W3[:, kc, cj * 128:(cj + 1) * 128], src(kc), kc == 0, kc == nk - 1, [wk, (srckey, kc)], [pk])
        return pb, pk

    def rope(self, s, npart, perm_off, tabkey):
        P, n = self.P, self.n
        stt_ = self.st[s]
        pb, pk = self.bank(4, 6)
        P.mm(pb[0:npart, :n], self.cf[0:npart, perm_off:perm_off + npart], stt_[0:npart, :n], True, True, [("cf",), ("st", s)], [pk])
        t = self.rot("tm", 4)
        P.tt(self.tm[t][0:npart, :n], pb[0:npart, :n], self.tab[0:npart, 1, :n], ALU.mult, [pk, tabkey], [("tm", t)])
        P.tt(stt_[0:npart, :n], stt_[0:npart, :n], self.tab[0:npart, 0, :n], ALU.mult, [("st", s), tabkey], [("st", s)])
        P.tt(stt_[0:npart, :n], stt_[0:npart, :n], self.tm[t][0:npart, :n], ALU.add, [("st", s), ("tm", t)], [("st", s)])

    def kv_out_names(self, l):
        return self.O["kT%d" % l], self.O["v%d" % l]

    def layer(self, l):
        P, n, t0, i = self.P, self.n, self.tok0, self.i
        m = l % 4
        g = 1 if self.samp else 0
        P.dma("sp", self.mk[:], self.S["mK"][g, l], reads=[("mK", g, l)], writes=[("mk",)])
        P.dma("sp", self.mv[:], self.S["mV"][g, l], reads=[("mV", g, l)], writes=[("mv",)])
        if m == 2:
            self.mla_proj(l)
        else:
            self.qkv_proj(l)
        if m == 0:
            self.attn_diff(l)
        elif m == 1:
            self.attn_band(l)
        elif m == 2:
            self.attn_mla(l)
        else:
            self.attn_stick(l)
        self.attn_mem(l)
        self.out_proj_ffn(l)

    def qkv_proj(self, l):
        P, n, t0, i = self.P, self.n, self.tok0, self.i
        xsrc = lambda kc: self.xb[:, kc, :n]
        OK, OV = self.kv_out_names(l)
        if l == 0:
            P.dma("sp", self.tab[0:32, :, :n], self.I["tabA"][:, :, t0:t0 + n], writes=[("tab",)])
        for part in range(2):
            for jt in range(3):
                W, wk = self.wnext()
                W3 = W[:, :].rearrange("p (k c) -> p k c", c=512)
                for cj in range(4):
                    j = jt * 4 + cj
                    pb, pk = self.proj_fm(W3, wk, cj, 16, xsrc, "xb", n)
                    dst = self.qT(j) if part == 0 else self.kT(j)
                    dkey = ("qT", j) if part == 0 else ("kT", j)
                    if l == 0 or part == 1:
                        s = self.rot("st", 3)
                        P.copy(self.st[s][:, :n], pb[:, :n], [pk], [("st", s)], e="act")
                        if l == 0:
                            self.rope(s, 32, C_PERMA, ("tab",))
                        if part == 1:
                            P.dma("pool", OK[j * 128:(j + 1) * 128, t0:t0 + n], self.st[s][:, :n], reads=[("st", s)], writes=[("o_k", l, j, i)])
                            if l == 1 and self.samp:
                                P.dma("pool", self.O["bkT_s"][j * 128:(j + 1) * 128, 448:512], self.st[s][:, :n], reads=[("st", s)], writes=[("o_bks", j)])
                        P.copy(dst[:, :n], self.st[s][:, :n], [("st", s)], [dkey], e="pool")
                    else:
                        P.copy(dst[:, :n], pb[:, :n], [pk], [dkey], e="act")
        if not self.samp:
            P.dma("pool", self.S["KT%d" % l].ap().rearrange("j p t -> p j t")[:, :, t0:t0 + n],
                  self.reg[:, 12 * NT:24 * NT].rearrange("p (j t) -> p j t", t=NT), reads=R("kT", range(12)), writes=[("KTs", l, i)])
        ntb = (n + 127) // 128
        for jt in range(3):
            W, wk = self.wnext()
            W3 = W[:, :].rearrange("p (k c) -> p k c", c=512)
            for tb in range(ntb):
                np_ = min(128, n - tb * 128)
                pb, pk = self.bank(0, 4)
                for kc in range(16):
                    P.mm(pb[:np_, :], self.xb[:, kc, tb * 128:tb * 128 + np_], W3[:, kc, :], kc == 0, kc == 15, [wk, ("xb", kc)], [pk])
                s = self.rot("st", 3)
                P.copy(self.st[s][:np_, :], pb[:np_, :], [pk], [("st", s)], e="act")
                r0 = t0 + tb * 128
                P.dma("pool", OV[r0:r0 + np_, jt * 512:(jt + 1) * 512], self.st[s][:np_, :], reads=[("st", s)], writes=[("o_v", l, jt, tb, i)])
                if l == 1 and self.samp:
                    P.dma("pool", self.O["bv_s"][448:512, jt * 512:(jt + 1) * 512], self.st[s][:np_, :], reads=[("st", s)], writes=[("o_bvs", jt)])
                P.copy(self.vv(tb)[:np_, jt * 512:(jt + 1) * 512], self.st[s][:np_, :], [("st", s)], [("v", tb)], e="pool")
        if not self.samp:
            P.dma("pool", self.S["V%d" % l].ap()[t0:t0 + n, :].rearrange("(tb p) c -> p tb c", p=128),
                  self.reg[:, 24 * NT:24 * NT + 4 * 1536].rearrange("p (tb c) -> p tb c", c=1536), reads=R("v", range(4)), writes=[("Vs", l, i)])
        if l == 1 and self.samp:
            P.dma("pool", self.O["bkT_s"][:, 0:448], self.I["skt_b"].ap().rearrange("j p t -> (j p) t")[:, 64:512], writes=[("o_bks2",)])
            P.dma("pool", self.O["bv_s"][0:448, :], self.I["sv_b"][64:512, :], writes=[("o_bvs2",)])
        W, wk = self.wnext()
        W3 = W[:, :].rearrange("p (k c) -> p k c", c=512)
        for cj in range(4):
            pb, pk = self.proj_fm(W3, wk, cj, 16, xsrc, "xb", n)
            P.copy(self.qm(cj)[:, :n], pb[:, :n], [pk], [("qm", cj)], e="act")

    def kv_groups(self, l):
        S = self.S
        if self.samp:
            ng = 1 if l == 1 else 2
            return [(S["sKT%d" % l], S["sV%d" % l], S["sKR"], g, [("sKT%d" % l,), ("sV%d" % l,), ("sKR",)]) for g in range(ng)]
        return [(S["KT%d" % l], S["V%d" % l], S["KR"], g, [("KTs", l, g), ("Vs", l, g), ("KRs", g)]) for g in range(self.i)]

    def load_group(self, grp, j, vc0, dv, with_kr=False):
        P = self.P
        KT, V, KR, g, keys = grp
        s = self.rot("kvs", 2)
        P.dma("sp", self.ks[s][:, 0, :], KT[j, :, g * 512:(g + 1) * 512], reads=[keys[0]], writes=[("ks", s)])
        if with_kr:
            P.dma("sp", self.ks[s][:, 1, :], KR[:, g * 512:(g + 1) * 512], reads=[keys[2]], writes=[("ks", s)])
        P.dma("sp", self.vs[s][:, :, 0:dv], V.ap()[g * 512:(g + 1) * 512, vc0:vc0 + dv].rearrange("(kb p) c -> p kb c", p=128),
              reads=[keys[1]], writes=[("vs", s)])
        return s

    def accset(self):
        k = self.rot("accset", 2)
        base = 2 + 3 * k
        return [(self.ps[:, base + c, :], ("ps", base + c)) for c in range(3)]

    def pipe_push(self, s1, s2):
        s1()
        if self.pipe_pending is not None:
            self.pipe_pending()
        self.pipe_pending = s2

    def pipe_flush(self):
        if getattr(self, "pipe_pending", None) is not None:
            self.pipe_pending()
        self.pipe_pending = None

    def softmax_heads(self, l, heads, scale, masked):
        P, n, i = self.P, self.n, self.i
        groups = self.kv_groups(l)
        items = [(hd, grp) for hd in heads for grp in groups]
        slots = {}

        def issue(k):
            if k < len(items) and k not in slots:
                hd, grp = items[k]
                slots[k] = self.load_group(grp, hd["kj"], hd["vc0"], hd["dv"], with_kr=hd.get("qr") is not None)

        k = 0
        ones_b = self.cb[:, C_ONES:C_ONES + 128]
        ntb = (n + 127) // 128
        self.pipe_pending = None
        for hd in heads:
            q, qk = hd["q"]
            ndv = hd["dv"] // 128
            acc = self.accset()
            oacc = acc[:ndv]
            sacc = acc[2]
            nblocks = 4 * len(groups) + ntb
            st = dict(bi=0)

            def block(klhs, kkeys, krlhs, vfn, vkey, nk, q0, diag, hd=hd, q=q, qk=qk, oacc=oacc, sacc=sacc, ndv=ndv, st=st, nblocks=nblocks):
                bi = st["bi"]
                st["bi"] += 1
                first, last = (bi == 0), (bi == nblocks - 1)
                hold = {}

                def s1():
                    sb_, sk = self.bank(0, 2)
                    if krlhs is None:
                        P.mm(sb_[:nk, q0:n], klhs, q[:, q0:n], True, True, kkeys + [qk], [sk])
                    else:
                        qr, qrk, hp = hd["qr"]
                        P.mm(sb_[:nk, q0:n], klhs, q[:, q0:n], True, False, kkeys + [qk], [sk])
                        P.mm(sb_[:nk, q0:n], krlhs, qr[hp:hp + 64, q0:n], False, True, kkeys + [qrk], [sk])
                    p = self.rot("pt", 3)
                    pt = self.pt[p]
                    P.act(pt[:nk, q0:n], sb_[:nk, q0:n], AF.Exp, [sk], [("pt", p)], scale=scale)
                    if diag and masked:
                        w = min(128, n - q0)
                        P.tt(pt[:nk, q0:q0 + w], pt[:nk, q0:q0 + w], self.cb[:nk, C_MCC:C_MCC + w], ALU.mult, [("pt", p), ("cb",)], [("pt", p)])
                    hold["p"] = p

                def s2():
                    p = hold["p"]
                    pt = self.pt[p]
                    for c in range(ndv):
                        P.mm(oacc[c][0][:, q0:n], vfn(c), pt[:nk, q0:n], first, last, [vkey, ("pt", p)], [oacc[c][1]])
                    P.mm(sacc[0][:, q0:n], ones_b[:nk, :], pt[:nk, q0:n], first, last, [("cb",), ("pt", p)], [sacc[1]])
                    if last:
                        t = self.rot("tm", 4)
                        P.op("dve", lambda g_: g_.reciprocal(self.tm[t][:, :n], sacc[0][:, :n]), [sacc[1]], [("tm", t)])
                        hd["fin"](oacc, (self.tm[t], ("tm", t)))

                self.pipe_push(s1, s2)

            for grp in groups:
                issue(k)
                s = slots[k]
                k += 1
                for kb in range(4):
                    krl = None
                    if hd.get("qr") is not None:
                        hp = hd["qr"][2]
                        krl = self.ks[s][hp:hp + 64, 1, kb * 128:(kb + 1) * 128]
                    block(self.ks[s][:, 0, kb * 128:(kb + 1) * 128], [("ks", s)], krl,
                          lambda c, s=s, kb=kb: self.vs[s][:, kb, c * 128:(c + 1) * 128], ("vs", s), 128, 0, False)
                    if kb == 0:
                        issue(k)
            kown, kownkey = hd["kown"]
            for kb in range(ntb):
                nk = min(128, n - kb * 128)
                krl = None
                if hd.get("qr") is not None:
                    hp = hd["qr"][2]
                    krl = self.krT2[hp:hp + 64, kb * 128:kb * 128 + nk]
                block(kown[:, kb * 128:kb * 128 + nk], [kownkey] + ([("krT2",)] if krl is not None else []), krl,
                      lambda c, kb=kb, nk=nk, hd=hd: self.vv(kb)[:nk, hd["vc0"] + c * 128: hd["vc0"] + (c + 1) * 128], ("v", kb), nk, kb * 128, True)
        self.pipe_flush()

    def attn_diff(self, l):
        P, n = self.P, self.n
        heads = []
        for h in range(6):
            for m in range(2):
                j = 2 * h + m

                def fin(oacc, rec, h=h, m=m):
                    for c in range(2):
                        P.tt(self.dtmv(2 * m + c)[:, :n], oacc[c][0][:, :n], rec[0][:, :n], ALU.mult, [oacc[c][1], rec[1]], [("dtm", 2 * m + c)])
                    if m == 1:
                        for c in range(2):
                            P.stt(self.dtmv(c)[:, :n], self.dtmv(2 + c)[:, :n], self.sm[:, 1:2], self.dtmv(c)[:, :n], ALU.mult, ALU.add,
                                  [("dtm", 2 + c), ("dtm", c), ("sm",)], [("dtm", c)])
                        pb, pk = self.bank(0, 2)
                        for c in range(2):
                            t = self.rot("tm", 4)
                            P.act(self.tm[t][:, :n], self.dtmv(c)[:, :n], AF.Square, [("dtm", c)], [("tm", t)])
                            P.mm(pb[:, :n], self.cf[:, C_ONES:C_ONES + 128], self.tm[t][:, :n], c == 0, c == 1, [("cf",), ("tm", t)], [pk])
                        t = self.rot("tm", 4)
                        P.act(self.tm[t][:, :n], pb[:, :n], AF.Sqrt, [pk, ("sm",)], [("tm", t)], bias=self.sm[:, 0:1], scale=1.0 / 256)
                        P.op("dve", lambda g_: g_.reciprocal(self.tm[t][:, :n], self.tm[t][:, :n]), [("tm", t)], [("tm", t)])
                        for c in range(2):
                            P.tt(self.dtmv(c)[:, :n], self.dtmv(c)[:, :n], self.tm[t][:, :n], ALU.mult, [("dtm", c), ("tm", t)], [("dtm", c)])
                            P.ts(self.oT[:, 2 * h + c, :n], self.dtmv(c)[:, :n], self.sm[:, 2 + c:3 + c], None, ALU.mult, None,
                                 [("dtm", c), ("sm",)], [("oT", 2 * h + c)], e="pool")

                heads.append(dict(q=(self.qT(j), ("qT", j)), kj=j, kown=(self.kT(j), ("kT", j)), vc0=256 * h, dv=256, fin=fin))
        self.softmax_heads(l, heads, 128 ** -0.5, True)

    def attn_mem(self, l):
        P, n = self.P, self.n
        ones_b = self.cb[:, C_ONES:C_ONES + 128]
        self.pipe_pending = None
        for h in range(4):
            acc = self.accset()
            oacc, sacc = acc[0], acc[2]
            for kb in range(2):
                hold = {}

                def s1(h=h, kb=kb, hold=hold):
                    sb_, sk = self.bank(0, 2)
                    P.mm(sb_[:, :n], self.mk[:, h, kb * 128:(kb + 1) * 128], self.qm(h)[:, :n], True, True, [("mk",), ("qm", h)], [sk])
                    p = self.rot("pt", 3)
                    P.act(self.pt[p][:, :n], sb_[:, :n], AF.Exp, [sk], [("pt", p)], scale=128 ** -0.5)
                    hold["p"] = p

                def s2(h=h, kb=kb, hold=hold, oacc=oacc, sacc=sacc):
                    p = hold["p"]
                    P.mm(oacc[0][:, :n], self.mv[:, kb, h * 128:(h + 1) * 128], self.pt[p][:, :n], kb == 0, kb == 1, [("mv",), ("pt", p)], [oacc[1]])
                    P.mm(sacc[0][:, :n], ones_b, self.pt[p][:, :n], kb == 0, kb == 1, [("cb",), ("pt", p)], [sacc[1]])
                    if kb == 1:
                        t = self.rot("tm", 4)
                        P.op("dve", lambda g_: g_.reciprocal(self.tm[t][:, :n], sacc[0][:, :n]), [sacc[1]], [("tm", t)])
                        P.tt(self.oT[:, 12 + h, :n], oacc[0][:, :n], self.tm[t][:, :n], ALU.mult, [oacc[1], ("tm", t)], [("oT", 12 + h)])

                self.pipe_push(s1, s2)
        self.pipe_flush()

    def attn_band(self, l):
        P, n, i = self.P, self.n, self.i
        ones_b = self.cb[:, C_ONES:C_ONES + 128]
        has_prev = self.samp or i > 0
        npb = (n + 127) // 128
        self.pipe_pending = None
        for h in range(12):
            ebf = self.ebf
            ekey = ("ebf", 0)
            P.dma("sp", ebf[:], self.I["bbias"][:, h], writes=[ekey])
            P.act(ebf[:], ebf[:], AF.Exp, [ekey], [ekey])
            s = None
            if has_prev:
                s = self.rot("kvs", 2)
                if self.samp:
                    KT, V, g0, kk = self.S["sKT1"], self.S["sV1"], 0, [("sKT1",), ("sV1",)]
                else:
                    KT, V, g0, kk = self.S["KT1"], self.S["V1"], (i - 1) * 512, [("KTs", 1, i - 1), ("Vs", 1, i - 1)]
                P.dma("sp", self.ks[s][:, 0, :], KT[h, :, g0:g0 + 512], reads=[kk[0]], writes=[("ks", s)])
                P.dma("sp", self.vs[s][:, :, 0:128], V.ap()[g0:g0 + 512, h * 128:(h + 1) * 128].rearrange("(kb p) c -> p kb c", p=128),
                      reads=[kk[1]], writes=[("vs", s)])
            acc = self.accset()
            oacc, sacc = acc[0], acc[2]
            rs = [4, 3, 2, 1, 0]
            valid_r = [r for r in rs if has_prev or r == 4 or (npb - 1) - 4 + r >= 0]
            for ri, r in enumerate(valid_r):
                pbs = [pb_ for pb_ in range(npb) if (pb_ - 4 + r >= 0) or has_prev]
                hold = {}

                def s1(h=h, r=r, pbs=pbs, s=s, ebf=ebf, ekey=ekey, hold=hold):
                    pb0 = pbs[0]
                    sb_, sk = self.bank(0, 2)
                    nk = 128
                    for pb_ in pbs:
                        lb = pb_ - 4 + r
                        nq = min(128, n - pb_ * 128)
                        if lb < 0:
                            klhs, kkey, nk = self.ks[s][:, 0, (4 + lb) * 128:(5 + lb) * 128], ("ks", s), 128
                        else:
                            nk = min(128, n - lb * 128)
                            klhs, kkey = self.kT(h)[:, lb * 128:lb * 128 + nk], ("kT", h)
                        P.mm(sb_[:nk, pb_ * 128:pb_ * 128 + nq], klhs, self.qT(h)[:, pb_ * 128:pb_ * 128 + nq], True, True, [kkey, ("qT", h)], [sk])
                    c0, c1 = pb0 * 128, n
                    p = self.rot("pt", 3)
                    pt = self.pt[p]
                    P.act(pt[:nk, c0:c1], sb_[:nk, c0:c1], AF.Exp, [sk], [("pt", p)], scale=128 ** -0.5)
                    nqq = min(128, n)
                    ptv = pt[:nk, c0:c1].rearrange("p (a b) -> p a b", b=nqq)
                    ebv = ebf[:nk, r:r + 1, 0:nqq].broadcast_to([nk, len(pbs), nqq])
                    P.tt(ptv, ptv, ebv, ALU.mult, [("pt", p), ekey], [("pt", p)])
                    hold["p"], hold["nk"] = p, nk

                def s2(h=h, r=r, ri=ri, pbs=pbs, s=s, hold=hold, oacc=oacc, sacc=sacc, nvr=len(valid_r)):
                    p, nk_all = hold["p"], hold["nk"]
                    pt = self.pt[p]
                    last = (ri == nvr - 1)
                    c0, c1 = pbs[0] * 128, n
                    for pb_ in pbs:
                        lb = pb_ - 4 + r
                        nq = min(128, n - pb_ * 128)
                        if lb < 0:
                            vl, vkey, nk = self.vs[s][:, 4 + lb, 0:128], ("vs", s), 128
                        else:
                            nk = min(128, n - lb * 128)
                            vl, vkey = self.vv(lb)[:nk, h * 128:(h + 1) * 128], ("v", lb)
                        P.mm(oacc[0][:, pb_ * 128:pb_ * 128 + nq], vl, pt[:nk, pb_ * 128:pb_ * 128 + nq], ri == 0 and pb_ == pbs[0],
                             last and pb_ == pbs[-1], [vkey, ("pt", p)], [oacc[1]])
                    P.mm(sacc[0][:, c0:c1], ones_b[:nk_all, :], pt[:nk_all, c0:c1], ri == 0, last, [("cb",), ("pt", p)], [sacc[1]])
                    if last:
                        t = self.rot("tm", 4)
                        P.op("dve", lambda g_: g_.reciprocal(self.tm[t][:, :n], sacc[0][:, :n]), [sacc[1]], [("tm", t)])
                        P.tt(self.oT[:, h, :n], oacc[0][:, :n], self.tm[t][:, :n], ALU.mult, [oacc[1], ("tm", t)], [("oT", h)])

                self.pipe_push(s1, s2)
        self.pipe_flush()

    def attn_stick(self, l):
        P, n, i = self.P, self.n, self.i
        groups = self.kv_groups(l)
        rgroups = list(reversed(groups))
        items = [(h, grp) for h in range(12) for grp in rgroups]
        slots = {}

        def issue(k):
            if k < len(items) and k not in slots:
                h, grp = items[k]
                slots[k] = self.load_group(grp, h, h * 128, 128)

        k = 0
        scale = 128 ** -0.5
        onesf = self.cf[:, C_ONES:C_ONES + 128]
        ntb = (n + 127) // 128
        pend = []

        def push(s1, s2, s3):
            s1()
            if len(pend) >= 1:
                pend[-1][0]()
            if len(pend) >= 2:
                pend[-2][1]()
                pend.pop(0)
            pend.append([s2, s3])

        def flush():
            if len(pend) == 2:
                pend[1][0]()
                pend[0][1]()
                pend[1][1]()
            elif len(pend) == 1:
                pend[0][0]()
                pend[0][1]()
            del pend[:]

        for h in range(12):
            ob = 5 + self.rot("stick_o", 2)
            oacc = (self.ps[:, ob, :], ("ps", ob))
            ebv_ = self.ebf[:, :, :].rearrange("p a b -> p (a b)")
            spb = [(self.hb[0], ("hb", 0)), (self.hb[1], ("hb", 1)), (ebv_, ("ebf", 0))]
            for b_, bk_ in spb:
                P.op("pool", lambda g_, b_=b_: g_.memset(b_[:, :n], 0.0), [], [bk_])
            state = dict(bi=0, cur=0)
            nblocks = 4 * len(groups) + ntb

            def block(klhs, kkey, vl, vlkey, nk, q0, diag, h=h, oacc=oacc, state=state, nblocks=nblocks, spb=spb):
                bi = state["bi"]
                state["bi"] += 1
                first, last = (bi == 0), (bi == nblocks - 1)
                vcur, vkey = spb[bi % 3]
                vnew, vnkey = spb[(bi + 1) % 3]
                hold = {}

                def s1():
                    sb_, sk = self.bank(0, 3)
                    P.mm(sb_[:nk, q0:n], klhs, self.qT(h)[:, q0:n], True, True, [kkey, ("qT", h)], [sk])
                    te = self.rot("tm", 4)
                    P.act(self.tm[te][:nk, q0:n], sb_[:nk, q0:n], AF.Exp, [sk], [("tm", te)], scale=scale)
                    tsp = self.rot("st", 3)
                    sp = self.st[tsp]
                    P.act(sp[:nk, q0:n], self.tm[te][:nk, q0:n], AF.Ln, [("tm", te)], [("st", tsp)], bias=1.0)
                    if diag:
                        w = min(128, n - q0)
                        P.tt(sp[:nk, q0:q0 + w], sp[:nk, q0:q0 + w], self.cf[:nk, C_MSC:C_MSC + w], ALU.mult, [("st", tsp), ("cf",)], [("st", tsp)])
                    tz = self.rot("tm", 4)
                    P.stt(self.tm[tz][:nk, q0:n], sb_[:nk, q0:n], scale, sp[:nk, q0:n], ALU.mult, ALU.subtract, [sk, ("st", tsp)], [("tm", tz)])
                    P.tt(vnew[:nk, q0:n], vcur[:nk, q0:n], sp[:nk, q0:n], ALU.add, [vkey, ("st", tsp)], [vnkey], e="pool")
                    hold["tsp"], hold["tz"] = tsp, tz

                def s2():
                    tsp, tz = hold["tsp"], hold["tz"]
                    sp, arg = self.st[tsp], self.tm[tz]
                    tb_, tk = self.bank(3, 5)
                    P.mm(tb_[:nk, q0:n], self.cf[:nk, C_U:C_U + nk], sp[:nk, q0:n], True, first, [("cf",), ("st", tsp)], [tk])
                    if not first:
                        P.mm(tb_[:nk, q0:n], onesf[:, :nk], vcur[:, q0:n], False, True, [("cf",), vkey], [tk])
                    P.tt(arg[:nk, q0:n], arg[:nk, q0:n], tb_[:nk, q0:n], ALU.subtract, [("tm", tz), tk], [("tm", tz)])
                    p = self.rot("pt", 3)
                    pt = self.pt[p]
                    P.act(pt[:nk, q0:n], arg[:nk, q0:n], AF.Exp, [("tm", tz)], [("pt", p)])
                    if diag:
                        w = min(128, n - q0)
                        P.tt(pt[:nk, q0:q0 + w], pt[:nk, q0:q0 + w], self.cb[:nk, C_MSC:C_MSC + w], ALU.mult, [("pt", p), ("cb",)], [("pt", p)])
                    hold["p"] = p

                def s3():
                    p = hold["p"]
                    P.mm(oacc[0][:, q0:n], vl, self.pt[p][:nk, q0:n], first, last, [vlkey, ("pt", p)], [oacc[1]])
                    if last:
                        P.copy(self.oT[:, h, :n], oacc[0][:, :n], [oacc[1]], [("oT", h)], e="act")

                push(s1, s2, s3)

            issue(k)
            for kb in reversed(range(ntb)):
                nk = min(128, n - kb * 128)
                block(self.kT(h)[:, kb * 128:kb * 128 + nk], ("kT", h), self.vv(kb)[:nk, h * 128:(h + 1) * 128], ("v", kb), nk, kb * 128, True)
            for grp in rgroups:
                issue(k)
                s = slots[k]
                k += 1
                for bi_, kb in enumerate(reversed(range(4))):
                    block(self.ks[s][:, 0, kb * 128:(kb + 1) * 128], ("ks", s), self.vs[s][:, kb, 0:128], ("vs", s), 128, 0, False)
                    if bi_ == 1:
                        issue(k)
            flush()

    def rms_fm(self, src, srckey, nch, dim, gofs, outs):
        P, n = self.P, self.n
        pb, pk = self.bank(6, 8)
        for c in range(nch):
            t = self.rot("tm", 4)
            P.act(self.tm[t][:, :n], src(c)[:, :n], AF.Square, [(srckey, c)], [("tm", t)])
            P.mm(pb[:, :n], self.cf[:, C_ONES:C_ONES + 128], self.tm[t][:, :n], c == 0, c == nch - 1, [("cf",), ("tm", t)], [pk])
        tr = self.rot("hb", 2)
        rs = self.hb[tr]
        P.act(rs[:, :n], pb[:, :n], AF.Sqrt, [pk, ("sm",)], [("hb", tr)], bias=self.sm[:, 0:1], scale=1.0 / dim)
        P.op("dve", lambda g_: g_.reciprocal(rs[:, :n], rs[:, :n]), [("hb", tr)], [("hb", tr)])
        for c in range(nch):
            t = self.rot("tm", 4)
            P.tt(self.tm[t][:, :n], src(c)[:, :n], rs[:, :n], ALU.mult, [(srckey, c), ("hb", tr)], [("tm", t)])
            for (fn, key) in outs:
                P.ts(fn(c)[:, :n], self.tm[t][:, :n], self.vecs[:, gofs + c:gofs + c + 1], None, ALU.mult, None, [("tm", t), ("vecs",)], [(key, c)], e="pool")

    def mla_proj(self, l):
        P, n, t0, i = self.P, self.n, self.tok0, self.i
        xsrc = lambda kc: self.xb[:, kc, :n]
        regf = self.reg[:, 0:20 * NT].bitcast(F32)
        cqf = lambda c: regf[:, c * NT:(c + 1) * NT]
        lf = lambda c: regf[:, (6 + c) * NT:(7 + c) * NT]
        cqn = lambda c: self.oT[:, c, :]
        latT = lambda c: self.oT[:, 6 + c, :]
        self.krT2 = self.reg[:, 43 * NT:44 * NT]
        P.dma("sp", self.tab[:, :, :n], self.I["tabC"][:, :, t0:t0 + n], writes=[("tab",)])
        W, wk = self.wnext(); W3 = W[:, :].rearrange("p (k c) -> p k c", c=512)
        for cj in range(4):
            pb, pk = self.proj_fm(W3, wk, cj, 16, xsrc, "xb", n)
            P.copy(cqf(cj)[:, :n], pb[:, :n], [pk], [("cqf", cj)], e="act")
        W, wk = self.wnext(); W3 = W[:, :].rearrange("p (k c) -> p k c", c=512)
        for cj in range(2):
            pb, pk = self.proj_fm(W3, wk, cj, 16, xsrc, "xb", n)
            P.copy(cqf(4 + cj)[:, :n], pb[:, :n], [pk], [("cqf", 4 + cj)], e="act")
        pb, pk = self.bank(0, 4)
        for kc in range(16):
            P.mm(pb[0:64, :n], W3[:, kc, 256:320], xsrc(kc), kc == 0, kc == 15, [wk, ("xb", kc)], [pk])
        s = self.rot("st", 3)
        P.copy(self.st[s][0:64, :n], pb[0:64, :n], [pk], [("st", s)], e="act")
        pb2, pk2 = self.bank(4, 6)
        P.mm(pb2[:, :n], self.cf[0:64, C_DUP:C_DUP + 128], self.st[s][0:64, :n], True, True, [("cf",), ("st", s)], [pk2])
        s2 = self.rot("st", 3)
        P.copy(self.st[s2][:, :n], pb2[:, :n], [pk2], [("st", s2)], e="act")
        self.rope(s2, 128, C_PERMC, ("tab",))
        P.dma("pool", self.O["krT"][:, t0:t0 + n], self.st[s2][0:64, :n], reads=[("st", s2)], writes=[("o_kr", i)])
        P.copy(self.krT2[:, :n], self.st[s2][:, :n], [("st", s2)], [("krT2",)], e="pool")
        if not self.samp:
            P.dma("pool", self.S["KR"][:, t0:t0 + n], self.krT2[:, :n], reads=[("krT2",)], writes=[("KRs", i)])
        W, wk = self.wnext(); W3 = W[:, :].rearrange("p (k c) -> p k c", c=512)
        for cj in range(4):
            pb, pk = self.proj_fm(W3, wk, cj, 16, xsrc, "xb", n)
            P.copy(lf(cj)[:, :n], pb[:, :n], [pk], [("lf", cj)], e="act")
        W, wk = self.wnext(); W3 = W[:, :].rearrange("p (k c) -> p k c", c=512)
        for cj in range(4):
            pb, pk = self.proj_fm(W3, wk, cj, 16, xsrc, "xb", n)
            P.copy(self.qm(cj)[:, :n], pb[:, :n], [pk], [("qm", cj)], e="act")
        self.rms_fm(cqf, "cqf", 6, 768, V_QG, [(cqn, "cqn")])
        self.rms_fm(lf, "lf", 4, 512, V_KG, [(lf, "lf2"), (latT, "latT")])
        for c in range(4):
            P.dma("pool", self.O["latT"][c * 128:(c + 1) * 128, t0:t0 + n], lf(c)[:, :n], reads=[("lf2", c)], writes=[("o_lat", c, i)])
        P.barrier()
        csrc = lambda kc: cqn(kc)[:, :n]
        for jt in range(5):
            W, wk = self.wnext(); W3 = W[:, :].rearrange("p (k c) -> p k c", c=512)
            for cj in range(4):
                jglob = jt * 4 + cj
                if jglob >= 18:
                    break
                pb, pk = self.proj_fm(W3, wk, cj, 6, csrc, "cqn", n)
                if jglob < 12:
                    P.copy(self.qT(jglob)[:, :n], pb[:, :n], [pk], [("qT", jglob)], e="act")
                else:
                    jj = jglob - 12
                    s = self.rot("st", 3)
                    P.copy(self.st[s][:, :n], pb[:, :n], [pk], [("st", s)], e="act")
                    self.rope(s, 128, C_PERMC, ("tab",))
                    P.copy(self.qrb(jj)[:, :n], self.st[s][:, :n], [("st", s)], [("qrb", jj)], e="pool")
        sets = [("past", 0), ("past", 1)] if self.samp else []
        sets.append(("own", None))
        vsf = [self.vs[k_][:, :, :].rearrange("p a b -> p (a b)") for k_ in range(2)]
        plat = lambda kc: vsf[kc // 2][:, (kc % 2) * 512:(kc % 2 + 1) * 512]
        for jt in range(6):
            W, wk = self.wnext(); W3 = W[:, :].rearrange("p (k c) -> p k c", c=512)
            for kind, g in sets:
                if kind == "past":
                    for kc in range(4):
                        P.dma("sp", plat(kc), self.S["sLat"][kc, :, g * 512:(g + 1) * 512], reads=[("sLat",)], writes=[("vs", kc // 2)])
                    src = lambda kc: plat(kc)
                    rkey = lambda kc: ("vs", kc // 2)
                    nn = 512
                else:
                    src = lambda kc: latT(kc)[:, :n]
                    rkey = lambda kc: ("latT", kc)
                    nn = n
                if jt < 3:
                    for cj in range(4):
                        j = jt * 4 + cj
                        pb, pk = self.bank(0, 4)
                        for kc in range(4):
                            P.mm(pb[:, :nn], W3[:, kc, cj * 128:(cj + 1) * 128], src(kc), kc == 0, kc == 3, [wk, rkey(kc)], [pk])
                        if kind == "own":
                            P.copy(self.kT(j)[:, :n], pb[:, :n], [pk], [("kT", j)], e="act")
                        else:
                            p = self.rot("pt", 3)
                            P.copy(self.pt[p][:, :], pb[:, :], [pk], [("pt", p)], e="act")
                            P.dma("pool", self.S["sKT2"][j, :, g * 512:(g + 1) * 512], self.pt[p][:, :], reads=[("pt", p)], writes=[("sKT2",)])
                else:
                    c0 = (jt - 3) * 512
                    for tb in range((nn + 127) // 128):
                        np_ = min(128, nn - tb * 128)
                        pb, pk = self.bank(0, 4)
                        for kc in range(4):
                            P.mm(pb[:np_, :], src(kc)[:, tb * 128:tb * 128 + np_], W3[:, kc, :], kc == 0, kc == 3, [wk, rkey(kc)], [pk])
                        if kind == "own":
                            P.copy(self.vv(tb)[:np_, c0:c0 + 512], pb[:np_, :], [pk], [("v", tb)], e="act")
                        else:
                            p = self.rot("pt", 3)
                            P.copy(self.pt[p][:, :], pb[:, :], [pk], [("pt", p)], e="act")
                            r0 = g * 512 + tb * 128
                            P.dma("pool", self.S["sV2"][r0:r0 + 128, c0:c0 + 512], self.pt[p][:, :], reads=[("pt", p)], writes=[("sV2",)])
        if not self.samp:
            P.dma("pool", self.S["KT2"].ap().rearrange("j p t -> p j t")[:, :, t0:t0 + n],
                  self.reg[:, 12 * NT:24 * NT].rearrange("p (j t) -> p j t", t=NT), reads=R("kT", range(12)), writes=[("KTs", 2, i)])
            P.dma("pool", self.S["V2"].ap()[t0:t0 + n, :].rearrange("(tb p) c -> p tb c", p=128),
                  self.reg[:, 24 * NT:24 * NT + 4 * 1536].rearrange("p (tb c) -> p tb c", c=1536), reads=R("v", range(4)), writes=[("Vs", 2, i)])

    def attn_mla(self, l):
        P, n = self.P, self.n
        P.barrier()
        heads = []
        for h in range(12):
            def fin(oacc, rec, h=h):
                P.tt(self.oT[:, h, :n], oacc[0][0][:, :n], rec[0][:, :n], ALU.mult, [oacc[0][1], rec[1]], [("oT", h)])
            heads.append(dict(q=(self.qT(h), ("qT", h)), qr=(self.qrb(h // 2), ("qrb", h // 2), 64 * (h % 2)), kj=h,
                              kown=(self.kT(h), ("kT", h)), vc0=128 * h, dv=128, fin=fin))
        self.softmax_heads(l, heads, 192 ** -0.5, True)

    def ln_sq(self, which, c):
        if which == 1:
            return self.oT[:, c, :], ("oT", c)
        if c < 12:
            return self.qT(c), ("qT", c)
        return self.qm(c - 12), ("qm", c - 12)

    def ln_acc(self, which, c):
        P, n = self.P, self.n
        if c == 0:
            P.copy(self.hb[0][:, :n], self.xf[:, c, :n], [("xf", c)], [("hb", 0)], e="pool")
        else:
            P.tt(self.hb[0][:, :n], self.hb[0][:, :n], self.xf[:, c, :n], ALU.add, [("hb", 0), ("xf", c)], [("hb", 0)], e="pool")
        sq, sqk = self.ln_sq(which, c)
        P.act(sq[:, :n], self.xf[:, c, :n], AF.Square, [("xf", c)], [sqk])

    def layer_norm(self, l, which):
        P, n = self.P, self.n
        go = V_LN + ((2 * which) * 4 + l) * 16
        bo = V_LN + ((2 * which + 1) * 4 + l) * 16
        onesf = self.cf[:, C_ONES:C_ONES + 128]
        onesb = self.cb[:, C_ONES:C_ONES + 128]
        pa, pak = self.ps[:, 6, :], ("ps", 6)
        pq, pqk = self.ps[:, 7, :], ("ps", 7)
        P.mm(pa[:, :n], onesf, self.hb[0][:, :n], True, True, [("cf",), ("hb", 0)], [pak])
        for c in range(16):
            sq, sqk = self.ln_sq(which, c)
            P.mm(pq[:, :n], onesb, sq[:, :n], c == 0, c == 15, [("cb",), sqk], [pqk])
        mu, rstd = self.hb[0], self.hb[1]
        P.act(mu[:, :n], pa[:, :n], AF.Identity, [pak], [("hb", 0)], scale=1.0 / D)
        t = self.rot("tm", 4)
        P.tt(self.tm[t][:, :n], mu[:, :n], mu[:, :n], ALU.mult, [("hb", 0)], [("tm", t)])
        P.stt(rstd[:, :n], pq[:, :n], 1.0 / D, self.tm[t][:, :n], ALU.mult, ALU.subtract, [pqk, ("tm", t)], [("hb", 1)])
        P.act(rstd[:, :n], rstd[:, :n], AF.Sqrt, [("hb", 1), ("sm",)], [("hb", 1)], bias=self.sm[:, 0:1])
        P.op("dve", lambda g_: g_.reciprocal(rstd[:, :n], rstd[:, :n]), [("hb", 1)], [("hb", 1)])
        P.stt(mu[:, :n], mu[:, :n], -1.0, rstd[:, :n], ALU.mult, ALU.mult, [("hb", 0), ("hb", 1)], [("hb", 0)])
        for c in range(16):
            t = self.rot("tm", 4)
            e_ = "dve" if c % 3 != 2 else "pool"
            P.tt(self.tm[t][:, :n], self.xf[:, c, :n], rstd[:, :n], ALU.mult, [("xf", c), ("hb", 1)], [("tm", t)], e=e_)
            P.tt(self.tm[t][:, :n], self.tm[t][:, :n], mu[:, :n], ALU.add, [("tm", t), ("hb", 0)], [("tm", t)], e=e_)
            P.act(self.xf[:, c, :n], self.tm[t][:, :n], AF.Identity, [("tm", t), ("vecs",)], [("xf", c)],
                  bias=self.vecs[:, bo + c:bo + c + 1], scale=self.vecs[:, go + c:go + c + 1])
            P.act(self.xb[:, c, :n], self.tm[t][:, :n], AF.Identity, [("tm", t), ("vecs",)], [("xb", c)],
                  bias=self.vecs[:, bo + c:bo + c + 1], scale=self.vecs[:, go + c:go + c + 1])

    def out_proj_ffn(self, l):
        P, n, i = self.P, self.n, self.i
        for jt in range(4):
            W, wk = self.wnext(); W3 = W[:, :].rearrange("p (k c) -> p k c", c=512)
            for cj in range(4):
                oc = jt * 4 + cj
                pb, pk = self.bank(0, 4)
                for kc in range(16):
                    P.mm(pb[:, :n], W3[:, kc, cj * 128:(cj + 1) * 128], self.oT[:, kc, :n], kc == 0, kc == 15, [wk, ("oT", kc)], [pk])
                P.stt(self.xf[:, oc, :n], self.xf[:, oc, :n], ALPHA, pb[:, :n], ALU.mult, ALU.add, [("xf", oc), pk], [("xf", oc)])
                self.ln_acc(0, oc)
        self.layer_norm(l, 0)
        P.barrier()
        cst = self.cst
        ckey = ("cst",)
        v = self.vecs
        for t in range(22):
            W, wk = self.wnext(); W3 = W[:, :].rearrange("p (k c) -> p k c", c=512)
            hbufs = {}
            for cj in range(4):
                g = 2 * t + (cj % 2)
                ch = g if cj < 2 else 44 + g
                pb, pk = self.bank(0, 6)
                for kc in range(16):
                    P.mm(pb[:, :n], W3[:, kc, cj * 128:(cj + 1) * 128], self.xb[:, kc, :n], kc == 0, kc == 15, [wk, ("xb", kc)], [pk])
                u = self.rot("ue", 2)
                ue = self.ue[u]
                P.copy(ue[:, 0:2], cst[:, l, ch, :], [ckey], [("ue", u)], e="pool")
                P.copy(ue[:, 2:2 + n], pb[:, :n], [pk, ("ue", u)], [("ue", u)], e="act")
                P.copy(cst[:, l, ch, :], ue[:, n:n + 2], [("ue", u)], [ckey], e="pool")
                hbk = self.rot("hbc", 4)
                hb = self.hcv[hbk]
                cw = lambda j_: v[:, V_CW + (l * 3 + j_) * 88 + ch: V_CW + (l * 3 + j_) * 88 + ch + 1]
                cbias = v[:, V_CB + l * 88 + ch: V_CB + l * 88 + ch + 1]
                P.act(hb[:, :n], pb[:, :n], AF.Identity, [pk, ("vecs",)], [("hcv", hbk)], bias=cbias, scale=cw(2))
                P.stt(hb[:, :n], ue[:, 1:1 + n], cw(1), hb[:, :n], ALU.mult, ALU.add, [("ue", u), ("vecs",), ("hcv", hbk)], [("hcv", hbk)])
                P.stt(hb[:, :n], ue[:, 0:n], cw(0), hb[:, :n], ALU.mult, ALU.add, [("ue", u), ("vecs",), ("hcv", hbk)], [("hcv", hbk)])
                hbufs[cj] = hbk
            for gg in range(2):
                g = 2 * t + gg
                a, b_ = hbufs[gg], hbufs[2 + gg]
                P.act(self.hcv[a][:, :n], self.hcv[a][:, :n], AF.Silu, [("hcv", a)], [("hcv", a)])
                P.tt(self.hT(g)[:, :n], self.hcv[a][:, :n], self.hcv[b_][:, :n], ALU.mult, [("hcv", a), ("hcv", b_)], [("hT", g)], e="pool")
        for oc in range(16):
            W, wk = self.wnext(); Wd = W[:, 0:44 * 128].rearrange("p (k c) -> p k c", c=128)
            pb, pk = self.bank(0, 6)
            for fc in range(44):
                P.mm(pb[:, :n], Wd[:, fc, :], self.hT(fc)[:, :n], fc == 0, fc == 43, [wk, ("hT", fc)], [pk])
            P.stt(self.xf[:, oc, :n], self.xf[:, oc, :n], ALPHA, pb[:, :n], ALU.mult, ALU.add, [("xf", oc), pk], [("xf", oc)])
            self.ln_acc(1, oc)
        self.layer_norm(l, 1)


def run(inputs, cfg):
    k = Kern(cfg)
    nc = k.build()
    wt, nels, pro_seq, layer_seq = build_weight_tiles(inputs)
    shared = build_shared(inputs)
    in_maps = []
    for c in range(8):
        d = dict(shared)
        for gi, grp in enumerate([pro_seq] + layer_seq):
            d["wt%d" % gi] = wt[grp[0]:grp[-1] + 1] if gi <= k.NL else wt[grp[0]:grp[0] + 1]
        d.update(build_core_inputs(inputs, c))
        in_maps.append(d)
    import time as _t
    t0_ = _t.time()
    res = run_bass_kernel_spmd(nc, in_maps, core_ids=list(range(8)))
    print("spmd run seconds", _t.time() - t0_, flush=True)
    return res.results


def assemble(R_):
    f = np.float32
    def P2(fn):
        return np.stack([fn(R_[b]) for b in range(2)])
    def S8(fn):
        return np.stack([fn(R_[c]) for c in range(8)])
    y_p = P2(lambda r: r["yT"][:, :TP].T)
    y_s = S8(lambda r: r["yT"][:, TP:].T)
    def kfm(r, name, sl, H, dd):
        a = r[name][:, sl].T
        return a.reshape(a.shape[0], H, dd)
    pa, sa = slice(0, TP), slice(TP, TOT)
    outs = [y_p, y_s]
    outs.append(P2(lambda r: kfm(r, "kT0", pa, 6, 256))[None])
    outs.append(P2(lambda r: r["v0"][pa].reshape(TP, 6, 256))[None])
    outs.append(P2(lambda r: kfm(r, "kT1", slice(TP - 512, TP), 12, 128))[None])
    outs.append(P2(lambda r: r["v1"][TP - 512:TP].reshape(512, 12, 128))[None])
    outs.append(P2(lambda r: r["latT"][:, pa].T)[None])
    outs.append(P2(lambda r: r["krT"][:, pa].T)[None])
    outs.append(P2(lambda r: kfm(r, "kT3", pa, 12, 128))[None])
    outs.append(P2(lambda r: r["v3"][pa].reshape(TP, 12, 128))[None])
    outs.append(np.stack([np.stack([R_[b]["memkT"][l].T.reshape(256, 4, 128) for b in range(2)]) for l in range(4)]))
    outs.append(np.stack([np.stack([R_[b]["memv"][l].reshape(256, 4, 128) for b in range(2)]) for l in range(4)]))
    def conv(a):
        return a.transpose(1, 3, 2, 0).reshape(4, 2, 88 * 128)
    outs.append(np.stack([conv(R_[b]["conv_p"]) for b in range(2)], axis=1))
    outs.append(S8(lambda r: kfm(r, "kT0", sa, 6, 256))[None])
    outs.append(S8(lambda r: r["v0"][sa].reshape(NS, 6, 256))[None])
    outs.append(S8(lambda r: r["bkT_s"].T.reshape(512, 12, 128))[None])
    outs.append(S8(lambda r: r["bv_s"].reshape(512, 12, 128))[None])
    outs.append(S8(lambda r: r["latT"][:, sa].T)[None])
    outs.append(S8(lambda r: r["krT"][:, sa].T)[None])
    outs.append(S8(lambda r: kfm(r, "kT3", sa, 12, 128))[None])
    outs.append(S8(lambda r: r["v3"][sa].reshape(NS, 12, 128))[None])
    outs.append(np.stack([conv(R_[c]["conv_s"]) for c in range(8)], axis=1))
    return tuple(np.ascontiguousarray(o.astype(f)) for o in outs)


def kernel(**inputs):
    inputs = {k: np.asarray(v) for k, v in inputs.items()}
    R_ = run(inputs, {})
    return assemble(R_)
```
